# Optimizing a Trainium2 kernel written in Bass

```python
import jax, jax.numpy as jnp
from jax import lax
import numpy as np

D_MODEL = 1024
BATCH = 4
SEQ = 4096
DEPTH = 2

CTX_LEN = 256
GRID_W = 64
D_MIX = D_MODEL
D_SSD = D_MIX // 2
SSD_HEAD_DIM = 64
SSD_HEADS = D_SSD // SSD_HEAD_DIM
SSD_GROUPS = 2
SSD_HPG = SSD_HEADS // SSD_GROUPS
SSD_STATE = 128
SSD_CONV = 3
SSD_CHUNK = 128
D_S5 = D_MIX - D_SSD
S5_CH = 16
S5_GROUPS = D_S5 // S5_CH
S5_STATE = 64
FFN_CONV = 3
D_FF = 2816
XBC_DIM = D_SSD + 2 * SSD_GROUPS * SSD_STATE
D_PROJ = D_SSD + XBC_DIM + 2 * SSD_HEADS + D_S5
NORM_EPS = 1e-6

kernel_name = 'hybrid_bidir_ssd_s5_flow_block'


def rmsnorm(x, g):
    x32 = x.astype(jnp.float32)
    y = x32 * lax.rsqrt(jnp.mean(x32 * x32, axis=-1, keepdims=True) + NORM_EPS)
    return (y * g.astype(jnp.float32)).astype(x.dtype)


def modulate(x, shift, scale):
    return x * (1 + scale[:, None, :]) + shift[:, None, :]


def ada_params(cvec, w, b):
    return jnp.split(jax.nn.silu(cvec) @ w + b, 6, axis=-1)


def dwconv_centred(x, w, b):
    k_w = w.shape[0]
    pad = k_w // 2
    length = x.shape[1]
    xp = jnp.pad(x, ((0, 0), (pad, pad), (0, 0)))
    out = xp[:, 0:length] * w[0] + b
    for k in range(1, k_w):
        out = out + xp[:, k:k + length] * w[k]
    return out


def flip(t):
    return jnp.flip(t, axis=1)


def ssd_scan(xh, dt, a, bm, cm, h0, with_output):
    bsz, length = xh.shape[:2]
    q = SSD_CHUNK
    nc = length // q
    xh = xh.reshape(bsz, nc, q, SSD_GROUPS, SSD_HPG, SSD_HEAD_DIM)
    dt = dt.reshape(bsz, nc, q, SSD_GROUPS, SSD_HPG)
    bm = bm.reshape(bsz, nc, q, SSD_GROUPS, SSD_STATE)
    cm = cm.reshape(bsz, nc, q, SSD_GROUPS, SSD_STATE)
    cum = jnp.cumsum(dt * a.reshape(SSD_GROUPS, SSD_HPG), axis=2)
    xdt = xh * dt[..., None]
    to_end = jnp.exp(cum[:, :, -1:] - cum)
    states = jnp.einsum('bcqgn,bcqgep->bcgepn', bm, xdt * to_end[..., None])
    chunk_decay = jnp.exp(cum[:, :, -1])

    def step(h, inp):
        s, d = inp
        return h * d[..., None, None] + s, h

    h_final, h_prev = lax.scan(step, h0, (jnp.moveaxis(states, 1, 0), jnp.moveaxis(chunk_decay, 1, 0)))
    if not with_output:
        return None, h_final
    h_prev = jnp.moveaxis(h_prev, 0, 1)
    tri = jnp.tril(jnp.ones((q, q), dtype=bool))
    seg = cum[:, :, :, None] - cum[:, :, None, :]
    decay = jnp.exp(jnp.where(tri[:, :, None, None], seg, -jnp.inf))
    cb = jnp.einsum('bcign,bcjgn->bcijg', cm, bm)
    y_diag = jnp.einsum('bcijge,bcjgep->bcigep', cb[..., None] * decay, xdt)
    y_off = jnp.einsum('bcign,bcgepn->bcigep', cm, h_prev) * jnp.exp(cum)[..., None]
    y = (y_diag + y_off).reshape(bsz, length, SSD_HEADS, SSD_HEAD_DIM)
    return y, h_final


def s5_scan(bu, lam_bar, h0):
    bu = bu.at[:, 0].add(lam_bar * h0)
    a = jnp.broadcast_to(lam_bar, bu.shape)

    def combine(left, right):
        a_l, b_l = left
        a_r, b_r = right
        return a_r * a_l, a_r * b_l + b_r

    _, hs = lax.associative_scan(combine, (a, bu), axis=1)
    return hs


def token_mixers(h, p, init, with_output):
    f32 = jnp.float32
    bsz, length, _ = h.shape
    proj = h @ p['w_in']
    z, xbc, dt_raw, u = jnp.split(proj, [D_SSD, D_SSD + XBC_DIM, D_SSD + XBC_DIM + 2 * SSD_HEADS], axis=-1)

    xbc = jax.nn.silu(dwconv_centred(xbc, p['ssd_conv_w'], p['ssd_conv_b'])).astype(f32)
    xs, bm, cm = jnp.split(xbc, [D_SSD, D_SSD + SSD_GROUPS * SSD_STATE], axis=-1)
    xh = xs.reshape(bsz, length, SSD_HEADS, SSD_HEAD_DIM)
    bm = bm.reshape(bsz, length, SSD_GROUPS, SSD_STATE)
    cm = cm.reshape(bsz, length, SSD_GROUPS, SSD_STATE)
    dt = jax.nn.softplus(dt_raw.astype(f32).reshape(bsz, length, 2, SSD_HEADS) + p['ssd_dt_bias'].astype(f32))
    a = -jnp.exp(p['ssd_a_log'].astype(f32))
    h_f0, h_b0, s_f0, s_b0 = init
    y_f, hf = ssd_scan(xh, dt[:, :, 0], a[0], bm, cm, h_f0, with_output)
    y_b, hb = ssd_scan(flip(xh), flip(dt[:, :, 1]), a[1], flip(bm), flip(cm), h_b0, with_output)

    lam = lax.complex(p['s5_a_re'].astype(f32), p['s5_a_im'].astype(f32))
    step_size = jnp.exp(p['s5_log_dt'].astype(f32))[..., None]
    lam_bar = jnp.exp(lam * step_size)
    b_c = lax.complex(p['s5_b_re'].astype(f32), p['s5_b_im'].astype(f32))
    b_bar = ((lam_bar - 1) / lam)[..., None] * b_c
    ug = u.astype(f32).reshape(bsz, length, S5_GROUPS, S5_CH)
    bu_f = jnp.einsum('blgc,gpc->blgp', ug, b_bar[0])
    bu_b = jnp.einsum('blgc,gpc->blgp', flip(ug), b_bar[1])
    hs_f = s5_scan(bu_f, lam_bar[0], s_f0)
    hs_b = s5_scan(bu_b, lam_bar[1], s_b0)
    states = (hf, hb, hs_f[:, -1], hs_b[:, -1])
    if not with_output:
        return None, states

    y_ssd = y_f + flip(y_b) + xh * p['ssd_d'].astype(f32)[:, None]
    y_ssd = rmsnorm(y_ssd.reshape(bsz, length, D_SSD) * jax.nn.silu(z.astype(f32)), p['ssd_norm_g'])

    c_c = lax.complex(p['s5_c_re'].astype(f32), p['s5_c_im'].astype(f32))
    y_s5 = jnp.real(jnp.einsum('gcp,blgp->blgc', c_c, hs_f + flip(hs_b)))
    y_s5 = y_s5 + ug * p['s5_d'].astype(f32).reshape(S5_GROUPS, S5_CH)
    g = jax.nn.gelu(y_s5.reshape(bsz, length, D_S5))
    y_s5 = g * jax.nn.sigmoid(g @ p['s5_glu_w'] + p['s5_glu_b'])

    out = jnp.concatenate([y_ssd, y_s5], axis=-1).astype(h.dtype) @ p['w_out']
    return out, states


def conv_ffn(h, w_up, conv_w, conv_b, w_down, rows):
    up = h @ w_up
    bsz, length, ch = up.shape
    if rows is None:
        up = dwconv_centred(up, conv_w, conv_b)
    else:
        up = dwconv_centred(up.reshape(bsz * rows, GRID_W, ch), conv_w, conv_b).reshape(bsz, length, ch)
    gate, val = jnp.split(up, 2, axis=-1)
    return (jax.nn.silu(gate) * val) @ w_down


def setup_inputs(seed: int = 0) -> dict:
    key = jax.random.key(seed)
    ks = jax.random.split(key, 40)
    f32 = jnp.float32

    def nrm(k, shape, s):
        return jax.random.normal(k, shape, f32) * s

    dt0 = jnp.exp(jax.random.uniform(ks[12], (DEPTH, 2, SSD_HEADS), f32, np.log(1e-3), np.log(1e-1)))
    n_idx = jnp.arange(S5_STATE, dtype=f32)
    return {
        'x': nrm(ks[0], (BATCH, SEQ, D_MODEL), 1.0),
        'c': nrm(ks[1], (BATCH, D_MODEL), 1.0),
        'ctx': nrm(ks[2], (BATCH, CTX_LEN, D_MODEL), 1.0),
        'c_ctx': nrm(ks[3], (D_MODEL,), 1.0),
        'w_ada': nrm(ks[4], (DEPTH, D_MODEL, 6 * D_MODEL), D_MODEL ** -0.5),
        'b_ada': nrm(ks[5], (DEPTH, 6 * D_MODEL), 0.02),
        'g_pre_mix': 1.0 + nrm(ks[6], (DEPTH, D_MODEL), 0.05),
        'g_post_mix': 1.0 + nrm(ks[7], (DEPTH, D_MODEL), 0.05),
        'g_pre_ffn': 1.0 + nrm(ks[8], (DEPTH, D_MODEL), 0.05),
        'g_post_ffn': 1.0 + nrm(ks[9], (DEPTH, D_MODEL), 0.05),
        'w_in': nrm(ks[10], (DEPTH, D_MODEL, D_PROJ), D_MODEL ** -0.5),
        'ssd_conv_w': nrm(ks[11], (DEPTH, SSD_CONV, XBC_DIM), SSD_CONV ** -0.5),
        'ssd_conv_b': nrm(ks[13], (DEPTH, XBC_DIM), 0.02),
        'ssd_dt_bias': dt0 + jnp.log(-jnp.expm1(-dt0)),
        'ssd_a_log': jnp.log(jax.random.uniform(ks[14], (DEPTH, 2, SSD_HEADS), f32, 1.0, 16.0)),
        'ssd_d': 1.0 + nrm(ks[15], (DEPTH, SSD_HEADS), 0.05),
        'ssd_norm_g': 1.0 + nrm(ks[16], (DEPTH, D_SSD), 0.05),
        's5_a_re': -0.5 + nrm(ks[17], (DEPTH, 2, S5_GROUPS, S5_STATE), 0.01),
        's5_a_im': jnp.pi * n_idx + nrm(ks[18], (DEPTH, 2, S5_GROUPS, S5_STATE), 0.01),
        's5_log_dt': jax.random.uniform(ks[19], (DEPTH, 2, S5_GROUPS), f32, np.log(1e-3), np.log(1e-1)),
        's5_b_re': nrm(ks[20], (DEPTH, S5_GROUPS, S5_STATE, S5_CH), (2 * S5_CH) ** -0.5),
        's5_b_im': nrm(ks[21], (DEPTH, S5_GROUPS, S5_STATE, S5_CH), (2 * S5_CH) ** -0.5),
        's5_c_re': nrm(ks[22], (DEPTH, S5_GROUPS, S5_CH, S5_STATE), (2 * S5_STATE) ** -0.5),
        's5_c_im': nrm(ks[23], (DEPTH, S5_GROUPS, S5_CH, S5_STATE), (2 * S5_STATE) ** -0.5),
        's5_d': nrm(ks[24], (DEPTH, D_S5), 1.0),
        's5_glu_w': nrm(ks[25], (DEPTH, D_S5, D_S5), D_S5 ** -0.5),
        's5_glu_b': nrm(ks[26], (DEPTH, D_S5), 0.02),
        'w_out': nrm(ks[27], (DEPTH, D_MIX, D_MODEL), D_MIX ** -0.5),
        'ffn_w_up': nrm(ks[28], (DEPTH, D_MODEL, 2 * D_FF), D_MODEL ** -0.5),
        'ffn_conv_w': nrm(ks[29], (DEPTH, FFN_CONV, 2 * D_FF), FFN_CONV ** -0.5),
        'ffn_conv_b': nrm(ks[30], (DEPTH, 2 * D_FF), 0.02),
        'ffn_w_down': nrm(ks[31], (DEPTH, D_FF, D_MODEL), D_FF ** -0.5),
    }


def reference(x, c, ctx, c_ctx, w_ada, b_ada, g_pre_mix, g_post_mix, g_pre_ffn, g_post_ffn,
              w_in, ssd_conv_w, ssd_conv_b, ssd_dt_bias, ssd_a_log, ssd_d, ssd_norm_g,
              s5_a_re, s5_a_im, s5_log_dt, s5_b_re, s5_b_im, s5_c_re, s5_c_im, s5_d, s5_glu_w, s5_glu_b,
              w_out, ffn_w_up, ffn_conv_w, ffn_conv_b, ffn_w_down):
    bsz = x.shape[0]
    rows = x.shape[1] // GRID_W
    for l in range(DEPTH):
        last = l == DEPTH - 1
        p = {
            'w_in': w_in[l], 'ssd_conv_w': ssd_conv_w[l], 'ssd_conv_b': ssd_conv_b[l],
            'ssd_dt_bias': ssd_dt_bias[l], 'ssd_a_log': ssd_a_log[l], 'ssd_d': ssd_d[l],
            'ssd_norm_g': ssd_norm_g[l], 's5_a_re': s5_a_re[l], 's5_a_im': s5_a_im[l],
            's5_log_dt': s5_log_dt[l], 's5_b_re': s5_b_re[l], 's5_b_im': s5_b_im[l],
            's5_c_re': s5_c_re[l], 's5_c_im': s5_c_im[l], 's5_d': s5_d[l],
            's5_glu_w': s5_glu_w[l], 's5_glu_b': s5_glu_b[l], 'w_out': w_out[l],
        }
        sh1, sc1, gt1, sh2, sc2, gt2 = ada_params(c, w_ada[l], b_ada[l])
        csh1, csc1, cgt1, csh2, csc2, cgt2 = ada_params(c_ctx[None, :], w_ada[l], b_ada[l])
        zero_states = (
            jnp.zeros((bsz, SSD_GROUPS, SSD_HPG, SSD_HEAD_DIM, SSD_STATE), jnp.float32),
            jnp.zeros((bsz, SSD_GROUPS, SSD_HPG, SSD_HEAD_DIM, SSD_STATE), jnp.float32),
            jnp.zeros((bsz, S5_GROUPS, S5_STATE), jnp.complex64),
            jnp.zeros((bsz, S5_GROUPS, S5_STATE), jnp.complex64),
        )
        hc = modulate(rmsnorm(ctx, g_pre_mix[l]), csh1, csc1)
        ctx_mix, ctx_states = token_mixers(hc, p, zero_states, not last)
        hx = modulate(rmsnorm(x, g_pre_mix[l]), sh1, sc1)
        x_mix, _ = token_mixers(hx, p, ctx_states, True)
        x = x + gt1[:, None, :] * rmsnorm(x_mix, g_post_mix[l])
        hx = modulate(rmsnorm(x, g_pre_ffn[l]), sh2, sc2)
        x = x + gt2[:, None, :] * rmsnorm(conv_ffn(hx, ffn_w_up[l], ffn_conv_w[l], ffn_conv_b[l], ffn_w_down[l], rows), g_post_ffn[l])
        if not last:
            ctx = ctx + cgt1[:, None, :] * rmsnorm(ctx_mix, g_post_mix[l])
            hc = modulate(rmsnorm(ctx, g_pre_ffn[l]), csh2, csc2)
            ctx = ctx + cgt2[:, None, :] * rmsnorm(conv_ffn(hc, ffn_w_up[l], ffn_conv_w[l], ffn_conv_b[l], ffn_w_down[l], None), g_post_ffn[l])
    return x
```

```python
import math
import numpy as np
from contextlib import ExitStack
import concourse.bass as bass
import concourse.mybir as mybir
from concourse.bass_utils import run_bass_kernel_spmd

F32 = mybir.dt.float32
BF16 = mybir.dt.bfloat16
AF = mybir.ActivationFunctionType
ALU = mybir.AluOpType

NSLOT = 24
EPOCH = 20000

D = 1024
L = 4096
LC = 256
LT = L + LC
DEPTH = 2
DFF = 2816
NJ = DFF // 128
EPS = 1e-6
NSTEP = 13


class Dep:
    __slots__ = ("w", "r")

    def __init__(self):
        self.w = None
        self.r = {}


class T:
    def __init__(self, t):
        self.t = t
        self.d = Dep()

    def __getitem__(self, k):
        return self.t[k]


class Prog:
    ENGS = ("pe", "act", "dve", "pool", "sp")

    def __init__(self, nc, stack):
        self.nc = nc
        self.stack = stack
        self.eng = {"pe": nc.tensor, "act": nc.scalar, "dve": nc.vector, "pool": nc.gpsimd, "sp": nc.sync}
        self.count = {e: 0 for e in self.ENGS}
        self.glob = []
        self.seen = {e: {} for e in self.ENGS}
        self.slot_cnt = [0] * NSLOT
        self.next_slot = 0
        self.uid = 0
        self.psl = []
        self.psi = 0

    def sb(self, shape, dtype=F32):
        self.uid += 1
        t = self.stack.enter_context(self.nc.sbuf_tensor(f"sb{self.uid}", list(shape), dtype))
        return T(t)

    def nextps(self):
        p = self.psl[self.psi % len(self.psl)]
        self.psi += 1
        return p

    def _collect(self, eng, reads, writes):
        waits = {}

        def add(k):
            if k is None:
                return
            key, val = k
            if eng == "pe" and key == "pe":
                return
            if waits.get(key, 0) < val:
                waits[key] = val

        for d in reads:
            add(d.w)
        for d in writes:
            add(d.w)
            for key, val in d.r.items():
                add((key, val))
        out = {}
        seen = self.seen[eng]
        for key, val in waits.items():
            if seen.get(key, 0) < val:
                seen[key] = val
                out[key] = val
        return out

    def _mark(self, reads, writes, mykey):
        key, val = mykey
        for d in reads:
            if d.r.get(key, 0) < val:
                d.r[key] = val
        for d in writes:
            d.w = mykey
            d.r = {}

    def op(self, eng, fn, reads=(), writes=()):
        reads = [x.d for x in reads]
        writes = [x.d for x in writes]
        waits = self._collect(eng, reads, writes)
        self.count[eng] += 1
        mykey = (eng, self.count[eng])
        self.glob.append((eng, fn, waits, "c", mykey))
        self._mark(reads, writes, mykey)

    def dma(self, q, out_ap, in_ap, reads=(), writes=()):
        reads = [x.d for x in reads]
        writes = [x.d for x in writes]
        waits = self._collect(q, reads, writes)
        s = self.next_slot
        self.next_slot = (s + 1) % NSLOT
        key = ("dma", s)
        prev = 16 * self.slot_cnt[s]
        if prev > 0 and self.seen[q].get(key, 0) < prev:
            self.seen[q][key] = prev
            waits[key] = prev
        self.slot_cnt[s] += 1
        mykey = (key, 16 * self.slot_cnt[s])

        def fn(e, out_ap=out_ap, in_ap=in_ap):
            return e.dma_start(out=out_ap, in_=in_ap)

        self.glob.append((q, fn, waits, "d", mykey))
        self._mark(reads, writes, mykey)

    def barrier(self):
        for e in self.ENGS:
            waits = {}
            for e2 in self.ENGS:
                if e2 != e and self.count[e2] > 0 and self.seen[e].get(e2, 0) < self.count[e2]:
                    self.seen[e][e2] = self.count[e2]
                    waits[e2] = self.count[e2]
            for s in range(NSLOT):
                v = 16 * self.slot_cnt[s]
                key = ("dma", s)
                if v > 0 and self.seen[e].get(key, 0) < v:
                    self.seen[e][key] = v
                    waits[key] = v
            if waits:
                self.glob.append((e, None, waits, "w", None))

    def emit(self):
        nc = self.nc
        waited = {e: set() for e in self.ENGS}
        for eng, fn, waits, kind, mykey in self.glob:
            for key, val in waits.items():
                if key in waited:
                    waited[key].add(val)
        rank = {}
        sems = {}
        for e in self.ENGS:
            vals = sorted(waited[e])
            rank[e] = {v: i for i, v in enumerate(vals)}
            nep = (len(vals) + EPOCH - 1) // EPOCH
            sems[e] = [self.stack.enter_context(nc.semaphore(f"sem_{e}_{k}")) for k in range(max(nep, 1))]
        dsem = [self.stack.enter_context(nc.semaphore(f"sem_dma_{s}")) for s in range(NSLOT)]

        def semval(key, val):
            if isinstance(key, tuple):
                return dsem[key[1]], val
            r = rank[key][val]
            return sems[key][r // EPOCH], (r % EPOCH) + 1

        n = 0
        for eng, fn, waits, kind, mykey in self.glob:
            e = self.eng[eng]
            for key, val in waits.items():
                s, v = semval(key, val)
                e.wait_ge(s, v)
            if fn is None:
                continue
            inst = fn(e)
            n += 1
            if kind == "d":
                inst.then_inc(dsem[mykey[0][1]], 16)
            elif mykey[1] in rank[eng]:
                s, v = semval(eng, mykey[1])
                inst.then_inc(s, 1)
        e = self.eng["sp"]
        for s in range(NSLOT):
            if self.slot_cnt[s] > 0:
                e.wait_ge(dsem[s], 16 * self.slot_cnt[s])
        return n


def token_tiles(with_ctx=True):
    tl = []
    if with_ctx:
        tl.append((0, LC, 1, 0, LC, LC))
    for k in range(L // 512):
        tl.append((LC + 512 * k, 512, 0, LC, LT, 64))
    return tl


class K:
    pass


def build_program(debug_stop=None):
    nc = bass.Bass("TRN2", target_bir_lowering=False)
    g = K()

    def din(name, shape, dt=F32):
        return nc.dram_tensor(name, list(shape), dt, kind="ExternalInput").ap()

    def dscr(name, shape, dt=F32):
        kind = "ExternalOutput" if debug_stop is not None else "Internal"
        return nc.dram_tensor(name, list(shape), dt, kind=kind).ap()

    g.xc0 = din("xc0", [D, LT])
    g.cvec = din("cvec", [128, 8, 2])
    g.consts = din("consts", [128, 4, 128])
    g.w_ada = din("w_ada", [DEPTH, D, 6 * D])
    g.bada = din("bada", [128, DEPTH, 48])
    g.gvec = din("gvec", [128, DEPTH, 4, 8])
    g.w_in = din("w_in", [DEPTH, D, 2064])
    g.convp = din("convp", [128, DEPTH, 8, 4])
    g.dtb = din("dtb", [128, DEPTH, 16])
    g.alog = din("alog", [128, DEPTH, 16])
    g.dfull = din("dfull", [128, DEPTH, 512])
    g.gssd = din("gssd", [128, DEPTH, 512])
    g.s5a = din("s5a", [128, DEPTH, 2, 32])
    g.s5dt = din("s5dt", [128, DEPTH, 32])
    g.bpad = din("bpad", [DEPTH, 2, 16, 128, 128])
    g.cpad = din("cpad", [DEPTH, 2, 16, 128, 128])
    g.s5d = din("s5d", [128, DEPTH, 4])
    g.glu_w = din("glu_w", [DEPTH, 512, 512])
    g.glub = din("glub", [128, DEPTH, 4])
    g.w_out = din("w_out", [DEPTH, D, D])
    g.w_up = din("w_up", [DEPTH, D, 2 * DFF])
    g.cwf = din("cwf", [128, DEPTH, 2 * NJ, 4])
    g.w_down = din("w_down", [DEPTH, DFF, D])
    g.out = nc.dram_tensor("out", [D, L], F32, kind="ExternalOutput").ap()

    g.xa = dscr("xa", [D, LT])
    g.xb = dscr("xb", [D, LT])
    g.zs = dscr("zs", [LT, 512])
    g.dtk = dscr("dtk", [LT, 32])
    g.xbc = dscr("xbc", [1024, LT])
    g.xbcA = dscr("xbcA", [1024, LT])
    g.uT = dscr("uT", [512, LT])
    g.yf = dscr("yf", [LT, 512])
    g.ysT = dscr("ysT", [512, LT], BF16)
    g.y5T = dscr("y5T", [512, LT], BF16)
    g.g5T = dscr("g5T", [512, LT], BF16)

    with ExitStack() as st:
        P = Prog(nc, st)
        g.P = P
        for i in range(8):
            P.psl.append(T(st.enter_context(nc.psum_tensor(f"psb{i}", [128, 512], F32))))
        emit_all(nc, g, debug_stop)
        n = P.emit()
    return nc, n


def copy_op(P, eng, out_ap, in_ap, reads, writes):
    if eng == "act":
        P.op("act", lambda e: e.copy(out_ap, in_ap), reads=reads, writes=writes)
    else:
        P.op(eng, lambda e: e.tensor_copy(out_ap, in_ap), reads=reads, writes=writes)


def emit_all(nc, g, debug_stop):
    P = g.P
    g.cst = P.sb([128, 4, 128])
    P.dma("sp", g.cst[:], g.consts[:, :, :], writes=[g.cst])
    g.ident = lambda: g.cst[:, 0, :]
    g.onesf = lambda: g.cst[:, 3, :]
    g.onesb = P.sb([128, 128], BF16)
    P.op("dve", lambda e: e.tensor_copy(g.onesb[:], g.cst[:, 3, :]), reads=[g.cst], writes=[g.onesb])
    g.eps = P.sb([128, 1])
    P.op("dve", lambda e: e.memset(g.eps[:], EPS), writes=[g.eps])
    small = {}
    for name, shape in [("bada", [128, DEPTH, 48]), ("gvec", [128, DEPTH, 4, 8]), ("convp", [128, DEPTH, 8, 4]),
                        ("dtb", [128, DEPTH, 16]), ("alog", [128, DEPTH, 16]), ("dfull", [128, DEPTH, 512]),
                        ("gssd", [128, DEPTH, 512]), ("s5a", [128, DEPTH, 2, 32]), ("s5dt", [128, DEPTH, 32]),
                        ("s5d", [128, DEPTH, 4]), ("glub", [128, DEPTH, 4]), ("cwf", [128, DEPTH, 2 * NJ, 4]),
                        ("cvec", [128, 8, 2])]:
        t = P.sb(shape)
        src = getattr(g, name)
        P.dma("sp", t[:], src, writes=[t])
        small[name] = t
    g.sm = small
    g.abc = P.sb([128, DEPTH, 16])
    P.op("act", lambda e: e.activation(g.abc[:], small["alog"][:], AF.Exp), reads=[small["alog"]], writes=[g.abc])
    P.op("dve", lambda e: e.tensor_scalar(g.abc[:], g.abc[:], -1.0, None, ALU.mult), reads=[g.abc], writes=[g.abc])
    g.sc = P.sb([128, 8, 2])
    P.op("act", lambda e: e.activation(g.sc[:], small["cvec"][:], AF.Silu), reads=[small["cvec"]], writes=[g.sc])
    g.mod = P.sb([128, DEPTH, 6, 8, 2])
    g.HT = [P.sb([128, 512]) for _ in range(2)]
    g.HTb = [P.sb([128, 512], BF16) for _ in range(2)]

    stage_ada(nc, g)
    if debug_stop == "ada":
        return
    xsrc = g.xc0
    for l in range(DEPTH):
        last = l == DEPTH - 1
        xdst = g.out if last else g.xb
        stage_inproj(nc, g, l, xsrc)
        if debug_stop == f"inproj{l}":
            return
        stage_conv(nc, g, l)
        if debug_stop == f"conv{l}":
            return
        for d in range(2):
            stage_ssd(nc, g, l, d)
        if debug_stop == f"ssd{l}":
            return
        stage_s5(nc, g, l)
        if debug_stop == f"s5{l}":
            return
        stage_outproj(nc, g, l, xsrc, g.xa, last)
        if debug_stop == f"outproj{l}":
            return
        stage_ffn(nc, g, l, g.xa, xdst, last)
        if debug_stop == f"ffn{l}":
            return
        xsrc = g.xb


def stage_ada(nc, g):
    P = g.P
    sm = g.sm
    with ExitStack() as st:
        old = P.stack
        P.stack = st
        wb = [P.sb([128, 8, 512]) for _ in range(2)]
        ada = P.sb([128, 48, 2])
        tmp = P.sb([128, 8, 2])
        for l in range(DEPTH):
            for pc in range(12):
                w = wb[pc % 2]
                P.dma("sp", w[:], g.w_ada[l, :, pc * 512:(pc + 1) * 512].rearrange("(k p) f -> p k f", p=128), writes=[w])
                for jj in range(4):
                    j = pc * 4 + jj
                    ps = P.nextps()
                    for k in range(8):
                        P.op("pe", lambda e, ps=ps, w=w, k=k, jj=jj: e.matmul(ps[:, 0:2], w[:, k, jj * 128:(jj + 1) * 128], g.sc[:, k, :], start=(k == 0), stop=(k == 7)),
                             reads=[w, g.sc], writes=[ps])
                    P.op("dve", lambda e, ps=ps, j=j, l=l: e.tensor_tensor(ada[:, j, :], ps[:, 0:2], sm["bada"][:, l, j:j + 1].to_broadcast([128, 2]), ALU.add),
                         reads=[ps, sm["bada"]], writes=[ada])
            gv = sm["gvec"]
            md = g.mod
            for (dst, scl, gi) in [(0, 8, 0), (3, 32, 2)]:
                P.op("dve", lambda e, scl=scl: e.tensor_scalar(tmp[:], ada[:, scl:scl + 8, :], 1.0, None, ALU.add), reads=[ada], writes=[tmp])
                P.op("dve", lambda e, dst=dst, gi=gi, l=l: e.tensor_tensor(md[:, l, dst, :, :], tmp[:], gv[:, l, gi, :].unsqueeze(2).to_broadcast([128, 8, 2]), ALU.mult),
                     reads=[tmp, gv], writes=[md])
            for (dst, src) in [(1, 0), (4, 24)]:
                P.op("dve", lambda e, dst=dst, src=src, l=l: e.tensor_copy(md[:, l, dst, :, :], ada[:, src:src + 8, :]), reads=[ada], writes=[md])
            for (dst, src, gi) in [(2, 16, 1), (5, 40, 3)]:
                P.op("dve", lambda e, dst=dst, src=src, gi=gi, l=l: e.tensor_tensor(md[:, l, dst, :, :], ada[:, src:src + 8, :], gv[:, l, gi, :].unsqueeze(2).to_broadcast([128, 8, 2]), ALU.mult),
                     reads=[ada, gv], writes=[md])
        P.barrier()
        P.stack = old


def load_weight_bf16(P, dst, dst_cols, src_ap_fn, ncols, stg, piece=512):
    i = 0
    c0 = 0
    while c0 < ncols:
        c1 = min(ncols, c0 + piece)
        s = stg[i % len(stg)]
        i += 1
        w = c1 - c0
        P.dma("sp", s[:, :, 0:w], src_ap_fn(c0, c1), writes=[s])
        P.op("pool", lambda e, s=s, c0=c0, c1=c1, w=w: e.tensor_copy(dst[:, :, dst_cols + c0:dst_cols + c1], s[:, :, 0:w]), reads=[s], writes=[dst])
        c0 = c1


def norm_mod(P, g, xt, hT, TW, sq, tmp, rstd, s_ap, sh_ap, hoff=0):
    P.op("act", lambda e: e.activation(sq[:, :, 0:TW], xt[:, :, 0:TW], AF.Square), reads=[xt], writes=[sq])
    ps = P.nextps()
    for k in range(8):
        P.op("pe", lambda e, k=k: e.matmul(ps[:, 0:TW], g.onesb[:], sq[:, k, 0:TW], start=(k == 0), stop=(k == 7)), reads=[g.onesb, sq], writes=[ps])
    P.op("act", lambda e: e.activation(rstd[:, 0:TW], ps[:, 0:TW], AF.Sqrt, bias=g.eps[:], scale=1.0 / D), reads=[ps, g.eps], writes=[rstd])
    P.op("dve", lambda e: e.reciprocal(rstd[:, 0:TW], rstd[:, 0:TW]), reads=[rstd], writes=[rstd])
    for k in range(8):
        sa = s_ap(k)
        sha = sh_ap(k)
        P.op("dve", lambda e, k=k: e.tensor_tensor(tmp[:, k, 0:TW], xt[:, k, 0:TW], rstd[:, 0:TW], ALU.mult), reads=[xt, rstd], writes=[tmp])
        P.op("act", lambda e, k=k, sa=sa, sha=sha: e.activation(hT[:, k, hoff:hoff + TW], tmp[:, k, 0:TW], AF.Identity, bias=sha, scale=sa),
             reads=[tmp, g.mod], writes=[hT])


def stage_inproj(nc, g, l, xsrc):
    P = g.P
    sm = g.sm
    with ExitStack() as st:
        old = P.stack
        P.stack = st
        win = P.sb([128, 8, 2064], BF16)
        stg = [P.sb([128, 8, 512]) for _ in range(2)]
        load_weight_bf16(P, win, 0, lambda c0, c1: g.w_in[l, :, c0:c1].rearrange("(k p) f -> p k f", p=128), 2064, stg)
        xts = [P.sb([128, 8, 512]) for _ in range(2)]
        sq = P.sb([128, 8, 512], BF16)
        tmp = P.sb([128, 8, 512])
        rstd = P.sb([128, 512])
        hT = P.sb([128, 8, 512], BF16)
        zsb = [P.sb([128, 512]) for _ in range(2)]
        dts = [P.sb([128, 32]) for _ in range(2)]
        xo = [P.sb([128, 8, 512]) for _ in range(2)]
        uo = [P.sb([128, 4, 512]) for _ in range(2)]
        ev = 0
        for ti, (tok0, TW, v, s0, s1, rl) in enumerate(token_tiles()):
            xt = xts[ti % 2]
            P.dma("sp", xt[:, :, 0:TW], xsrc[:, tok0:tok0 + TW].rearrange("(k p) t -> p k t", p=128), writes=[xt])
            norm_mod(P, g, xt, hT, TW, sq, tmp, rstd,
                     lambda k: g.mod[:, l, 0, k, v:v + 1], lambda k: g.mod[:, l, 1, k, v:v + 1])
            for s in range(TW // 128):
                ps = P.nextps()
                for k in range(8):
                    P.op("pe", lambda e, ps=ps, k=k, s=s: e.matmul(ps[:, :], hT[:, k, s * 128:(s + 1) * 128], win[:, k, 0:512], start=(k == 0), stop=(k == 7)),
                         reads=[hT, win], writes=[ps])
                zb = zsb[s % 2]
                P.op("act", lambda e, ps=ps, zb=zb: e.activation(zb[:], ps[:, :], AF.Silu), reads=[ps], writes=[zb])
                P.dma("pool", g.zs[tok0 + s * 128:tok0 + (s + 1) * 128, :], zb[:], reads=[zb])
                ps2 = P.nextps()
                for k in range(8):
                    P.op("pe", lambda e, ps2=ps2, k=k, s=s: e.matmul(ps2[:, 0:16], hT[:, k, s * 128:(s + 1) * 128], win[:, k, 1536:1552], start=(k == 0), stop=(k == 7)),
                         reads=[hT, win], writes=[ps2])
                db = dts[s % 2]
                P.op("dve", lambda e, ps2=ps2, db=db: e.tensor_tensor(db[:, 0:16], ps2[:, 0:16], sm["dtb"][:, l, :], ALU.add), reads=[ps2, sm["dtb"]], writes=[db])
                P.op("act", lambda e, db=db: e.activation(db[:, 0:16], db[:, 0:16], AF.Exp), reads=[db], writes=[db])
                P.op("act", lambda e, db=db: e.activation(db[:, 0:16], db[:, 0:16], AF.Ln, bias=1.0), reads=[db], writes=[db])
                P.op("dve", lambda e, db=db: e.tensor_tensor(db[:, 16:32], db[:, 0:16], g.abc[:, l, :], ALU.mult), reads=[db, g.abc], writes=[db])
                P.dma("pool", g.dtk[tok0 + s * 128:tok0 + (s + 1) * 128, :], db[:], reads=[db])
            xob = xo[ti % 2]
            uob = uo[ti % 2]
            for j in range(12):
                c0 = 512 + j * 128 if j < 8 else 1552 + (j - 8) * 128
                ps = P.nextps()
                for k in range(8):
                    P.op("pe", lambda e, ps=ps, k=k, c0=c0, TW=TW: e.matmul(ps[:, 0:TW], win[:, k, c0:c0 + 128], hT[:, k, 0:TW], start=(k == 0), stop=(k == 7)),
                         reads=[hT, win], writes=[ps])
                dstt = xob if j < 8 else uob
                jj = j if j < 8 else j - 8
                copy_op(P, "act" if ev % 2 == 0 else "dve", dstt[:, jj, 0:TW], ps[:, 0:TW], [ps], [dstt])
                ev += 1
            P.dma("pool", g.xbc[:, tok0:tok0 + TW].rearrange("(k p) t -> p k t", p=128), xob[:, :, 0:TW], reads=[xob])
            P.dma("pool", g.uT[:, tok0:tok0 + TW].rearrange("(k p) t -> p k t", p=128), uob[:, :, 0:TW], reads=[uob])
        P.barrier()
        P.stack = old


def stage_conv(nc, g, l):
    P = g.P
    cp = g.sm["convp"]
    with ExitStack() as st:
        old = P.stack
        P.stack = st
        xin = [P.sb([128, 8, 514]) for _ in range(2)]
        acc = [P.sb([128, 8, 512]) for _ in range(2)]
        for ti, (tok0, TW, v, s0, s1, rl) in enumerate(token_tiles()):
            xi = xin[ti % 2]
            ac = acc[ti % 2]
            P.op("pool", lambda e, xi=xi: e.memset(xi[:, :, 0:1], 0.0), writes=[xi])
            P.op("pool", lambda e, xi=xi, TW=TW: e.memset(xi[:, :, TW + 1:TW + 2], 0.0), writes=[xi])
            a = max(s0, tok0 - 1)
            b = min(s1, tok0 + TW + 1)
            P.dma("sp", xi[:, :, a - (tok0 - 1):b - (tok0 - 1)], g.xbc[:, a:b].rearrange("(k p) t -> p k t", p=128), writes=[xi])
            for j in range(8):
                P.op("dve", lambda e, xi=xi, ac=ac, j=j, TW=TW: e.tensor_scalar(ac[:, j, 0:TW], xi[:, j, 0:TW], cp[:, l, j, 0:1], cp[:, l, j, 3:4], ALU.mult, ALU.add),
                     reads=[xi, cp], writes=[ac])
                P.op("dve", lambda e, xi=xi, ac=ac, j=j, TW=TW: e.scalar_tensor_tensor(ac[:, j, 0:TW], xi[:, j, 1:TW + 1], cp[:, l, j, 1:2], ac[:, j, 0:TW], ALU.mult, ALU.add),
                     reads=[xi, cp, ac], writes=[ac])
                P.op("dve", lambda e, xi=xi, ac=ac, j=j, TW=TW: e.scalar_tensor_tensor(ac[:, j, 0:TW], xi[:, j, 2:TW + 2], cp[:, l, j, 2:3], ac[:, j, 0:TW], ALU.mult, ALU.add),
                     reads=[xi, cp, ac], writes=[ac])
            P.op("act", lambda e, ac=ac, TW=TW: e.activation(ac[:, :, 0:TW], ac[:, :, 0:TW], AF.Silu), reads=[ac], writes=[ac])
            P.dma("pool", g.xbcA[:, tok0:tok0 + TW].rearrange("(k p) t -> p k t", p=128), ac[:, :, 0:TW], reads=[ac])
        P.barrier()
        P.stack = old


STAGES = {}


def chunk_list(d):
    ctx = [c * 128 for c in range(LC // 128)]
    lat = [LC + c * 128 for c in range(L // 128)]
    if d == 0:
        return ctx + lat
    return ctx[::-1] + lat[::-1]


def stage_ssd(nc, g, l, d):
    P = g.P
    sm = g.sm
    msk = (lambda: g.cst[:, 1, :]) if d == 0 else (lambda: g.cst[:, 2, :])
    idx = 127 if d == 0 else 0
    HT = g.HT[d]
    HTb = g.HTb[d]
    with ExitStack() as st:
        old = P.stack
        P.stack = st
        P.op("dve", lambda e: e.memset(HT[:], 0.0), writes=[HT])
        P.op("dve", lambda e: e.memset(HTb[:], 0.0), writes=[HTb])
        xin = [P.sb([128, 8, 128]) for _ in range(2)]
        dtk = [P.sb([128, 32]) for _ in range(2)]
        yfl = [P.sb([128, 512]) for _ in range(2)]
        zsl = [P.sb([128, 512]) for _ in range(2)]
        xtok = P.sb([128, 512])
        btok = P.sb([128, 256], BF16)
        R = P.sb([128, 1024])
        seg = P.sb([128, 1024])
        cbt = P.sb([128, 256])
        MT = P.sb([128, 1024], BF16)
        E = P.sb([128, 1024])
        CTs = P.sb([128, 1024], BF16)
        xdt = P.sb([128, 512], BF16)
        xdtw = P.sb([128, 512], BF16)
        te = P.sb([128, 8])
        cumtok = P.sb([128, 8])
        dx = P.sb([128, 512])
        yo = [P.sb([128, 512]) for _ in range(2)]
        htmp = P.sb([128, 512])
        ss = P.sb([128, 1])
        junk = P.sb([128, 512])
        yst = [P.sb([128, 4, 128], BF16) for _ in range(2)]
        for ci, tok in enumerate(chunk_list(d)):
            xi = xin[ci % 2]
            dk = dtk[ci % 2]
            P.dma("sp", xi[:], g.xbcA[:, tok:tok + 128].rearrange("(k p) t -> p k t", p=128), writes=[xi])
            P.dma("sp", dk[:], g.dtk[tok:tok + 128, :], writes=[dk])
            if d == 1:
                yl = yfl[ci % 2]
                zl = zsl[ci % 2]
                P.dma("sp", yl[:], g.yf[tok:tok + 128, :], writes=[yl])
                P.dma("sp", zl[:], g.zs[tok:tok + 128, :], writes=[zl])
            dtA = lambda dk=dk: dk[:, 16 + d * 8:24 + d * 8]
            dtv = lambda dk=dk: dk[:, d * 8:d * 8 + 8]
            psX = P.nextps()
            for j in range(4):
                P.op("pe", lambda e, j=j, xi=xi, psX=psX: e.transpose(psX[:, j * 128:(j + 1) * 128], xi[:, j, :], g.ident()), reads=[xi, g.cst], writes=[psX])
            P.op("act", lambda e, psX=psX: e.copy(xtok[:], psX[:, :]), reads=[psX], writes=[xtok])
            psB = P.nextps()
            for gg in range(2):
                P.op("pe", lambda e, gg=gg, xi=xi, psB=psB: e.transpose(psB[:, gg * 128:(gg + 1) * 128], xi[:, 4 + gg, :], g.ident()), reads=[xi, g.cst], writes=[psB])
            P.op("dve", lambda e, psB=psB: e.tensor_copy(btok[:], psB[:, 0:256]), reads=[psB], writes=[btok])
            P.op("dve", lambda e, dtA=dtA: e.tensor_tensor(R[:].rearrange("p (h i) -> p h i", h=8), msk().unsqueeze(1).to_broadcast([128, 8, 128]),
                                                          dtA().unsqueeze(2).to_broadcast([128, 8, 128]), ALU.mult), reads=[g.cst, dk], writes=[R])
            cum = [P.nextps(), P.nextps()]
            for hh in range(2):
                P.op("pe", lambda e, hh=hh, cum=cum: e.matmul(cum[hh][:, :], g.onesf(), R[:, hh * 512:(hh + 1) * 512], start=True, stop=True), reads=[g.cst, R], writes=[cum[hh]])
            psS = P.nextps()
            P.op("pe", lambda e, psS=psS, dtA=dtA: e.matmul(psS[:, 256:264], msk(), dtA(), start=True, stop=True), reads=[g.cst, dk], writes=[psS])
            for gg in range(2):
                P.op("pe", lambda e, psS=psS, gg=gg, xi=xi: e.matmul(psS[:, gg * 128:(gg + 1) * 128], xi[:, 4 + gg, :], xi[:, 6 + gg, :], start=True, stop=True), reads=[xi], writes=[psS])
            P.op("act", lambda e, psS=psS: e.copy(cumtok[:], psS[:, 256:264]), reads=[psS], writes=[cumtok])
            for h in range(8):
                P.op("dve", lambda e, h=h, cum=cum: e.tensor_scalar(seg[:, h * 128:(h + 1) * 128], cum[h // 4][:, (h % 4) * 128:(h % 4 + 1) * 128], cumtok[:, h:h + 1], 0.0, ALU.subtract, ALU.min),
                     reads=[cum[h // 4], cumtok], writes=[seg])
            P.op("act", lambda e: e.activation(seg[:], seg[:], AF.Exp), reads=[seg], writes=[seg])
            P.op("dve", lambda e, psS=psS: e.tensor_tensor(cbt[:].rearrange("p (g i) -> p g i", g=2), psS[:, 0:256].rearrange("p (g i) -> p g i", g=2),
                                                          msk().unsqueeze(1).to_broadcast([128, 2, 128]), ALU.mult), reads=[psS, g.cst], writes=[cbt])
            P.op("dve", lambda e: e.tensor_tensor(MT[:].rearrange("p (g e i) -> p g e i", g=2, e=4), seg[:].rearrange("p (g e i) -> p g e i", g=2, e=4),
                                                 cbt[:].rearrange("p (g i) -> p g i", g=2).unsqueeze(2).to_broadcast([128, 2, 4, 128]), ALU.mult), reads=[seg, cbt], writes=[MT])
            for hh in range(2):
                P.op("act", lambda e, hh=hh, cum=cum: e.activation(E[:, hh * 512:(hh + 1) * 512], cum[hh][:, :], AF.Exp), reads=[cum[hh]], writes=[E])
            P.op("dve", lambda e, xi=xi: e.tensor_tensor(CTs[:].rearrange("p (g e i) -> p g e i", g=2, e=4), E[:].rearrange("p (g e i) -> p g e i", g=2, e=4),
                                                        xi[:, 6:8, :].unsqueeze(2).to_broadcast([128, 2, 4, 128]), ALU.mult), reads=[E, xi], writes=[CTs])
            P.op("dve", lambda e, dtv=dtv: e.tensor_tensor(xdt[:].rearrange("p (h q) -> p h q", h=8), xtok[:].rearrange("p (h q) -> p h q", h=8),
                                                          dtv().unsqueeze(2).to_broadcast([128, 8, 64]), ALU.mult), reads=[xtok, dk], writes=[xdt])
            for hh in range(2):
                P.op("dve", lambda e, hh=hh, cum=cum: e.tensor_tensor(te[:, hh * 4:(hh + 1) * 4], cum[hh][:, :].rearrange("p (h i) -> p h i", h=4)[:, :, idx], cumtok[:, hh * 4:(hh + 1) * 4], ALU.subtract),
                     reads=[cum[hh], cumtok], writes=[te])
            P.op("act", lambda e: e.activation(te[:], te[:], AF.Exp), reads=[te], writes=[te])
            P.op("dve", lambda e: e.tensor_tensor(xdtw[:].rearrange("p (h q) -> p h q", h=8), xdt[:].rearrange("p (h q) -> p h q", h=8),
                                                 te[:].unsqueeze(2).to_broadcast([128, 8, 64]), ALU.mult), reads=[xdt, te], writes=[xdtw])
            psY = P.nextps()
            for h in range(8):
                P.op("pe", lambda e, h=h, psY=psY: e.matmul(psY[:, h * 64:(h + 1) * 64], MT[:, h * 128:(h + 1) * 128], xdt[:, h * 64:(h + 1) * 64], start=True, stop=False),
                     reads=[MT, xdt], writes=[psY])
                P.op("pe", lambda e, h=h, psY=psY: e.matmul(psY[:, h * 64:(h + 1) * 64], CTs[:, h * 128:(h + 1) * 128], HTb[:, h * 64:(h + 1) * 64], start=False, stop=True),
                     reads=[CTs, HTb], writes=[psY])
            psH = P.nextps()
            for gg in range(2):
                P.op("pe", lambda e, gg=gg, psH=psH: e.matmul(psH[:, gg * 256:(gg + 1) * 256], btok[:, gg * 128:(gg + 1) * 128], xdtw[:, gg * 256:(gg + 1) * 256], start=True, stop=True),
                     reads=[btok, xdtw], writes=[psH])
            P.op("dve", lambda e: e.tensor_tensor(htmp[:].rearrange("p (h q) -> p h q", h=8), HT[:].rearrange("p (h q) -> p h q", h=8),
                                                 E[:].rearrange("p (h i) -> p h i", h=8)[:, :, idx:idx + 1].to_broadcast([128, 8, 64]), ALU.mult), reads=[HT, E], writes=[htmp])
            P.op("dve", lambda e, psH=psH: e.tensor_tensor(HT[:], htmp[:], psH[:, :], ALU.add), reads=[htmp, psH], writes=[HT])
            P.op("act", lambda e: e.copy(HTb[:], HT[:]), reads=[HT], writes=[HTb])
            y = yo[ci % 2]
            if d == 0:
                P.op("dve", lambda e: e.tensor_tensor(dx[:], xtok[:], sm["dfull"][:, l, :], ALU.mult), reads=[xtok, sm["dfull"]], writes=[dx])
                P.op("dve", lambda e, y=y, psY=psY: e.tensor_tensor(y[:], psY[:, :], dx[:], ALU.add), reads=[psY, dx], writes=[y])
                P.dma("pool", g.yf[tok:tok + 128, :], y[:], reads=[y])
            else:
                P.op("dve", lambda e, y=y, psY=psY, yl=yl: e.tensor_tensor(y[:], psY[:, :], yl[:], ALU.add), reads=[psY, yl], writes=[y])
                P.op("dve", lambda e, y=y, zl=zl: e.tensor_tensor(y[:], y[:], zl[:], ALU.mult), reads=[y, zl], writes=[y])
                P.op("act", lambda e, y=y: e.activation(junk[:], y[:], AF.Square, accum_out=ss[:]), reads=[y], writes=[junk, ss])
                P.op("act", lambda e: e.activation(ss[:], ss[:], AF.Sqrt, bias=g.eps[:], scale=1.0 / 512), reads=[ss, g.eps], writes=[ss])
                P.op("dve", lambda e: e.reciprocal(ss[:], ss[:]), reads=[ss], writes=[ss])
                P.op("dve", lambda e, y=y: e.scalar_tensor_tensor(y[:], y[:], ss[:, 0:1], sm["gssd"][:, l, :], ALU.mult, ALU.mult), reads=[y, ss, sm["gssd"]], writes=[y])
                psT = P.nextps()
                for j in range(4):
                    P.op("pe", lambda e, j=j, y=y, psT=psT: e.transpose(psT[:, j * 128:(j + 1) * 128], y[:, j * 128:(j + 1) * 128], g.ident()), reads=[y, g.cst], writes=[psT])
                ys = yst[ci % 2]
                P.op("act", lambda e, ys=ys, psT=psT: e.copy(ys[:].rearrange("p j t -> p (j t)"), psT[:, :]), reads=[psT], writes=[ys])
                P.dma("pool", g.ysT[:, tok:tok + 128].rearrange("(k p) t -> p k t", p=128), ys[:], reads=[ys])
        P.barrier()
        P.stack = old


def stage_s5(nc, g, l):
    P = g.P
    sm = g.sm
    tiles = token_tiles()
    with ExitStack() as st:
        old = P.stack
        P.stack = st
        Bexp = P.sb([128, 2, 16, 2, 128], BF16)
        Cexp = P.sb([128, 2, 16, 128], BF16)
        pw = P.sb([128, NSTEP, 3, 32])
        with ExitStack() as st2:
            P.stack = st2
            NT = 14
            tt = [P.sb([128, 32]) for _ in range(NT)]
            step, xr, xi_, mag, cs, sn, t1, t2, t3, nr, den, cr, ci, nci = tt
            a_re = lambda: sm["s5a"][:, l, 0, :]
            a_im = lambda: sm["s5a"][:, l, 1, :]
            TTm = lambda o, a, b, opx, rd, wr: P.op("dve", lambda e: e.tensor_tensor(o(), a(), b(), opx), reads=rd, writes=wr)
            P.op("act", lambda e: e.activation(step[:], sm["s5dt"][:, l, :], AF.Exp), reads=[sm["s5dt"]], writes=[step])
            TTm(lambda: xr[:], a_re, lambda: step[:], ALU.mult, [sm["s5a"], step], [xr])
            TTm(lambda: xi_[:], a_im, lambda: step[:], ALU.mult, [sm["s5a"], step], [xi_])
            P.op("act", lambda e: e.activation(mag[:], xr[:], AF.Exp), reads=[xr], writes=[mag])
            P.op("act", lambda e: e.activation(sn[:], xi_[:], AF.Sin, scale=1.0 / 16), reads=[xi_], writes=[sn])
            hp = P.sb([128, 1])
            P.op("dve", lambda e: e.memset(hp[:], math.pi / 2), writes=[hp])
            P.op("act", lambda e: e.activation(cs[:], xi_[:], AF.Sin, bias=hp[:], scale=1.0 / 16), reads=[xi_, hp], writes=[cs])
            for _ in range(4):
                TTm(lambda: t1[:], lambda: cs[:], lambda: cs[:], ALU.mult, [cs], [t1])
                TTm(lambda: t2[:], lambda: sn[:], lambda: sn[:], ALU.mult, [sn], [t2])
                P.op("dve", lambda e: e.scalar_tensor_tensor(t3[:], sn[:], 2.0, cs[:], ALU.mult, ALU.mult), reads=[sn, cs], writes=[t3])
                TTm(lambda: cs[:], lambda: t1[:], lambda: t2[:], ALU.subtract, [t1, t2], [cs])
                P.op("dve", lambda e: e.tensor_copy(sn[:], t3[:]), reads=[t3], writes=[sn])
            TTm(lambda: pw[:, 0, 0, :], lambda: mag[:], lambda: cs[:], ALU.mult, [mag, cs], [pw])
            TTm(lambda: pw[:, 0, 1, :], lambda: mag[:], lambda: sn[:], ALU.mult, [mag, sn], [pw])
            for s_ in range(1, NSTEP):
                TTm(lambda s_=s_: t1[:], lambda s_=s_: pw[:, s_ - 1, 0, :], lambda s_=s_: pw[:, s_ - 1, 0, :], ALU.mult, [pw], [t1])
                TTm(lambda s_=s_: t2[:], lambda s_=s_: pw[:, s_ - 1, 1, :], lambda s_=s_: pw[:, s_ - 1, 1, :], ALU.mult, [pw], [t2])
                P.op("dve", lambda e, s_=s_: e.scalar_tensor_tensor(pw[:, s_, 1, :], pw[:, s_ - 1, 0, :], 2.0, pw[:, s_ - 1, 1, :], ALU.mult, ALU.mult), reads=[pw], writes=[pw])
                TTm(lambda s_=s_: pw[:, s_, 0, :], lambda: t1[:], lambda: t2[:], ALU.subtract, [t1, t2, pw], [pw])
            P.op("dve", lambda e: e.tensor_scalar(pw[:, :, 2, :], pw[:, :, 1, :], -1.0, None, ALU.mult), reads=[pw], writes=[pw])
            P.op("dve", lambda e: e.tensor_scalar(nr[:], pw[:, 0, 0, :], -1.0, None, ALU.add), reads=[pw], writes=[nr])
            TTm(lambda: t1[:], a_re, a_re, ALU.mult, [sm["s5a"]], [t1])
            TTm(lambda: t2[:], a_im, a_im, ALU.mult, [sm["s5a"]], [t2])
            TTm(lambda: den[:], lambda: t1[:], lambda: t2[:], ALU.add, [t1, t2], [den])
            P.op("dve", lambda e: e.reciprocal(den[:], den[:]), reads=[den], writes=[den])
            TTm(lambda: t1[:], lambda: nr[:], a_re, ALU.mult, [nr, sm["s5a"]], [t1])
            TTm(lambda: t2[:], lambda: pw[:, 0, 1, :], a_im, ALU.mult, [pw, sm["s5a"]], [t2])
            TTm(lambda: t1[:], lambda: t1[:], lambda: t2[:], ALU.add, [t1, t2], [t1])
            TTm(lambda: cr[:], lambda: t1[:], lambda: den[:], ALU.mult, [t1, den], [cr])
            TTm(lambda: t1[:], lambda: pw[:, 0, 1, :], a_re, ALU.mult, [pw, sm["s5a"]], [t1])
            TTm(lambda: t2[:], lambda: nr[:], a_im, ALU.mult, [nr, sm["s5a"]], [t2])
            TTm(lambda: t1[:], lambda: t1[:], lambda: t2[:], ALU.subtract, [t1, t2], [t1])
            TTm(lambda: ci[:], lambda: t1[:], lambda: den[:], ALU.mult, [t1, den], [ci])
            P.op("dve", lambda e: e.tensor_scalar(nci[:], ci[:], -1.0, None, ALU.mult), reads=[ci], writes=[nci])
            bst = P.sb([128, 32, 128])
            cst_ = P.sb([128, 32, 128])
            P.dma("sp", bst[:], g.bpad[l].rearrange("pl gp m k -> m (pl gp) k"), writes=[bst])
            P.dma("sp", cst_[:], g.cpad[l].rearrange("pl gp m k -> m (pl gp) k"), writes=[cst_])
            P.op("pool", lambda e: e.tensor_copy(Cexp[:, 0, :, :], cst_[:, 0:16, :]), reads=[cst_], writes=[Cexp])
            P.op("pool", lambda e: e.tensor_scalar(Cexp[:, 1, :, :], cst_[:, 16:32, :], -1.0, None, ALU.mult), reads=[cst_], writes=[Cexp])
            dg = [P.sb([128, 3, 128]) for _ in range(2)]
            for d in range(2):
                for gp in range(16):
                    col = d * 16 + gp
                    dd = dg[col % 2]
                    for i, src in enumerate([cr, ci, nci]):
                        P.op("dve", lambda e, dd=dd, i=i, src=src, col=col: e.tensor_scalar(dd[:, i, :], g.ident(), src[:, col:col + 1], None, ALU.mult), reads=[g.cst, src], writes=[dd])
                    psr = P.nextps()
                    P.op("pe", lambda e, psr=psr, gp=gp, dd=dd: e.matmul(psr[:, 0:128], bst[:, gp, :], dd[:, 0, :], start=True, stop=False), reads=[bst, dd], writes=[psr])
                    P.op("pe", lambda e, psr=psr, gp=gp, dd=dd: e.matmul(psr[:, 0:128], bst[:, 16 + gp, :], dd[:, 2, :], start=False, stop=True), reads=[bst, dd], writes=[psr])
                    P.op("pe", lambda e, psr=psr, gp=gp, dd=dd: e.matmul(psr[:, 128:256], bst[:, gp, :], dd[:, 1, :], start=True, stop=False), reads=[bst, dd], writes=[psr])
                    P.op("pe", lambda e, psr=psr, gp=gp, dd=dd: e.matmul(psr[:, 128:256], bst[:, 16 + gp, :], dd[:, 0, :], start=False, stop=True), reads=[bst, dd], writes=[psr])
                    P.op("act", lambda e, psr=psr, gp=gp, d=d: e.copy(Bexp[:, d, gp, :, :].rearrange("p a b -> p (a b)"), psr[:, 0:256]), reads=[psr], writes=[Bexp])
            P.barrier()
            P.stack = st
        A = P.sb([128, 2, LT])
        B = P.sb([128, 2, LT])
        Rf = P.sb([128, 2, LT], BF16)
        Hs = P.sb([128, 2, LT], BF16)
        ubf = P.sb([128, LT], BF16)
        ust = [P.sb([128, 512]) for _ in range(2)]
        y5 = P.sb([128, LT])
        tg = P.sb([128, 512])
        go = [P.sb([128, 512], BF16) for _ in range(2)]
        ev = 0
        for rt in range(4):
            for ti, (tok0, TW, v, s0, s1, rl) in enumerate(tiles):
                u_ = ust[ti % 2]
                P.dma("sp", u_[:, 0:TW], g.uT[rt * 128:(rt + 1) * 128, tok0:tok0 + TW], writes=[u_])
                P.op("pool", lambda e, u_=u_, tok0=tok0, TW=TW: e.tensor_copy(ubf[:, tok0:tok0 + TW], u_[:, 0:TW]), reads=[u_], writes=[ubf])
            for q in range(4):
                gp = rt * 4 + q
                for d in range(2):
                    col = d * 16 + gp
                    for (tok0, TW, v, s0, s1, rl) in tiles:
                        pos = tok0 if d == 0 else (tok0 - LC if tok0 >= LC else L)
                        for pl in range(2):
                            ps = P.nextps()
                            P.op("pe", lambda e, ps=ps, d=d, gp=gp, pl=pl, tok0=tok0, TW=TW: e.matmul(ps[:, 0:TW], Bexp[:, d, gp, pl, :], ubf[:, tok0:tok0 + TW], start=True, stop=True),
                                 reads=[Bexp, ubf], writes=[ps])
                            copy_op(P, "act" if ev % 2 == 0 else "dve", A[:, pl, pos:pos + TW], ps[:, 0:TW], [ps], [A])
                            ev += 1
                    src, dst = A, B
                    for s_ in range(NSTEP):
                        sh = 1 << s_
                        n = LT - sh
                        ar = pw[:, s_, 0, col:col + 1]
                        ai = pw[:, s_, 1, col:col + 1]
                        nai = pw[:, s_, 2, col:col + 1]
                        if d == 0:
                            keep = (0, sh)
                            o0, i0 = sh, 0
                        else:
                            keep = (n, LT)
                            o0, i0 = 0, sh
                        P.op("act", lambda e, src=src, dst=dst, keep=keep: e.copy(dst[:, :, keep[0]:keep[1]], src[:, :, keep[0]:keep[1]]), reads=[src], writes=[dst])
                        P.op("dve", lambda e, src=src, dst=dst, ar=ar, o0=o0, i0=i0, n=n: e.scalar_tensor_tensor(dst[:, 0, o0:o0 + n], src[:, 0, i0:i0 + n], ar, src[:, 0, o0:o0 + n], ALU.mult, ALU.add), reads=[src, pw], writes=[dst])
                        P.op("dve", lambda e, src=src, dst=dst, nai=nai, o0=o0, i0=i0, n=n: e.scalar_tensor_tensor(dst[:, 0, o0:o0 + n], src[:, 1, i0:i0 + n], nai, dst[:, 0, o0:o0 + n], ALU.mult, ALU.add), reads=[src, pw, dst], writes=[dst])
                        P.op("dve", lambda e, src=src, dst=dst, ai=ai, o0=o0, i0=i0, n=n: e.scalar_tensor_tensor(dst[:, 1, o0:o0 + n], src[:, 0, i0:i0 + n], ai, src[:, 1, o0:o0 + n], ALU.mult, ALU.add), reads=[src, pw, dst], writes=[dst])
                        P.op("dve", lambda e, src=src, dst=dst, ar=ar, o0=o0, i0=i0, n=n: e.scalar_tensor_tensor(dst[:, 1, o0:o0 + n], src[:, 1, i0:i0 + n], ar, dst[:, 1, o0:o0 + n], ALU.mult, ALU.add), reads=[src, pw, dst], writes=[dst])
                        src, dst = dst, src
                    res = src
                    if d == 0:
                        P.op("act", lambda e, res=res: e.copy(Rf[:], res[:]), reads=[res], writes=[Rf])
                    else:
                        P.op("dve", lambda e, res=res: e.tensor_tensor(Hs[:, :, LC:LT], res[:, :, 0:L], Rf[:, :, LC:LT], ALU.add), reads=[res, Rf], writes=[Hs])
                        P.op("dve", lambda e, res=res: e.tensor_tensor(Hs[:, :, 0:LC], res[:, :, L:LT], Rf[:, :, 0:LC], ALU.add), reads=[res, Rf], writes=[Hs])
                for (tok0, TW, v, s0, s1, rl) in tiles:
                    ps = P.nextps()
                    for pl in range(2):
                        P.op("pe", lambda e, ps=ps, pl=pl, gp=gp, tok0=tok0, TW=TW: e.matmul(ps[:, 0:TW], Cexp[:, pl, gp, :], Hs[:, pl, tok0:tok0 + TW], start=(pl == 0), stop=(pl == 1)),
                             reads=[Cexp, Hs], writes=[ps])
                    if q == 0:
                        P.op("act", lambda e, ps=ps, tok0=tok0, TW=TW: e.copy(y5[:, tok0:tok0 + TW], ps[:, 0:TW]), reads=[ps], writes=[y5])
                    else:
                        P.op("dve", lambda e, ps=ps, tok0=tok0, TW=TW: e.tensor_tensor(y5[:, tok0:tok0 + TW], y5[:, tok0:tok0 + TW], ps[:, 0:TW], ALU.add), reads=[ps, y5], writes=[y5])
            for ti, (tok0, TW, v, s0, s1, rl) in enumerate(tiles):
                u_ = ust[ti % 2]
                gq = go[ti % 2]
                yv = lambda tok0=tok0, TW=TW: y5[:, tok0:tok0 + TW]
                P.dma("sp", u_[:, 0:TW], g.uT[rt * 128:(rt + 1) * 128, tok0:tok0 + TW], writes=[u_])
                P.op("dve", lambda e, u_=u_, yv=yv, TW=TW, rt=rt: e.scalar_tensor_tensor(yv(), u_[:, 0:TW], sm["s5d"][:, l, rt:rt + 1], yv(), ALU.mult, ALU.add), reads=[u_, sm["s5d"], y5], writes=[y5])
                P.op("dve", lambda e, yv=yv, TW=TW: e.tensor_tensor(tg[:, 0:TW], yv(), yv(), ALU.mult), reads=[y5], writes=[tg])
                P.op("dve", lambda e, TW=TW: e.tensor_scalar(tg[:, 0:TW], tg[:, 0:TW], 0.044715, 1.0, ALU.mult, ALU.add), reads=[tg], writes=[tg])
                P.op("dve", lambda e, yv=yv, TW=TW: e.tensor_tensor(tg[:, 0:TW], tg[:, 0:TW], yv(), ALU.mult), reads=[tg, y5], writes=[tg])
                P.op("act", lambda e, TW=TW: e.activation(tg[:, 0:TW], tg[:, 0:TW], AF.Sigmoid, scale=1.5957691216057308), reads=[tg], writes=[tg])
                P.op("dve", lambda e, gq=gq, yv=yv, TW=TW: e.tensor_tensor(gq[:, 0:TW], tg[:, 0:TW], yv(), ALU.mult), reads=[tg, y5], writes=[gq])
                P.dma("pool", g.g5T[rt * 128:(rt + 1) * 128, tok0:tok0 + TW], gq[:, 0:TW], reads=[gq])
        P.barrier()
        P.stack = old
    with ExitStack() as st:
        old = P.stack
        P.stack = st
        glw = P.sb([128, 4, 512], BF16)
        gst = [P.sb([128, 4, 512])]
        load_weight_bf16(P, glw, 0, lambda c0, c1: g.glu_w[l, :, c0:c1].rearrange("(k p) f -> p k f", p=128), 512, gst)
        gts = [P.sb([128, 4, 512], BF16) for _ in range(2)]
        yos = [P.sb([128, 4, 512], BF16) for _ in range(2)]
        sg = P.sb([128, 512])
        for ti, (tok0, TW, v, s0, s1, rl) in enumerate(tiles):
            gt = gts[ti % 2]
            yo_ = yos[ti % 2]
            P.dma("sp", gt[:, :, 0:TW], g.g5T[:, tok0:tok0 + TW].rearrange("(k p) t -> p k t", p=128), writes=[gt])
            for fo in range(4):
                ps = P.nextps()
                for k in range(4):
                    P.op("pe", lambda e, ps=ps, k=k, fo=fo, gt=gt, TW=TW: e.matmul(ps[:, 0:TW], glw[:, k, fo * 128:(fo + 1) * 128], gt[:, k, 0:TW], start=(k == 0), stop=(k == 3)),
                         reads=[glw, gt], writes=[ps])
                P.op("act", lambda e, ps=ps, fo=fo, TW=TW: e.activation(sg[:, 0:TW], ps[:, 0:TW], AF.Sigmoid, bias=sm["glub"][:, l, fo:fo + 1]), reads=[ps, sm["glub"]], writes=[sg])
                P.op("dve", lambda e, fo=fo, gt=gt, yo_=yo_, TW=TW: e.tensor_tensor(yo_[:, fo, 0:TW], sg[:, 0:TW], gt[:, fo, 0:TW], ALU.mult), reads=[sg, gt], writes=[yo_])
            P.dma("pool", g.y5T[:, tok0:tok0 + TW].rearrange("(k p) t -> p k t", p=128), yo_[:, :, 0:TW], reads=[yo_])
        P.barrier()
        P.stack = old


def post_norm_res(P, g, xm, xt, xn, TW, sq, tmp, rstd, gate_ap):
    P.op("act", lambda e: e.activation(sq[:, :, 0:TW], xm[:, :, 0:TW], AF.Square), reads=[xm], writes=[sq])
    ps = P.nextps()
    for k in range(8):
        P.op("pe", lambda e, k=k: e.matmul(ps[:, 0:TW], g.onesb[:], sq[:, k, 0:TW], start=(k == 0), stop=(k == 7)), reads=[g.onesb, sq], writes=[ps])
    P.op("act", lambda e: e.activation(rstd[:, 0:TW], ps[:, 0:TW], AF.Sqrt, bias=g.eps[:], scale=1.0 / D), reads=[ps, g.eps], writes=[rstd])
    P.op("dve", lambda e: e.reciprocal(rstd[:, 0:TW], rstd[:, 0:TW]), reads=[rstd], writes=[rstd])
    for k in range(8):
        ga = gate_ap(k)
        P.op("dve", lambda e, k=k: e.tensor_tensor(tmp[:, k, 0:TW], xm[:, k, 0:TW], rstd[:, 0:TW], ALU.mult), reads=[xm, rstd], writes=[tmp])
        P.op("dve", lambda e, k=k, ga=ga: e.scalar_tensor_tensor(xn[:, k, 0:TW], tmp[:, k, 0:TW], ga, xt[:, k, 0:TW], ALU.mult, ALU.add), reads=[tmp, xt, g.mod], writes=[xn])


def stage_outproj(nc, g, l, xsrc, xdst, last):
    P = g.P
    with ExitStack() as st:
        old = P.stack
        P.stack = st
        wo = P.sb([128, 8, 1024], BF16)
        stg = [P.sb([128, 8, 512]) for _ in range(2)]
        load_weight_bf16(P, wo, 0, lambda c0, c1: g.w_out[l, :, c0:c1].rearrange("(k p) f -> p k f", p=128), 1024, stg)
        xts = [P.sb([128, 8, 512]) for _ in range(2)]
        yin = [P.sb([128, 8, 512], BF16) for _ in range(2)]
        xm = P.sb([128, 8, 512])
        xn = [P.sb([128, 8, 512]) for _ in range(2)]
        sq = P.sb([128, 8, 512], BF16)
        tmp = P.sb([128, 8, 512])
        rstd = P.sb([128, 512])
        ev = 0
        for ti, (tok0, TW, v, s0, s1, rl) in enumerate(token_tiles(with_ctx=not last)):
            xt = xts[ti % 2]
            yi = yin[ti % 2]
            xo = xn[ti % 2]
            P.dma("sp", xt[:, :, 0:TW], xsrc[:, tok0:tok0 + TW].rearrange("(k p) t -> p k t", p=128), writes=[xt])
            P.dma("sp", yi[:, 0:4, 0:TW], g.ysT[:, tok0:tok0 + TW].rearrange("(k p) t -> p k t", p=128), writes=[yi])
            P.dma("sp", yi[:, 4:8, 0:TW], g.y5T[:, tok0:tok0 + TW].rearrange("(k p) t -> p k t", p=128), writes=[yi])
            for j in range(8):
                ps = P.nextps()
                for k in range(8):
                    P.op("pe", lambda e, ps=ps, k=k, j=j, yi=yi, TW=TW: e.matmul(ps[:, 0:TW], wo[:, k, j * 128:(j + 1) * 128], yi[:, k, 0:TW], start=(k == 0), stop=(k == 7)),
                         reads=[wo, yi], writes=[ps])
                copy_op(P, "act" if ev % 2 == 0 else "dve", xm[:, j, 0:TW], ps[:, 0:TW], [ps], [xm])
                ev += 1
            post_norm_res(P, g, xm, xt, xo, TW, sq, tmp, rstd, lambda k, v=v: g.mod[:, l, 2, k, v:v + 1])
            P.dma("pool", xdst[:, tok0:tok0 + TW].rearrange("(k p) t -> p k t", p=128), xo[:, :, 0:TW], reads=[xo])
        P.barrier()
        P.stack = old


def stage_ffn(nc, g, l, xsrc, xdst, last):
    P = g.P
    cw = g.sm["cwf"]
    tl = token_tiles(with_ctx=not last)
    supers = []
    if not last:
        supers.append([tl[0]])
        tl = tl[1:]
    for i in range(0, len(tl), 2):
        supers.append(tl[i:i + 2])
    for sup in supers:
        with ExitStack() as st:
            old = P.stack
            P.stack = st
            act = P.sb([128, NJ, 1024], BF16)
            with ExitStack() as st2:
                P.stack = st2
                hT = P.sb([128, 8, 1024], BF16)
                xts = [P.sb([128, 8, 512]) for _ in range(2)]
                sq = P.sb([128, 8, 512], BF16)
                tmp = P.sb([128, 8, 512])
                rstd = P.sb([128, 512])
                for si, (tok0, TW, v, s0, s1, rl) in enumerate(sup):
                    xt = xts[si % 2]
                    P.dma("sp", xt[:, :, 0:TW], xsrc[:, tok0:tok0 + TW].rearrange("(k p) t -> p k t", p=128), writes=[xt])
                    norm_mod(P, g, xt, hT, TW, sq, tmp, rstd,
                             lambda k, v=v: g.mod[:, l, 3, k, v:v + 1], lambda k, v=v: g.mod[:, l, 4, k, v:v + 1], hoff=si * 512)
                wst = [P.sb([128, 8, 256]) for _ in range(2)]
                wjb = [P.sb([128, 8, 256], BF16) for _ in range(2)]
                accg = P.sb([128, 512])
                accv = P.sb([128, 512])
                for j in range(NJ):
                    ws = wst[j % 2]
                    wj = wjb[j % 2]
                    P.dma("sp", ws[:, :, 0:128], g.w_up[l, :, j * 128:(j + 1) * 128].rearrange("(k p) f -> p k f", p=128), writes=[ws])
                    P.dma("sp", ws[:, :, 128:256], g.w_up[l, :, DFF + j * 128:DFF + (j + 1) * 128].rearrange("(k p) f -> p k f", p=128), writes=[ws])
                    P.op("pool", lambda e, ws=ws, wj=wj: e.tensor_copy(wj[:], ws[:]), reads=[ws], writes=[wj])
                    for si, (tok0, TW, v, s0, s1, rl) in enumerate(sup):
                        nr_ = TW // rl
                        pss = []
                        for half in range(2):
                            ps = P.nextps()
                            pss.append(ps)
                            for k in range(8):
                                P.op("pe", lambda e, ps=ps, k=k, wj=wj, half=half, si=si, TW=TW: e.matmul(ps[:, 0:TW], wj[:, k, half * 128:(half + 1) * 128], hT[:, k, si * 512:si * 512 + TW], start=(k == 0), stop=(k == 7)),
                                     reads=[wj, hT], writes=[ps])
                        for half, (ps, ac) in enumerate(zip(pss, [accg, accv])):
                            ch = j + half * NJ
                            v3 = lambda ap, a, b, TW=TW, rl=rl: ap[:, 0:TW].rearrange("p (r w) -> p r w", w=rl)[:, :, a:b]
                            P.op("dve", lambda e, ps=ps, ac=ac, ch=ch, TW=TW: e.tensor_scalar(ac[:, 0:TW], ps[:, 0:TW], cw[:, l, ch, 1:2], cw[:, l, ch, 3:4], ALU.mult, ALU.add), reads=[ps, cw], writes=[ac])
                            P.op("dve", lambda e, ps=ps, ac=ac, ch=ch, v3=v3, rl=rl: e.scalar_tensor_tensor(v3(ac, 1, rl), v3(ps, 0, rl - 1), cw[:, l, ch, 0:1], v3(ac, 1, rl), ALU.mult, ALU.add), reads=[ps, cw, ac], writes=[ac])
                            P.op("dve", lambda e, ps=ps, ac=ac, ch=ch, v3=v3, rl=rl: e.scalar_tensor_tensor(v3(ac, 0, rl - 1), v3(ps, 1, rl), cw[:, l, ch, 2:3], v3(ac, 0, rl - 1), ALU.mult, ALU.add), reads=[ps, cw, ac], writes=[ac])
                        P.op("act", lambda e, TW=TW: e.activation(accg[:, 0:TW], accg[:, 0:TW], AF.Silu), reads=[accg], writes=[accg])
                        P.op("dve", lambda e, j=j, si=si, TW=TW: e.tensor_tensor(act[:, j, si * 512:si * 512 + TW], accg[:, 0:TW], accv[:, 0:TW], ALU.mult), reads=[accg, accv], writes=[act])
                P.barrier()
                P.stack = st
            wds = [P.sb([128, NJ, 128]) for _ in range(2)]
            wdb = [P.sb([128, NJ, 128], BF16) for _ in range(2)]
            xm = P.sb([128, 8, 1024])
            xts = [P.sb([128, 8, 512]) for _ in range(1)]
            xn = [P.sb([128, 8, 512]) for _ in range(1)]
            sq = P.sb([128, 8, 512], BF16)
            tmp = P.sb([128, 8, 512])
            rstd = P.sb([128, 512])
            ev = 0
            for jo in range(8):
                ws = wds[jo % 2]
                wd = wdb[jo % 2]
                P.dma("sp", ws[:], g.w_down[l, :, jo * 128:(jo + 1) * 128].rearrange("(c p) f -> p c f", p=128), writes=[ws])
                P.op("pool", lambda e, ws=ws, wd=wd: e.tensor_copy(wd[:], ws[:]), reads=[ws], writes=[wd])
                for si, (tok0, TW, v, s0, s1, rl) in enumerate(sup):
                    ps = P.nextps()
                    for c in range(NJ):
                        P.op("pe", lambda e, ps=ps, c=c, wd=wd, si=si, TW=TW: e.matmul(ps[:, 0:TW], wd[:, c, :], act[:, c, si * 512:si * 512 + TW], start=(c == 0), stop=(c == NJ - 1)),
                             reads=[wd, act], writes=[ps])
                    copy_op(P, "act" if ev % 2 == 0 else "dve", xm[:, jo, si * 512:si * 512 + TW], ps[:, 0:TW], [ps], [xm])
                    ev += 1
            for si, (tok0, TW, v, s0, s1, rl) in enumerate(sup):
                xt = xts[0]
                xo = xn[0]
                P.dma("sp", xt[:, :, 0:TW], xsrc[:, tok0:tok0 + TW].rearrange("(k p) t -> p k t", p=128), writes=[xt])
                xmv = T.__new__(T)
                xmv.t = xm.t[:, :, si * 512:si * 512 + 512]
                xmv.d = xm.d
                post_norm_res(P, g, xmv, xt, xo, TW, sq, tmp, rstd, lambda k, v=v: g.mod[:, l, 5, k, v:v + 1])
                if last:
                    P.dma("pool", xdst[:, tok0 - LC:tok0 - LC + TW].rearrange("(k p) t -> p k t", p=128), xo[:, :, 0:TW], reads=[xo])
                else:
                    P.dma("pool", xdst[:, tok0:tok0 + TW].rearrange("(k p) t -> p k t", p=128), xo[:, :, 0:TW], reads=[xo])
            P.barrier()
            P.stack = old


def _colvec(v):
    return np.ascontiguousarray(np.asarray(v, np.float32).reshape(-1, 128).T)


def shared_layout(inp):
    f = lambda a: np.ascontiguousarray(np.asarray(a, np.float32))
    m = {}
    consts = np.zeros((128, 4, 128), np.float32)
    consts[:, 0] = np.eye(128)
    consts[:, 1] = np.triu(np.ones((128, 128)))
    consts[:, 2] = np.tril(np.ones((128, 128)))
    consts[:, 3] = 1.0
    m["consts"] = consts
    m["w_ada"] = f(inp["w_ada"])
    m["bada"] = np.ascontiguousarray(np.stack([_colvec(inp["b_ada"][l]) for l in range(DEPTH)], 1))
    gv = np.zeros((128, DEPTH, 4, 8), np.float32)
    for l in range(DEPTH):
        for i, nm in enumerate(["g_pre_mix", "g_post_mix", "g_pre_ffn", "g_post_ffn"]):
            gv[:, l, i, :] = _colvec(inp[nm][l])
    m["gvec"] = gv
    m["w_in"] = f(inp["w_in"])
    cp = np.zeros((128, DEPTH, 8, 4), np.float32)
    for l in range(DEPTH):
        for k in range(3):
            cp[:, l, :, k] = _colvec(inp["ssd_conv_w"][l, k])
        cp[:, l, :, 3] = _colvec(inp["ssd_conv_b"][l])
    m["convp"] = cp
    m["dtb"] = np.ascontiguousarray(np.broadcast_to(f(inp["ssd_dt_bias"]).reshape(1, DEPTH, 16), (128, DEPTH, 16)))
    m["alog"] = np.ascontiguousarray(np.broadcast_to(f(inp["ssd_a_log"]).reshape(1, DEPTH, 16), (128, DEPTH, 16)))
    m["dfull"] = np.ascontiguousarray(np.broadcast_to(np.repeat(f(inp["ssd_d"]), 64, axis=1).reshape(1, DEPTH, 512), (128, DEPTH, 512)))
    m["gssd"] = np.ascontiguousarray(np.broadcast_to(f(inp["ssd_norm_g"]).reshape(1, DEPTH, 512), (128, DEPTH, 512)))
    s5a = np.zeros((128, DEPTH, 2, 32), np.float32)
    s5dt = np.zeros((128, DEPTH, 32), np.float32)
    for l in range(DEPTH):
        for c, nm in enumerate(["s5_a_re", "s5_a_im"]):
            a = f(inp[nm][l]).reshape(2, 16, 2, 64)
            s5a[:, l, c, :] = a.transpose(2, 3, 0, 1).reshape(128, 32)
        ld = f(inp["s5_log_dt"][l]).reshape(2, 16, 2)
        s5dt[:, l, :] = np.repeat(ld.transpose(2, 0, 1).reshape(2, 1, 32), 64, axis=1).reshape(128, 32)
    m["s5a"] = s5a
    m["s5dt"] = s5dt
    bpad = np.zeros((DEPTH, 2, 16, 128, 128), np.float32)
    cpad = np.zeros((DEPTH, 2, 16, 128, 128), np.float32)
    for l in range(DEPTH):
        for pl, (bn, cn) in enumerate([("s5_b_re", "s5_c_re"), ("s5_b_im", "s5_c_im")]):
            bb = f(inp[bn][l])
            cc = f(inp[cn][l])
            for gp in range(16):
                q = gp % 4
                for g2 in range(2):
                    gi = 2 * gp + g2
                    col = (2 * q + g2) * 16
                    bpad[l, pl, gp, g2 * 64:(g2 + 1) * 64, col:col + 16] = bb[gi]
                    cpad[l, pl, gp, g2 * 64:(g2 + 1) * 64, col:col + 16] = cc[gi].T
    m["bpad"] = bpad
    m["cpad"] = cpad
    m["s5d"] = np.ascontiguousarray(np.stack([_colvec(inp["s5_d"][l]) for l in range(DEPTH)], 1))
    m["glu_w"] = f(inp["s5_glu_w"])
    m["glub"] = np.ascontiguousarray(np.stack([_colvec(inp["s5_glu_b"][l]) for l in range(DEPTH)], 1))
    m["w_out"] = f(inp["w_out"])
    m["w_up"] = f(inp["ffn_w_up"])
    cwf = np.zeros((128, DEPTH, 2 * NJ, 4), np.float32)
    for l in range(DEPTH):
        for k in range(3):
            cwf[:, l, :, k] = _colvec(inp["ffn_conv_w"][l, k])
        cwf[:, l, :, 3] = _colvec(inp["ffn_conv_b"][l])
    m["cwf"] = cwf
    m["w_down"] = f(inp["ffn_w_down"])
    return m


def core_layout(inp, b):
    x = np.asarray(inp["x"][b], np.float32)
    ctx = np.asarray(inp["ctx"][b], np.float32)
    xc0 = np.ascontiguousarray(np.concatenate([ctx.T, x.T], axis=1))
    cv = np.zeros((128, 8, 2), np.float32)
    cv[:, :, 0] = _colvec(inp["c"][b])
    cv[:, :, 1] = _colvec(inp["c_ctx"])
    return {"xc0": xc0, "cvec": cv}


_CACHE = {}


def kernel(**inputs):
    if "nc" not in _CACHE:
        _CACHE["nc"] = build_program(None)[0]
    nc = _CACHE["nc"]
    sh = shared_layout(inputs)
    in_maps = []
    for core in range(8):
        m = dict(sh)
        m.update(core_layout(inputs, core % 4))
        in_maps.append(m)
    res = run_bass_kernel_spmd(nc, in_maps, core_ids=list(range(8)))
    out = np.stack([np.ascontiguousarray(res.results[b]["out"].T) for b in range(4)], 0)
    return out.astype(np.float32)
```

```python
import math
import numpy as np
from contextlib import ExitStack
import concourse.bass as bass
import concourse.mybir as mybir
from concourse.bass_utils import run_bass_kernel_spmd

F32 = mybir.dt.float32
BF16 = mybir.dt.bfloat16
AF = mybir.ActivationFunctionType
ALU = mybir.AluOpType

NSLOT = 24
EPOCH = 20000

D = 1024
L = 4096
LC = 256
LT = L + LC
DEPTH = 2
DFF = 2816
NJ = DFF // 128
EPS = 1e-6
NSTEP = 13


class Dep:
    __slots__ = ("w", "r")

    def __init__(self):
        self.w = None
        self.r = {}


class T:
    def __init__(self, t):
        self.t = t
        self.d = Dep()

    def __getitem__(self, k):
        return self.t[k]


class Prog:
    ENGS = ("pe", "act", "dve", "pool", "sp")

    def __init__(self, nc, stack):
        self.nc = nc
        self.stack = stack
        self.eng = {"pe": nc.tensor, "act": nc.scalar, "dve": nc.vector, "pool": nc.gpsimd, "sp": nc.sync}
        self.count = {e: 0 for e in self.ENGS}
        self.glob = []
        self.seen = {e: {} for e in self.ENGS}
        self.slot_cnt = [0] * NSLOT
        self.next_slot = 0
        self.uid = 0
        self.psl = []
        self.psi = 0

    def sb(self, shape, dtype=F32):
        self.uid += 1
        t = self.stack.enter_context(self.nc.sbuf_tensor(f"sb{self.uid}", list(shape), dtype))
        return T(t)

    def nextps(self):
        p = self.psl[self.psi % len(self.psl)]
        self.psi += 1
        return p

    def _collect(self, eng, reads, writes):
        waits = {}

        def add(k):
            if k is None:
                return
            key, val = k
            if eng == "pe" and key == "pe":
                return
            if waits.get(key, 0) < val:
                waits[key] = val

        for d in reads:
            add(d.w)
        for d in writes:
            add(d.w)
            for key, val in d.r.items():
                add((key, val))
        out = {}
        seen = self.seen[eng]
        for key, val in waits.items():
            if seen.get(key, 0) < val:
                seen[key] = val
                out[key] = val
        return out

    def _mark(self, reads, writes, mykey):
        key, val = mykey
        for d in reads:
            if d.r.get(key, 0) < val:
                d.r[key] = val
        for d in writes:
            d.w = mykey
            d.r = {}

    def op(self, eng, fn, reads=(), writes=()):
        reads = [x.d for x in reads]
        writes = [x.d for x in writes]
        waits = self._collect(eng, reads, writes)
        self.count[eng] += 1
        mykey = (eng, self.count[eng])
        self.glob.append((eng, fn, waits, "c", mykey))
        self._mark(reads, writes, mykey)

    def dma(self, q, out_ap, in_ap, reads=(), writes=()):
        reads = [x.d for x in reads]
        writes = [x.d for x in writes]
        waits = self._collect(q, reads, writes)
        s = self.next_slot
        self.next_slot = (s + 1) % NSLOT
        key = ("dma", s)
        prev = 16 * self.slot_cnt[s]
        if prev > 0 and self.seen[q].get(key, 0) < prev:
            self.seen[q][key] = prev
            waits[key] = prev
        self.slot_cnt[s] += 1
        mykey = (key, 16 * self.slot_cnt[s])

        def fn(e, out_ap=out_ap, in_ap=in_ap):
            return e.dma_start(out=out_ap, in_=in_ap)

        self.glob.append((q, fn, waits, "d", mykey))
        self._mark(reads, writes, mykey)

    def barrier(self):
        for e in self.ENGS:
            waits = {}
            for e2 in self.ENGS:
                if e2 != e and self.count[e2] > 0 and self.seen[e].get(e2, 0) < self.count[e2]:
                    self.seen[e][e2] = self.count[e2]
                    waits[e2] = self.count[e2]
            for s in range(NSLOT):
                v = 16 * self.slot_cnt[s]
                key = ("dma", s)
                if v > 0 and self.seen[e].get(key, 0) < v:
                    self.seen[e][key] = v
                    waits[key] = v
            if waits:
                self.glob.append((e, None, waits, "w", None))

    def emit(self):
        nc = self.nc
        waited = {e: set() for e in self.ENGS}
        for eng, fn, waits, kind, mykey in self.glob:
            for key, val in waits.items():
                if key in waited:
                    waited[key].add(val)
        rank = {}
        sems = {}
        for e in self.ENGS:
            vals = sorted(waited[e])
            rank[e] = {v: i for i, v in enumerate(vals)}
            nep = (len(vals) + EPOCH - 1) // EPOCH
            sems[e] = [self.stack.enter_context(nc.semaphore(f"sem_{e}_{k}")) for k in range(max(nep, 1))]
        dsem = [self.stack.enter_context(nc.semaphore(f"sem_dma_{s}")) for s in range(NSLOT)]

        def semval(key, val):
            if isinstance(key, tuple):
                return dsem[key[1]], val
            r = rank[key][val]
            return sems[key][r // EPOCH], (r % EPOCH) + 1

        n = 0
        for eng, fn, waits, kind, mykey in self.glob:
            e = self.eng[eng]
            for key, val in waits.items():
                s, v = semval(key, val)
                e.wait_ge(s, v)
            if fn is None:
                continue
            inst = fn(e)
            n += 1
            if kind == "d":
                inst.then_inc(dsem[mykey[0][1]], 16)
            elif mykey[1] in rank[eng]:
                s, v = semval(eng, mykey[1])
                inst.then_inc(s, 1)
        e = self.eng["sp"]
        for s in range(NSLOT):
            if self.slot_cnt[s] > 0:
                e.wait_ge(dsem[s], 16 * self.slot_cnt[s])
        return n


def token_tiles(with_ctx=True):
    tl = []
    if with_ctx:
        tl.append((0, LC, 1, 0, LC, LC))
    for k in range(L // 512):
        tl.append((LC + 512 * k, 512, 0, LC, LT, 64))
    return tl


class K:
    pass


def build_program(debug_stop=None):
    nc = bass.Bass("TRN2", target_bir_lowering=False)
    g = K()

    def din(name, shape, dt=F32):
        return nc.dram_tensor(name, list(shape), dt, kind="ExternalInput").ap()

    def dscr(name, shape, dt=F32):
        kind = "ExternalOutput" if debug_stop is not None else "Internal"
        return nc.dram_tensor(name, list(shape), dt, kind=kind).ap()

    g.xc0 = din("xc0", [D, LT])
    g.cvec = din("cvec", [128, 8, 2])
    g.consts = din("consts", [128, 6, 128])
    g.w_ada = din("w_ada", [DEPTH, D, 6 * D])
    g.bada = din("bada", [128, DEPTH, 48])
    g.gvec = din("gvec", [128, DEPTH, 4, 8])
    g.w_in = din("w_in", [DEPTH, D, 2064])
    g.convp = din("convp", [128, DEPTH, 8, 4])
    g.dtb = din("dtb", [128, DEPTH, 16])
    g.alog = din("alog", [128, DEPTH, 16])
    g.dfull = din("dfull", [128, DEPTH, 512])
    g.gssd = din("gssd", [128, DEPTH, 512])
    g.s5a = din("s5a", [128, DEPTH, 2, 32])
    g.s5dt = din("s5dt", [128, DEPTH, 32])
    g.btab = din("btab", [128, DEPTH, 2, 16, 16])
    g.ctab = din("ctab", [128, DEPTH, 2, 16, 16])
    g.d5full = din("d5full", [128, DEPTH, 512])
    g.glu_w = din("glu_w", [DEPTH, 512, 512])
    g.glub = din("glub", [128, DEPTH, 4])
    g.w_out = din("w_out", [DEPTH, D, D])
    g.w_up = din("w_up", [DEPTH, D, 2 * DFF])
    g.cwf = din("cwf", [128, DEPTH, 2 * NJ, 4])
    g.w_down = din("w_down", [DEPTH, DFF, D])
    g.out = nc.dram_tensor("out", [D, L], F32, kind="ExternalOutput").ap()

    g.xa = dscr("xa", [D, LT])
    g.xb = dscr("xb", [D, LT])
    g.zs = dscr("zs", [LT, 512])
    g.dtk = dscr("dtk", [LT, 32])
    g.xbc = dscr("xbc", [1024, LT])
    g.xbcA = dscr("xbcA", [1024, LT])
    g.yf = dscr("yf", [LT, 512])
    g.ysT = dscr("ysT", [512, LT], BF16)
    g.y5T = dscr("y5T", [512, LT], BF16)
    g.uTok = dscr("uTok", [LT, 512])
    g.y5pre = dscr("y5pre", [LT, 512])

    with ExitStack() as st:
        P = Prog(nc, st)
        g.P = P
        for i in range(8):
            P.psl.append(T(st.enter_context(nc.psum_tensor(f"psb{i}", [128, 512], F32))))
        emit_all(nc, g, debug_stop)
        n = P.emit()
    return nc, n


def copy_op(P, eng, out_ap, in_ap, reads, writes):
    if eng == "act":
        P.op("act", lambda e: e.copy(out_ap, in_ap), reads=reads, writes=writes)
    else:
        P.op(eng, lambda e: e.tensor_copy(out_ap, in_ap), reads=reads, writes=writes)


def emit_all(nc, g, debug_stop):
    P = g.P
    g.cst = P.sb([128, 6, 128])
    P.dma("sp", g.cst[:], g.consts[:, :, :], writes=[g.cst])
    g.ident = lambda: g.cst[:, 0, :]
    g.onesf = lambda: g.cst[:, 3, :]
    g.onesb = P.sb([128, 128], BF16)
    P.op("dve", lambda e: e.tensor_copy(g.onesb[:], g.cst[:, 3, :]), reads=[g.cst], writes=[g.onesb])
    g.eps = P.sb([128, 1])
    P.op("dve", lambda e: e.memset(g.eps[:], EPS), writes=[g.eps])
    small = {}
    for name, shape in [("bada", [128, DEPTH, 48]), ("gvec", [128, DEPTH, 4, 8]), ("convp", [128, DEPTH, 8, 4]),
                        ("dtb", [128, DEPTH, 16]), ("alog", [128, DEPTH, 16]), ("dfull", [128, DEPTH, 512]),
                        ("gssd", [128, DEPTH, 512]), ("s5a", [128, DEPTH, 2, 32]), ("s5dt", [128, DEPTH, 32]),
                        ("btab", [128, DEPTH, 2, 16, 16]), ("ctab", [128, DEPTH, 2, 16, 16]), ("d5full", [128, DEPTH, 512]), ("glub", [128, DEPTH, 4]), ("cwf", [128, DEPTH, 2 * NJ, 4]),
                        ("cvec", [128, 8, 2])]:
        t = P.sb(shape)
        src = getattr(g, name)
        P.dma("sp", t[:], src, writes=[t])
        small[name] = t
    g.sm = small
    g.abc = P.sb([128, DEPTH, 16])
    P.op("act", lambda e: e.activation(g.abc[:], small["alog"][:], AF.Exp), reads=[small["alog"]], writes=[g.abc])
    P.op("dve", lambda e: e.tensor_scalar(g.abc[:], g.abc[:], -1.0, None, ALU.mult), reads=[g.abc], writes=[g.abc])
    g.sc = P.sb([128, 8, 2])
    P.op("act", lambda e: e.activation(g.sc[:], small["cvec"][:], AF.Silu), reads=[small["cvec"]], writes=[g.sc])
    g.mod = P.sb([128, DEPTH, 6, 8, 2])
    g.HT = [P.sb([128, 512]) for _ in range(2)]
    g.HTb = [P.sb([128, 512], BF16) for _ in range(2)]

    stage_ada(nc, g)
    if debug_stop == "ada":
        return
    xsrc = g.xc0
    for l in range(DEPTH):
        last = l == DEPTH - 1
        xdst = g.out if last else g.xb
        stage_inproj(nc, g, l, xsrc)
        if debug_stop == f"inproj{l}":
            return
        stage_conv(nc, g, l)
        if debug_stop == f"conv{l}":
            return
        for d in range(2):
            stage_ssd(nc, g, l, d)
        if debug_stop == f"ssd{l}":
            return
        stage_s5(nc, g, l, last)
        if S5STOP:
            return
        if debug_stop == f"s5{l}":
            return
        stage_outproj(nc, g, l, xsrc, g.xa, last)
        if debug_stop == f"outproj{l}":
            return
        stage_ffn(nc, g, l, g.xa, xdst, last)
        if debug_stop == f"ffn{l}":
            return
        xsrc = g.xb


def stage_ada(nc, g):
    P = g.P
    sm = g.sm
    with ExitStack() as st:
        old = P.stack
        P.stack = st
        wb = [P.sb([128, 8, 512]) for _ in range(2)]
        ada = P.sb([128, 48, 2])
        tmp = P.sb([128, 8, 2])
        for l in range(DEPTH):
            for pc in range(12):
                w = wb[pc % 2]
                P.dma("sp", w[:], g.w_ada[l, :, pc * 512:(pc + 1) * 512].rearrange("(k p) f -> p k f", p=128), writes=[w])
                for jj in range(4):
                    j = pc * 4 + jj
                    ps = P.nextps()
                    for k in range(8):
                        P.op("pe", lambda e, ps=ps, w=w, k=k, jj=jj: e.matmul(ps[:, 0:2], w[:, k, jj * 128:(jj + 1) * 128], g.sc[:, k, :], start=(k == 0), stop=(k == 7)),
                             reads=[w, g.sc], writes=[ps])
                    P.op("dve", lambda e, ps=ps, j=j, l=l: e.tensor_tensor(ada[:, j, :], ps[:, 0:2], sm["bada"][:, l, j:j + 1].to_broadcast([128, 2]), ALU.add),
                         reads=[ps, sm["bada"]], writes=[ada])
            gv = sm["gvec"]
            md = g.mod
            for (dst, scl, gi) in [(0, 8, 0), (3, 32, 2)]:
                P.op("dve", lambda e, scl=scl: e.tensor_scalar(tmp[:], ada[:, scl:scl + 8, :], 1.0, None, ALU.add), reads=[ada], writes=[tmp])
                P.op("dve", lambda e, dst=dst, gi=gi, l=l: e.tensor_tensor(md[:, l, dst, :, :], tmp[:], gv[:, l, gi, :].unsqueeze(2).to_broadcast([128, 8, 2]), ALU.mult),
                     reads=[tmp, gv], writes=[md])
            for (dst, src) in [(1, 0), (4, 24)]:
                P.op("dve", lambda e, dst=dst, src=src, l=l: e.tensor_copy(md[:, l, dst, :, :], ada[:, src:src + 8, :]), reads=[ada], writes=[md])
            for (dst, src, gi) in [(2, 16, 1), (5, 40, 3)]:
                P.op("dve", lambda e, dst=dst, src=src, gi=gi, l=l: e.tensor_tensor(md[:, l, dst, :, :], ada[:, src:src + 8, :], gv[:, l, gi, :].unsqueeze(2).to_broadcast([128, 8, 2]), ALU.mult),
                     reads=[ada, gv], writes=[md])
        P.barrier()
        P.stack = old


def load_weight_bf16(P, dst, dst_cols, src_ap_fn, ncols, stg, piece=512):
    i = 0
    c0 = 0
    while c0 < ncols:
        c1 = min(ncols, c0 + piece)
        s = stg[i % len(stg)]
        i += 1
        w = c1 - c0
        P.dma("sp", s[:, :, 0:w], src_ap_fn(c0, c1), writes=[s])
        P.op("pool", lambda e, s=s, c0=c0, c1=c1, w=w: e.tensor_copy(dst[:, :, dst_cols + c0:dst_cols + c1], s[:, :, 0:w]), reads=[s], writes=[dst])
        c0 = c1


def norm_mod(P, g, xt, hT, TW, sq, tmp, rstd, s_ap, sh_ap, hoff=0):
    P.op("act", lambda e: e.activation(sq[:, :, 0:TW], xt[:, :, 0:TW], AF.Square), reads=[xt], writes=[sq])
    ps = P.nextps()
    for k in range(8):
        P.op("pe", lambda e, k=k: e.matmul(ps[:, 0:TW], g.onesb[:], sq[:, k, 0:TW], start=(k == 0), stop=(k == 7)), reads=[g.onesb, sq], writes=[ps])
    P.op("act", lambda e: e.activation(rstd[:, 0:TW], ps[:, 0:TW], AF.Sqrt, bias=g.eps[:], scale=1.0 / D), reads=[ps, g.eps], writes=[rstd])
    P.op("dve", lambda e: e.reciprocal(rstd[:, 0:TW], rstd[:, 0:TW]), reads=[rstd], writes=[rstd])
    for k in range(8):
        sa = s_ap(k)
        sha = sh_ap(k)
        P.op("dve", lambda e, k=k: e.tensor_tensor(tmp[:, k, 0:TW], xt[:, k, 0:TW], rstd[:, 0:TW], ALU.mult), reads=[xt, rstd], writes=[tmp])
        P.op("act", lambda e, k=k, sa=sa, sha=sha: e.activation(hT[:, k, hoff:hoff + TW], tmp[:, k, 0:TW], AF.Identity, bias=sha, scale=sa),
             reads=[tmp, g.mod], writes=[hT])


def stage_inproj(nc, g, l, xsrc):
    P = g.P
    sm = g.sm
    with ExitStack() as st:
        old = P.stack
        P.stack = st
        win = P.sb([128, 8, 2064], BF16)
        stg = [P.sb([128, 8, 512]) for _ in range(2)]
        load_weight_bf16(P, win, 0, lambda c0, c1: g.w_in[l, :, c0:c1].rearrange("(k p) f -> p k f", p=128), 2064, stg)
        xts = [P.sb([128, 8, 512]) for _ in range(2)]
        sq = P.sb([128, 8, 512], BF16)
        tmp = P.sb([128, 8, 512])
        rstd = P.sb([128, 512])
        hT = P.sb([128, 8, 512], BF16)
        zsb = [P.sb([128, 512]) for _ in range(2)]
        dts = [P.sb([128, 32]) for _ in range(2)]
        xo = [P.sb([128, 8, 512]) for _ in range(2)]
        uo = [P.sb([128, 512]) for _ in range(2)]
        ev = 0
        for ti, (tok0, TW, v, s0, s1, rl) in enumerate(token_tiles()):
            xt = xts[ti % 2]
            P.dma("sp", xt[:, :, 0:TW], xsrc[:, tok0:tok0 + TW].rearrange("(k p) t -> p k t", p=128), writes=[xt])
            norm_mod(P, g, xt, hT, TW, sq, tmp, rstd,
                     lambda k: g.mod[:, l, 0, k, v:v + 1], lambda k: g.mod[:, l, 1, k, v:v + 1])
            for s in range(TW // 128):
                ps = P.nextps()
                for k in range(8):
                    P.op("pe", lambda e, ps=ps, k=k, s=s: e.matmul(ps[:, :], hT[:, k, s * 128:(s + 1) * 128], win[:, k, 0:512], start=(k == 0), stop=(k == 7)),
                         reads=[hT, win], writes=[ps])
                zb = zsb[s % 2]
                P.op("act", lambda e, ps=ps, zb=zb: e.activation(zb[:], ps[:, :], AF.Silu), reads=[ps], writes=[zb])
                P.dma("pool", g.zs[tok0 + s * 128:tok0 + (s + 1) * 128, :], zb[:], reads=[zb])
                ps2 = P.nextps()
                for k in range(8):
                    P.op("pe", lambda e, ps2=ps2, k=k, s=s: e.matmul(ps2[:, 0:16], hT[:, k, s * 128:(s + 1) * 128], win[:, k, 1536:1552], start=(k == 0), stop=(k == 7)),
                         reads=[hT, win], writes=[ps2])
                db = dts[s % 2]
                P.op("dve", lambda e, ps2=ps2, db=db: e.tensor_tensor(db[:, 0:16], ps2[:, 0:16], sm["dtb"][:, l, :], ALU.add), reads=[ps2, sm["dtb"]], writes=[db])
                P.op("act", lambda e, db=db: e.activation(db[:, 0:16], db[:, 0:16], AF.Exp), reads=[db], writes=[db])
                P.op("act", lambda e, db=db: e.activation(db[:, 0:16], db[:, 0:16], AF.Ln, bias=1.0), reads=[db], writes=[db])
                P.op("dve", lambda e, db=db: e.tensor_tensor(db[:, 16:32], db[:, 0:16], g.abc[:, l, :], ALU.mult), reads=[db, g.abc], writes=[db])
                P.dma("pool", g.dtk[tok0 + s * 128:tok0 + (s + 1) * 128, :], db[:], reads=[db])
                ps3 = P.nextps()
                for k in range(8):
                    P.op("pe", lambda e, ps3=ps3, k=k, s=s: e.matmul(ps3[:, :], hT[:, k, s * 128:(s + 1) * 128], win[:, k, 1552:2064], start=(k == 0), stop=(k == 7)),
                         reads=[hT, win], writes=[ps3])
                ub = uo[s % 2]
                P.op("dve", lambda e, ps3=ps3, ub=ub: e.tensor_copy(ub[:], ps3[:, :]), reads=[ps3], writes=[ub])
                P.dma("pool", g.uTok[tok0 + s * 128:tok0 + (s + 1) * 128, :], ub[:], reads=[ub])
            xob = xo[ti % 2]
            for j in range(8):
                c0 = 512 + j * 128
                ps = P.nextps()
                for k in range(8):
                    P.op("pe", lambda e, ps=ps, k=k, c0=c0, TW=TW: e.matmul(ps[:, 0:TW], win[:, k, c0:c0 + 128], hT[:, k, 0:TW], start=(k == 0), stop=(k == 7)),
                         reads=[hT, win], writes=[ps])
                copy_op(P, "act" if ev % 2 == 0 else "dve", xob[:, j, 0:TW], ps[:, 0:TW], [ps], [xob])
                ev += 1
            P.dma("pool", g.xbc[:, tok0:tok0 + TW].rearrange("(k p) t -> p k t", p=128), xob[:, :, 0:TW], reads=[xob])
        P.barrier()
        P.stack = old


def stage_conv(nc, g, l):
    P = g.P
    cp = g.sm["convp"]
    with ExitStack() as st:
        old = P.stack
        P.stack = st
        xin = [P.sb([128, 8, 514]) for _ in range(2)]
        acc = [P.sb([128, 8, 512]) for _ in range(2)]
        for ti, (tok0, TW, v, s0, s1, rl) in enumerate(token_tiles()):
            xi = xin[ti % 2]
            ac = acc[ti % 2]
            P.op("pool", lambda e, xi=xi: e.memset(xi[:, :, 0:1], 0.0), writes=[xi])
            P.op("pool", lambda e, xi=xi, TW=TW: e.memset(xi[:, :, TW + 1:TW + 2], 0.0), writes=[xi])
            a = max(s0, tok0 - 1)
            b = min(s1, tok0 + TW + 1)
            P.dma("sp", xi[:, :, a - (tok0 - 1):b - (tok0 - 1)], g.xbc[:, a:b].rearrange("(k p) t -> p k t", p=128), writes=[xi])
            for j in range(8):
                P.op("dve", lambda e, xi=xi, ac=ac, j=j, TW=TW: e.tensor_scalar(ac[:, j, 0:TW], xi[:, j, 0:TW], cp[:, l, j, 0:1], cp[:, l, j, 3:4], ALU.mult, ALU.add),
                     reads=[xi, cp], writes=[ac])
                P.op("dve", lambda e, xi=xi, ac=ac, j=j, TW=TW: e.scalar_tensor_tensor(ac[:, j, 0:TW], xi[:, j, 1:TW + 1], cp[:, l, j, 1:2], ac[:, j, 0:TW], ALU.mult, ALU.add),
                     reads=[xi, cp, ac], writes=[ac])
                P.op("dve", lambda e, xi=xi, ac=ac, j=j, TW=TW: e.scalar_tensor_tensor(ac[:, j, 0:TW], xi[:, j, 2:TW + 2], cp[:, l, j, 2:3], ac[:, j, 0:TW], ALU.mult, ALU.add),
                     reads=[xi, cp, ac], writes=[ac])
            P.op("act", lambda e, ac=ac, TW=TW: e.activation(ac[:, :, 0:TW], ac[:, :, 0:TW], AF.Silu), reads=[ac], writes=[ac])
            P.dma("pool", g.xbcA[:, tok0:tok0 + TW].rearrange("(k p) t -> p k t", p=128), ac[:, :, 0:TW], reads=[ac])
        P.barrier()
        P.stack = old


STAGES = {}


def chunk_list(d):
    ctx = [c * 128 for c in range(LC // 128)]
    lat = [LC + c * 128 for c in range(L // 128)]
    if d == 0:
        return ctx + lat
    return ctx[::-1] + lat[::-1]


def stage_ssd(nc, g, l, d):
    P = g.P
    sm = g.sm
    msk = (lambda: g.cst[:, 1, :]) if d == 0 else (lambda: g.cst[:, 2, :])
    idx = 127 if d == 0 else 0
    HT = g.HT[d]
    HTb = g.HTb[d]
    with ExitStack() as st:
        old = P.stack
        P.stack = st
        P.op("dve", lambda e: e.memset(HT[:], 0.0), writes=[HT])
        P.op("dve", lambda e: e.memset(HTb[:], 0.0), writes=[HTb])
        xin = [P.sb([128, 8, 128]) for _ in range(2)]
        dtk = [P.sb([128, 32]) for _ in range(2)]
        yfl = [P.sb([128, 512]) for _ in range(2)]
        zsl = [P.sb([128, 512]) for _ in range(2)]
        xtok = P.sb([128, 512])
        btok = P.sb([128, 256], BF16)
        R = P.sb([128, 1024])
        seg = P.sb([128, 1024])
        cbt = P.sb([128, 256])
        MT = P.sb([128, 1024], BF16)
        E = P.sb([128, 1024])
        CTs = P.sb([128, 1024], BF16)
        xdt = P.sb([128, 512], BF16)
        xdtw = P.sb([128, 512], BF16)
        te = P.sb([128, 8])
        cumtok = P.sb([128, 8])
        dx = P.sb([128, 512])
        yo = [P.sb([128, 512]) for _ in range(2)]
        htmp = P.sb([128, 512])
        ss = P.sb([128, 1])
        junk = P.sb([128, 512])
        yst = [P.sb([128, 4, 128], BF16) for _ in range(2)]
        for ci, tok in enumerate(chunk_list(d)):
            xi = xin[ci % 2]
            dk = dtk[ci % 2]
            P.dma("sp", xi[:], g.xbcA[:, tok:tok + 128].rearrange("(k p) t -> p k t", p=128), writes=[xi])
            P.dma("sp", dk[:], g.dtk[tok:tok + 128, :], writes=[dk])
            if d == 1:
                yl = yfl[ci % 2]
                zl = zsl[ci % 2]
                P.dma("sp", yl[:], g.yf[tok:tok + 128, :], writes=[yl])
                P.dma("sp", zl[:], g.zs[tok:tok + 128, :], writes=[zl])
            dtA = lambda dk=dk: dk[:, 16 + d * 8:24 + d * 8]
            dtv = lambda dk=dk: dk[:, d * 8:d * 8 + 8]
            psX = P.nextps()
            for j in range(4):
                P.op("pe", lambda e, j=j, xi=xi, psX=psX: e.transpose(psX[:, j * 128:(j + 1) * 128], xi[:, j, :], g.ident()), reads=[xi, g.cst], writes=[psX])
            P.op("act", lambda e, psX=psX: e.copy(xtok[:], psX[:, :]), reads=[psX], writes=[xtok])
            psB = P.nextps()
            for gg in range(2):
                P.op("pe", lambda e, gg=gg, xi=xi, psB=psB: e.transpose(psB[:, gg * 128:(gg + 1) * 128], xi[:, 4 + gg, :], g.ident()), reads=[xi, g.cst], writes=[psB])
            P.op("dve", lambda e, psB=psB: e.tensor_copy(btok[:], psB[:, 0:256]), reads=[psB], writes=[btok])
            P.op("dve", lambda e, dtA=dtA: e.tensor_tensor(R[:].rearrange("p (h i) -> p h i", h=8), msk().unsqueeze(1).to_broadcast([128, 8, 128]),
                                                          dtA().unsqueeze(2).to_broadcast([128, 8, 128]), ALU.mult), reads=[g.cst, dk], writes=[R])
            cum = [P.nextps(), P.nextps()]
            for hh in range(2):
                P.op("pe", lambda e, hh=hh, cum=cum: e.matmul(cum[hh][:, :], g.onesf(), R[:, hh * 512:(hh + 1) * 512], start=True, stop=True), reads=[g.cst, R], writes=[cum[hh]])
            psS = P.nextps()
            P.op("pe", lambda e, psS=psS, dtA=dtA: e.matmul(psS[:, 256:264], msk(), dtA(), start=True, stop=True), reads=[g.cst, dk], writes=[psS])
            for gg in range(2):
                P.op("pe", lambda e, psS=psS, gg=gg, xi=xi: e.matmul(psS[:, gg * 128:(gg + 1) * 128], xi[:, 4 + gg, :], xi[:, 6 + gg, :], start=True, stop=True), reads=[xi], writes=[psS])
            P.op("act", lambda e, psS=psS: e.copy(cumtok[:], psS[:, 256:264]), reads=[psS], writes=[cumtok])
            for h in range(8):
                P.op("dve", lambda e, h=h, cum=cum: e.tensor_scalar(seg[:, h * 128:(h + 1) * 128], cum[h // 4][:, (h % 4) * 128:(h % 4 + 1) * 128], cumtok[:, h:h + 1], 0.0, ALU.subtract, ALU.min),
                     reads=[cum[h // 4], cumtok], writes=[seg])
            P.op("act", lambda e: e.activation(seg[:], seg[:], AF.Exp), reads=[seg], writes=[seg])
            P.op("dve", lambda e, psS=psS: e.tensor_tensor(cbt[:].rearrange("p (g i) -> p g i", g=2), psS[:, 0:256].rearrange("p (g i) -> p g i", g=2),
                                                          msk().unsqueeze(1).to_broadcast([128, 2, 128]), ALU.mult), reads=[psS, g.cst], writes=[cbt])
            P.op("dve", lambda e: e.tensor_tensor(MT[:].rearrange("p (g e i) -> p g e i", g=2, e=4), seg[:].rearrange("p (g e i) -> p g e i", g=2, e=4),
                                                 cbt[:].rearrange("p (g i) -> p g i", g=2).unsqueeze(2).to_broadcast([128, 2, 4, 128]), ALU.mult), reads=[seg, cbt], writes=[MT])
            for hh in range(2):
                P.op("act", lambda e, hh=hh, cum=cum: e.activation(E[:, hh * 512:(hh + 1) * 512], cum[hh][:, :], AF.Exp), reads=[cum[hh]], writes=[E])
            P.op("dve", lambda e, xi=xi: e.tensor_tensor(CTs[:].rearrange("p (g e i) -> p g e i", g=2, e=4), E[:].rearrange("p (g e i) -> p g e i", g=2, e=4),
                                                        xi[:, 6:8, :].unsqueeze(2).to_broadcast([128, 2, 4, 128]), ALU.mult), reads=[E, xi], writes=[CTs])
            P.op("dve", lambda e, dtv=dtv: e.tensor_tensor(xdt[:].rearrange("p (h q) -> p h q", h=8), xtok[:].rearrange("p (h q) -> p h q", h=8),
                                                          dtv().unsqueeze(2).to_broadcast([128, 8, 64]), ALU.mult), reads=[xtok, dk], writes=[xdt])
            for hh in range(2):
                P.op("dve", lambda e, hh=hh, cum=cum: e.tensor_tensor(te[:, hh * 4:(hh + 1) * 4], cum[hh][:, :].rearrange("p (h i) -> p h i", h=4)[:, :, idx], cumtok[:, hh * 4:(hh + 1) * 4], ALU.subtract),
                     reads=[cum[hh], cumtok], writes=[te])
            P.op("act", lambda e: e.activation(te[:], te[:], AF.Exp), reads=[te], writes=[te])
            P.op("dve", lambda e: e.tensor_tensor(xdtw[:].rearrange("p (h q) -> p h q", h=8), xdt[:].rearrange("p (h q) -> p h q", h=8),
                                                 te[:].unsqueeze(2).to_broadcast([128, 8, 64]), ALU.mult), reads=[xdt, te], writes=[xdtw])
            psY = P.nextps()
            for h in range(8):
                P.op("pe", lambda e, h=h, psY=psY: e.matmul(psY[:, h * 64:(h + 1) * 64], MT[:, h * 128:(h + 1) * 128], xdt[:, h * 64:(h + 1) * 64], start=True, stop=False),
                     reads=[MT, xdt], writes=[psY])
                P.op("pe", lambda e, h=h, psY=psY: e.matmul(psY[:, h * 64:(h + 1) * 64], CTs[:, h * 128:(h + 1) * 128], HTb[:, h * 64:(h + 1) * 64], start=False, stop=True),
                     reads=[CTs, HTb], writes=[psY])
            psH = P.nextps()
            for gg in range(2):
                P.op("pe", lambda e, gg=gg, psH=psH: e.matmul(psH[:, gg * 256:(gg + 1) * 256], btok[:, gg * 128:(gg + 1) * 128], xdtw[:, gg * 256:(gg + 1) * 256], start=True, stop=True),
                     reads=[btok, xdtw], writes=[psH])
            P.op("dve", lambda e: e.tensor_tensor(htmp[:].rearrange("p (h q) -> p h q", h=8), HT[:].rearrange("p (h q) -> p h q", h=8),
                                                 E[:].rearrange("p (h i) -> p h i", h=8)[:, :, idx:idx + 1].to_broadcast([128, 8, 64]), ALU.mult), reads=[HT, E], writes=[htmp])
            P.op("dve", lambda e, psH=psH: e.tensor_tensor(HT[:], htmp[:], psH[:, :], ALU.add), reads=[htmp, psH], writes=[HT])
            P.op("act", lambda e: e.copy(HTb[:], HT[:]), reads=[HT], writes=[HTb])
            y = yo[ci % 2]
            if d == 0:
                P.op("dve", lambda e: e.tensor_tensor(dx[:], xtok[:], sm["dfull"][:, l, :], ALU.mult), reads=[xtok, sm["dfull"]], writes=[dx])
                P.op("dve", lambda e, y=y, psY=psY: e.tensor_tensor(y[:], psY[:, :], dx[:], ALU.add), reads=[psY, dx], writes=[y])
                P.dma("pool", g.yf[tok:tok + 128, :], y[:], reads=[y])
            else:
                P.op("dve", lambda e, y=y, psY=psY, yl=yl: e.tensor_tensor(y[:], psY[:, :], yl[:], ALU.add), reads=[psY, yl], writes=[y])
                P.op("dve", lambda e, y=y, zl=zl: e.tensor_tensor(y[:], y[:], zl[:], ALU.mult), reads=[y, zl], writes=[y])
                P.op("act", lambda e, y=y: e.activation(junk[:], y[:], AF.Square, accum_out=ss[:]), reads=[y], writes=[junk, ss])
                P.op("act", lambda e: e.activation(ss[:], ss[:], AF.Sqrt, bias=g.eps[:], scale=1.0 / 512), reads=[ss, g.eps], writes=[ss])
                P.op("dve", lambda e: e.reciprocal(ss[:], ss[:]), reads=[ss], writes=[ss])
                P.op("dve", lambda e, y=y: e.scalar_tensor_tensor(y[:], y[:], ss[:, 0:1], sm["gssd"][:, l, :], ALU.mult, ALU.mult), reads=[y, ss, sm["gssd"]], writes=[y])
                psT = P.nextps()
                for j in range(4):
                    P.op("pe", lambda e, j=j, y=y, psT=psT: e.transpose(psT[:, j * 128:(j + 1) * 128], y[:, j * 128:(j + 1) * 128], g.ident()), reads=[y, g.cst], writes=[psT])
                ys = yst[ci % 2]
                P.op("act", lambda e, ys=ys, psT=psT: e.copy(ys[:].rearrange("p j t -> p (j t)"), psT[:, :]), reads=[psT], writes=[ys])
                P.dma("pool", g.ysT[:, tok:tok + 128].rearrange("(k p) t -> p k t", p=128), ys[:], reads=[ys])
        P.barrier()
        P.stack = old


NB = LT // 8
NBC = LC // 8
NBL = L // 8
NS5 = 10
BT = [(0, 128), (128, 128), (256, 128), (384, 128), (512, 32)]
CH = [(0, 272), (272, 544)]


import os
S5STOP = int(os.environ.get('S5STOP', '0'))


def stage_s5(nc, g, l, last=False):
    P = g.P
    sm = g.sm
    with ExitStack() as st:
        old = P.stack
        P.stack = st
        Pasc = [P.sb([128, 32, 9]) for _ in range(2)]
        Pneg = [P.sb([128, 32, 8]) for _ in range(2)]
        Pdsc = [P.sb([128, 32, 9]) for _ in range(2)]
        Qasc = [P.sb([128, 32, 8]) for _ in range(2)]
        Qneg = [P.sb([128, 32, 8]) for _ in range(2)]
        Qdsc = [P.sb([128, 32, 8]) for _ in range(2)]
        HS = P.sb([128, NS5, 3, 32])
        with ExitStack() as st2:
            P.stack = st2
            tt = [P.sb([128, 32]) for _ in range(16)]
            step, xr, xi_, mag, cs, sn, t1, t2, t3, nr, den, cr, ci, ir, ii, t4 = tt
            a_re = lambda: sm["s5a"][:, l, 0, :]
            a_im = lambda: sm["s5a"][:, l, 1, :]

            def TTm(o, a, b, opx, rd, wr):
                P.op("dve", lambda e: e.tensor_tensor(o(), a(), b(), opx), reads=rd, writes=wr)

            P.op("act", lambda e: e.activation(step[:], sm["s5dt"][:, l, :], AF.Exp), reads=[sm["s5dt"]], writes=[step])
            TTm(lambda: xr[:], a_re, lambda: step[:], ALU.mult, [sm["s5a"], step], [xr])
            TTm(lambda: xi_[:], a_im, lambda: step[:], ALU.mult, [sm["s5a"], step], [xi_])
            P.op("act", lambda e: e.activation(mag[:], xr[:], AF.Exp), reads=[xr], writes=[mag])
            P.op("act", lambda e: e.activation(sn[:], xi_[:], AF.Sin, scale=1.0 / 16), reads=[xi_], writes=[sn])
            hp = P.sb([128, 1])
            P.op("dve", lambda e: e.memset(hp[:], math.pi / 2), writes=[hp])
            P.op("act", lambda e: e.activation(cs[:], xi_[:], AF.Sin, bias=hp[:], scale=1.0 / 16), reads=[xi_, hp], writes=[cs])
            for _ in range(4):
                TTm(lambda: t1[:], lambda: cs[:], lambda: cs[:], ALU.mult, [cs], [t1])
                TTm(lambda: t2[:], lambda: sn[:], lambda: sn[:], ALU.mult, [sn], [t2])
                P.op("dve", lambda e: e.scalar_tensor_tensor(t3[:], sn[:], 2.0, cs[:], ALU.mult, ALU.mult), reads=[sn, cs], writes=[t3])
                TTm(lambda: cs[:], lambda: t1[:], lambda: t2[:], ALU.subtract, [t1, t2], [cs])
                P.op("dve", lambda e: e.tensor_copy(sn[:], t3[:]), reads=[t3], writes=[sn])
            P.op("dve", lambda e: e.memset(Pasc[0][:, :, 0:1], 1.0), writes=[Pasc[0]])
            P.op("dve", lambda e: e.memset(Pasc[1][:, :, 0:1], 0.0), writes=[Pasc[1]])
            P.op("dve", lambda e: e.memset(Pneg[0][:, :, 0:1], 1.0), writes=[Pneg[0]])
            P.op("dve", lambda e: e.memset(Pneg[1][:, :, 0:1], 0.0), writes=[Pneg[1]])
            TTm(lambda: Pasc[0][:, :, 1], lambda: mag[:], lambda: cs[:], ALU.mult, [mag, cs], [Pasc[0]])
            TTm(lambda: Pasc[1][:, :, 1], lambda: mag[:], lambda: sn[:], ALU.mult, [mag, sn], [Pasc[1]])
            lbr = lambda: Pasc[0][:, :, 1]
            lbi = lambda: Pasc[1][:, :, 1]

            def cmul(o_r, o_i, ar_, ai_, br_, bi_, rd, wr):
                TTm(lambda: t1[:], ar_, br_, ALU.mult, rd, [t1])
                TTm(lambda: t2[:], ai_, bi_, ALU.mult, rd, [t2])
                TTm(lambda: t3[:], ar_, bi_, ALU.mult, rd, [t3])
                TTm(lambda: t4[:], ai_, br_, ALU.mult, rd, [t4])
                TTm(o_r, lambda: t1[:], lambda: t2[:], ALU.subtract, [t1, t2], wr)
                TTm(o_i, lambda: t3[:], lambda: t4[:], ALU.add, [t3, t4], wr)

            for k in range(1, 8):
                cmul(lambda k=k: Pasc[0][:, :, k + 1], lambda k=k: Pasc[1][:, :, k + 1], lambda k=k: Pasc[0][:, :, k], lambda k=k: Pasc[1][:, :, k], lbr, lbi, Pasc, Pasc)
            TTm(lambda: t1[:], lbr, lbr, ALU.mult, Pasc, [t1])
            TTm(lambda: t2[:], lbi, lbi, ALU.mult, Pasc, [t2])
            TTm(lambda: den[:], lambda: t1[:], lambda: t2[:], ALU.add, [t1, t2], [den])
            P.op("dve", lambda e: e.reciprocal(den[:], den[:]), reads=[den], writes=[den])
            TTm(lambda: ir[:], lbr, lambda: den[:], ALU.mult, Pasc + [den], [ir])
            P.op("dve", lambda e: e.scalar_tensor_tensor(ii[:], lbi(), -1.0, den[:], ALU.mult, ALU.mult), reads=Pasc + [den], writes=[ii])
            P.op("dve", lambda e: e.tensor_copy(Pneg[0][:, :, 1], ir[:]), reads=[ir], writes=[Pneg[0]])
            P.op("dve", lambda e: e.tensor_copy(Pneg[1][:, :, 1], ii[:]), reads=[ii], writes=[Pneg[1]])
            for k in range(1, 7):
                cmul(lambda k=k: Pneg[0][:, :, k + 1], lambda k=k: Pneg[1][:, :, k + 1], lambda k=k: Pneg[0][:, :, k], lambda k=k: Pneg[1][:, :, k], lambda: ir[:], lambda: ii[:], Pneg + [ir, ii], Pneg)
            for c in range(2):
                for i in range(9):
                    P.op("dve", lambda e, c=c, i=i: e.tensor_copy(Pdsc[c][:, :, i], Pasc[c][:, :, 8 - i]), reads=[Pasc[c]], writes=[Pdsc[c]])
            P.op("dve", lambda e: e.tensor_scalar(nr[:], lbr(), -1.0, None, ALU.add), reads=Pasc, writes=[nr])
            TTm(lambda: t1[:], a_re, a_re, ALU.mult, [sm["s5a"]], [t1])
            TTm(lambda: t2[:], a_im, a_im, ALU.mult, [sm["s5a"]], [t2])
            TTm(lambda: den[:], lambda: t1[:], lambda: t2[:], ALU.add, [t1, t2], [den])
            P.op("dve", lambda e: e.reciprocal(den[:], den[:]), reads=[den], writes=[den])
            TTm(lambda: t1[:], lambda: nr[:], a_re, ALU.mult, [nr, sm["s5a"]], [t1])
            TTm(lambda: t2[:], lbi, a_im, ALU.mult, Pasc + [sm["s5a"]], [t2])
            TTm(lambda: t1[:], lambda: t1[:], lambda: t2[:], ALU.add, [t1, t2], [t1])
            TTm(lambda: cr[:], lambda: t1[:], lambda: den[:], ALU.mult, [t1, den], [cr])
            TTm(lambda: t1[:], lbi, a_re, ALU.mult, Pasc + [sm["s5a"]], [t1])
            TTm(lambda: t2[:], lambda: nr[:], a_im, ALU.mult, [nr, sm["s5a"]], [t2])
            TTm(lambda: t1[:], lambda: t1[:], lambda: t2[:], ALU.subtract, [t1, t2], [t1])
            TTm(lambda: ci[:], lambda: t1[:], lambda: den[:], ALU.mult, [t1, den], [ci])
            big = [P.sb([128, 32, 8]) for _ in range(4)]
            crb = lambda: cr[:].unsqueeze(2).to_broadcast([128, 32, 8])
            cib = lambda: ci[:].unsqueeze(2).to_broadcast([128, 32, 8])
            for (Qt, Pt) in [(Qasc, Pasc), (Qneg, Pneg)]:
                pr_ = lambda Pt=Pt: Pt[0][:, :, 0:8]
                pi_ = lambda Pt=Pt: Pt[1][:, :, 0:8]
                TTm(lambda: big[0][:], crb, pr_, ALU.mult, [cr] + Pt, [big[0]])
                TTm(lambda: big[1][:], cib, pi_, ALU.mult, [ci] + Pt, [big[1]])
                TTm(lambda: big[2][:], crb, pi_, ALU.mult, [cr] + Pt, [big[2]])
                TTm(lambda: big[3][:], cib, pr_, ALU.mult, [ci] + Pt, [big[3]])
                TTm(lambda Qt=Qt: Qt[0][:], lambda: big[0][:], lambda: big[1][:], ALU.subtract, [big[0], big[1]], [Qt[0]])
                TTm(lambda Qt=Qt: Qt[1][:], lambda: big[2][:], lambda: big[3][:], ALU.add, [big[2], big[3]], [Qt[1]])
            for c in range(2):
                for i in range(8):
                    P.op("dve", lambda e, c=c, i=i: e.tensor_copy(Qdsc[c][:, :, i], Qasc[c][:, :, 7 - i]), reads=[Qasc[c]], writes=[Qdsc[c]])
            P.op("dve", lambda e: e.tensor_copy(HS[:, 0, 0, :], Pasc[0][:, :, 8]), reads=[Pasc[0]], writes=[HS])
            P.op("dve", lambda e: e.tensor_copy(HS[:, 0, 1, :], Pasc[1][:, :, 8]), reads=[Pasc[1]], writes=[HS])
            for s_ in range(1, NS5):
                TTm(lambda s_=s_: t1[:], lambda s_=s_: HS[:, s_ - 1, 0, :], lambda s_=s_: HS[:, s_ - 1, 0, :], ALU.mult, [HS], [t1])
                TTm(lambda s_=s_: t2[:], lambda s_=s_: HS[:, s_ - 1, 1, :], lambda s_=s_: HS[:, s_ - 1, 1, :], ALU.mult, [HS], [t2])
                P.op("dve", lambda e, s_=s_: e.scalar_tensor_tensor(HS[:, s_, 1, :], HS[:, s_ - 1, 0, :], 2.0, HS[:, s_ - 1, 1, :], ALU.mult, ALU.mult), reads=[HS], writes=[HS])
                TTm(lambda s_=s_: HS[:, s_, 0, :], lambda: t1[:], lambda: t2[:], ALU.subtract, [t1, t2, HS], [HS])
            P.op("dve", lambda e: e.tensor_scalar(HS[:, :, 2, :], HS[:, :, 1, :], -1.0, None, ALU.mult), reads=[HS], writes=[HS])
            P.barrier()
            P.stack = st
        if S5STOP == 1:
            P.barrier()
            P.stack = old
            return
        T8 = [[P.sb([128, 2, 128], BF16) for _ in range(2)] for _ in range(2)]
        Bz = [[P.sb([128, 2, 2, 128], BF16) for _ in range(2)] for _ in range(2)]
        OC = [[[P.sb([128, 2, 128], BF16) for _ in range(2)] for _ in range(2)] for _ in range(2)]
        for par in range(2):
            for d in range(2):
                for g2 in range(2):
                    P.op("pool", lambda e, par=par, d=d, g2=g2: e.memset(OC[par][d][g2][:], 0.0), writes=[OC[par][d][g2]])
        for par in range(2):
            for d in range(2):
                P.op("pool", lambda e, par=par, d=d: e.memset(Bz[par][d][:], 0.0), writes=[Bz[par][d]])
        tb = [P.sb([128, 128]) for _ in range(4)]
        LW = [[P.sb([128, 128], BF16) for _ in range(2)] for _ in range(2)]
        for g2 in range(2):
            for c_ in range(2):
                P.op("pool", lambda e, g2=g2, c_=c_: e.memset(LW[g2][c_][:], 0.0), writes=[LW[g2][c_]])
        RC = [P.sb([128, 128], BF16) for _ in range(2)]
        PB = [P.sb([128, 128]) for _ in range(2)]
        u8 = [P.sb([128, 2, NB], BF16) for _ in range(2)]
        ul = [P.sb([128, 128]) for _ in range(3)]
        ulz = P.sb([128, 128])
        P.op("pool", lambda e: e.memset(ulz[:], 0.0), writes=[ulz])
        X = [P.sb([128, 2, NB]) for _ in range(2)]
        X2 = P.sb([128, 2, NB])
        Xb = [P.sb([128, 2, NB], BF16) for _ in range(2)]
        Y8 = [P.sb([128, 640]) for _ in range(2)]
        for i_ in range(2):
            P.op("pool", lambda e, i_=i_: e.memset(Y8[i_][:], 0.0), writes=[Y8[i_]])
        yb = [P.sb([128, 5, 128]) for _ in range(2)]
        uli = 0
        for gp in range(16):
            par = gp % 2
            for g2 in range(2):
                gi = 2 * gp + g2
                psU = [P.nextps(), P.nextps()]
                for bi, (b0, nb) in enumerate(BT):
                    if nb == 128:
                        u_ = ul[uli % 3]
                        uli += 1
                    else:
                        u_ = ulz
                    P.dma("sp", u_[0:nb, :].rearrange("b (t c) -> b t c", t=8), g.uTok[b0 * 8:(b0 + nb) * 8, gi * 16:(gi + 1) * 16].rearrange("(b t) c -> b t c", t=8), writes=[u_])
                    tgt = psU[0][:, b0:b0 + 128] if bi < 4 else psU[1][:, 0:128]
                    P.op("pe", lambda e, u_=u_, tgt=tgt: e.transpose(tgt, u_[:, :], g.ident()), reads=[u_, g.cst], writes=[psU[0] if bi < 4 else psU[1]])
                P.op("act", lambda e, par=par, g2=g2, psU=psU: e.copy(u8[par][:, g2, 0:512], psU[0][:, :]), reads=[psU[0]], writes=[u8[par]])
                P.op("act", lambda e, par=par, g2=g2, psU=psU: e.copy(u8[par][:, g2, 512:NB], psU[1][:, 0:32]), reads=[psU[1]], writes=[u8[par]])
            if S5STOP == 2:
                P.barrier()
                P.stack = old
                return
            for d in range(2):
                col = d * 16 + gp
                bre = lambda gp=gp: sm["btab"][:, l, 0, gp, :].unsqueeze(1).to_broadcast([128, 8, 16])
                bim = lambda gp=gp: sm["btab"][:, l, 1, gp, :].unsqueeze(1).to_broadcast([128, 8, 16])
                cre = lambda gp=gp: sm["ctab"][:, l, 0, gp, :].unsqueeze(1).to_broadcast([128, 8, 16])
                cim = lambda gp=gp: sm["ctab"][:, l, 1, gp, :].unsqueeze(1).to_broadcast([128, 8, 16])
                if d == 0:
                    specs = [(Qneg, 0, bre, bim, LW, True, False), (Pasc, 0, cre, cim, RC, False, False),
                             (Qdsc, 0, bre, bim, PB, False, False), (Pasc, 1, cre, cim, None, True, True)]
                else:
                    specs = [(Qasc, 0, bre, bim, LW, True, False), (Pneg, 0, cre, cim, RC, False, False),
                             (Qasc, 0, bre, bim, PB, False, False), (Pdsc, 0, cre, cim, None, True, True)]
                v3 = lambda t: t[:].rearrange("p (a b) -> p a b", a=8)
                for (Tb, k0, yr, yi, dst, negim, isoc) in specs:
                    Tr = lambda Tb=Tb, k0=k0, col=col: Tb[0][:, col, k0:k0 + 8].unsqueeze(2).to_broadcast([128, 8, 16])
                    Ti = lambda Tb=Tb, k0=k0, col=col: Tb[1][:, col, k0:k0 + 8].unsqueeze(2).to_broadcast([128, 8, 16])
                    rdT = [Tb[0], Tb[1], sm["btab"], sm["ctab"]]
                    P.op("pool", lambda e, Tr=Tr, yr=yr: e.tensor_tensor(v3(tb[0]), Tr(), yr(), ALU.mult), reads=rdT, writes=[tb[0]])
                    P.op("pool", lambda e, Ti=Ti, yi=yi: e.tensor_tensor(v3(tb[1]), Ti(), yi(), ALU.mult), reads=rdT, writes=[tb[1]])
                    P.op("pool", lambda e, Ti=Ti, yr=yr: e.tensor_tensor(v3(tb[2]), Ti(), yr(), ALU.mult), reads=rdT, writes=[tb[2]])
                    P.op("pool", lambda e, Tr=Tr, yi=yi: e.tensor_tensor(v3(tb[3]), Tr(), yi(), ALU.mult), reads=rdT, writes=[tb[3]])
                    if negim:
                        P.op("pool", lambda e: e.tensor_scalar(tb[2][:], tb[2][:], -1.0, None, ALU.mult), reads=[tb[2]], writes=[tb[2]])
                    imop = ALU.subtract if negim else ALU.add
                    if isoc or dst is LW:
                        for g2 in range(2):
                            r0, r1 = g2 * 64, (g2 + 1) * 64
                            if isoc:
                                tre, tim = OC[par][d][g2], OC[par][d][g2]
                                ore = lambda tre=tre, r0=r0, r1=r1: tre[r0:r1, 0, :]
                                oim = lambda tim=tim, r0=r0, r1=r1: tim[r0:r1, 1, :]
                            else:
                                tre, tim = LW[g2][0], LW[g2][1]
                                ore = lambda tre=tre, r0=r0, r1=r1: tre[r0:r1, :]
                                oim = lambda tim=tim, r0=r0, r1=r1: tim[r0:r1, :]
                            P.op("pool", lambda e, ore=ore, r0=r0, r1=r1: e.tensor_tensor(ore(), tb[0][r0:r1, :], tb[1][r0:r1, :], ALU.subtract), reads=[tb[0], tb[1]], writes=[tre])
                            P.op("pool", lambda e, oim=oim, r0=r0, r1=r1, imop=imop: e.tensor_tensor(oim(), tb[2][r0:r1, :], tb[3][r0:r1, :], imop), reads=[tb[2], tb[3]], writes=[tim])
                    else:
                        P.op("pool", lambda e, dst=dst: e.tensor_tensor(dst[0][:], tb[0][:], tb[1][:], ALU.subtract), reads=[tb[0], tb[1]], writes=[dst[0]])
                        P.op("pool", lambda e, dst=dst, imop=imop: e.tensor_tensor(dst[1][:], tb[2][:], tb[3][:], imop), reads=[tb[2], tb[3]], writes=[dst[1]])
                if S5STOP == 3:
                    P.barrier()
                    P.stack = old
                    return
                psT8 = P.nextps()
                for g2 in range(2):
                    P.op("pe", lambda e, psT8=psT8, g2=g2: e.matmul(psT8[:, g2 * 128:(g2 + 1) * 128], LW[g2][0][:], RC[0][:], start=True, stop=False), reads=[LW[g2][0], RC[0]], writes=[psT8])
                    P.op("pe", lambda e, psT8=psT8, g2=g2: e.matmul(psT8[:, g2 * 128:(g2 + 1) * 128], LW[g2][1][:], RC[1][:], start=False, stop=True), reads=[LW[g2][1], RC[1]], writes=[psT8])
                mk = 4 + d
                P.op("dve", lambda e, psT8=psT8, par=par, d=d, mk=mk: e.tensor_tensor(T8[par][d][:], psT8[:, 0:256].rearrange("p (a b) -> p a b", a=2),
                                                                              g.cst[:, mk, :].unsqueeze(1).to_broadcast([128, 2, 128]), ALU.mult), reads=[psT8, g.cst], writes=[T8[par][d]])
                psP = P.nextps()
                for pl in range(2):
                    P.op("pe", lambda e, psP=psP, pl=pl: e.transpose(psP[:, pl * 128:(pl + 1) * 128], PB[pl][:], g.ident()), reads=[PB[pl], g.cst], writes=[psP])
                pv = lambda psP=psP: psP[:, 0:256].rearrange("p (a b) -> p a b", a=2)
                P.op("act", lambda e, pv=pv, par=par, d=d: e.copy(Bz[par][d][:, 0, :, 0:64], pv()[:, :, 0:64]), reads=[psP], writes=[Bz[par][d]])
                P.op("act", lambda e, pv=pv, par=par, d=d: e.copy(Bz[par][d][:, 1, :, 64:128], pv()[:, :, 64:128]), reads=[psP], writes=[Bz[par][d]])
                if S5STOP == 4:
                    P.barrier()
                    P.stack = old
                    return
                for pl in range(2):
                    for (c0, c1) in CH:
                        ps = P.nextps()
                        for g2 in range(2):
                            P.op("pe", lambda e, ps=ps, par=par, d=d, g2=g2, pl=pl, c0=c0, c1=c1: e.matmul(ps[:, 0:c1 - c0], Bz[par][d][:, g2, pl, :], u8[par][:, g2, c0:c1], start=(g2 == 0), stop=(g2 == 1)),
                                 reads=[Bz[par][d], u8[par]], writes=[ps])
                        if d == 0:
                            P.op("act", lambda e, ps=ps, pl=pl, c0=c0, c1=c1: e.copy(X[0][:, pl, c0:c1], ps[:, 0:c1 - c0]), reads=[ps], writes=[X[0]])
                        elif c0 == 0:
                            P.op("act", lambda e, ps=ps, pl=pl: e.copy(X[1][:, pl, NBL:NB], ps[:, 0:NBC]), reads=[ps], writes=[X[1]])
                            P.op("act", lambda e, ps=ps, pl=pl, c1=c1: e.copy(X[1][:, pl, 0:c1 - NBC], ps[:, NBC:c1]), reads=[ps], writes=[X[1]])
                        else:
                            P.op("act", lambda e, ps=ps, pl=pl, c0=c0, c1=c1: e.copy(X[1][:, pl, c0 - NBC:c1 - NBC], ps[:, 0:c1 - c0]), reads=[ps], writes=[X[1]])
                if S5STOP == 5:
                    P.barrier()
                    P.stack = old
                    return
                src, dst = X[d], X2
                for s_ in range(NS5):
                    sh = 1 << s_
                    n = NB - sh
                    ar = HS[:, s_, 0, col:col + 1]
                    ai = HS[:, s_, 1, col:col + 1]
                    nai = HS[:, s_, 2, col:col + 1]
                    if d == 0:
                        keep = (0, sh)
                        o0, i0 = sh, 0
                    else:
                        keep = (n, NB)
                        o0, i0 = 0, sh
                    P.op("act", lambda e, src=src, dst=dst, keep=keep: e.copy(dst[:, :, keep[0]:keep[1]], src[:, :, keep[0]:keep[1]]), reads=[src], writes=[dst])
                    P.op("dve", lambda e, src=src, dst=dst, ar=ar, o0=o0, i0=i0, n=n: e.scalar_tensor_tensor(dst[:, 0, o0:o0 + n], src[:, 0, i0:i0 + n], ar, src[:, 0, o0:o0 + n], ALU.mult, ALU.add), reads=[src, HS], writes=[dst])
                    P.op("dve", lambda e, src=src, dst=dst, nai=nai, o0=o0, i0=i0, n=n: e.scalar_tensor_tensor(dst[:, 0, o0:o0 + n], src[:, 1, i0:i0 + n], nai, dst[:, 0, o0:o0 + n], ALU.mult, ALU.add), reads=[src, HS, dst], writes=[dst])
                    P.op("dve", lambda e, src=src, dst=dst, ai=ai, o0=o0, i0=i0, n=n: e.scalar_tensor_tensor(dst[:, 1, o0:o0 + n], src[:, 0, i0:i0 + n], ai, src[:, 1, o0:o0 + n], ALU.mult, ALU.add), reads=[src, HS, dst], writes=[dst])
                    P.op("dve", lambda e, src=src, dst=dst, ar=ar, o0=o0, i0=i0, n=n: e.scalar_tensor_tensor(dst[:, 1, o0:o0 + n], src[:, 1, i0:i0 + n], ar, dst[:, 1, o0:o0 + n], ALU.mult, ALU.add), reads=[src, HS, dst], writes=[dst])
                    src, dst = dst, src
                assert src is X[d]
                P.op("act", lambda e, d=d: e.copy(Xb[d][:], X[d][:]), reads=[X[d]], writes=[Xb[d]])
            if S5STOP == 6:
                P.barrier()
                P.stack = old
                return
            for g2 in range(2):
                gi = 2 * gp + g2
                r0, r1 = g2 * 64, (g2 + 1) * 64
                y8 = Y8[g2]
                for (c0, c1) in CH:
                    ps = P.nextps()
                    w = c1 - c0
                    mms = []
                    for d in range(2):
                        mms.append((T8[par][d], lambda d=d, par=par, g2=g2: T8[par][d][:, g2, :], u8[par], lambda c0=c0, c1=c1, par=par, g2=g2: u8[par][:, g2, c0:c1], 0, w))
                    for pl in range(2):
                        oc0 = lambda pl=pl, par=par, g2=g2: OC[par][0][g2][:, pl, :]
                        oc1 = lambda pl=pl, par=par, g2=g2: OC[par][1][g2][:, pl, :]
                        if c0 == 0:
                            mms.append((OC[par][0][g2], oc0, Xb[0], lambda pl=pl, c1=c1: Xb[0][:, pl, 0:c1 - 1], 1, c1))
                            mms.append((OC[par][1][g2], oc1, Xb[1], lambda pl=pl: Xb[1][:, pl, NBL + 1:NB], 0, NBC - 1))
                            mms.append((OC[par][1][g2], oc1, Xb[1], lambda pl=pl, c1=c1: Xb[1][:, pl, 1:c1 - NBC + 1], NBC, c1))
                        else:
                            mms.append((OC[par][0][g2], oc0, Xb[0], lambda pl=pl, c0=c0, c1=c1: Xb[0][:, pl, c0 - 1:c1 - 1], 0, w))
                            mms.append((OC[par][1][g2], oc1, Xb[1], lambda pl=pl, c0=c0, c1=c1: Xb[1][:, pl, c0 - NBC + 1:c1 - NBC + 1], 0, w))
                    for mi, (lt_, lf, rt_, rf, o0, o1) in enumerate(mms):
                        P.op("pe", lambda e, ps=ps, lf=lf, rf=rf, o0=o0, o1=o1, mi=mi, nm=len(mms): e.matmul(ps[:, o0:o1], lf(), rf(), start=(mi == 0), stop=(mi == nm - 1)),
                             reads=[lt_, rt_], writes=[ps])
                    P.op("act", lambda e, ps=ps, y8=y8, c0=c0, c1=c1, w=w: e.copy(y8[:, c0:c1], ps[:, 0:w]), reads=[ps], writes=[y8])
                if S5STOP == 7:
                    P.barrier()
                    P.stack = old
                    return
                psY = [P.nextps(), P.nextps()]
                ybb = yb[g2]
                for bi, (b0, nb) in enumerate(BT):
                    tgt = psY[0][:, bi * 128:(bi + 1) * 128] if bi < 4 else psY[1][:, 0:128]
                    P.op("pe", lambda e, y8=y8, b0=b0, tgt=tgt: e.transpose(tgt, y8[:, b0:b0 + 128], g.ident()), reads=[y8, g.cst], writes=[psY[0] if bi < 4 else psY[1]])
                P.op("dve", lambda e, ybb=ybb, psY=psY: e.tensor_copy(ybb[:, 0:4, :].rearrange("p a b -> p (a b)"), psY[0][:, :]), reads=[psY[0]], writes=[ybb])
                P.op("dve", lambda e, ybb=ybb, psY=psY: e.tensor_copy(ybb[0:32, 4, :], psY[1][0:32, 0:128]), reads=[psY[1]], writes=[ybb])
                if S5STOP == 8:
                    P.barrier()
                    P.stack = old
                    return
                for bi, (b0, nb) in enumerate(BT):
                    P.dma("sp", g.y5pre[b0 * 8:(b0 + nb) * 8, gi * 16:(gi + 1) * 16].rearrange("(b t) c -> b t c", t=8), ybb[0:nb, bi, :].rearrange("b (t c) -> b t c", t=8), reads=[ybb])
        P.barrier()
        P.stack = old
    with ExitStack() as st:
        old = P.stack
        P.stack = st
        glw = P.sb([128, 4, 512], BF16)
        gst = [P.sb([128, 4, 512])]
        load_weight_bf16(P, glw, 0, lambda c0, c1: g.glu_w[l, :, c0:c1].rearrange("(k p) f -> p k f", p=128), 512, gst)
        yps = [P.sb([128, 512]) for _ in range(2)]
        uts = [P.sb([128, 512]) for _ in range(2)]
        tg = P.sb([128, 512])
        gq = P.sb([128, 512])
        gts = [P.sb([128, 4, 512], BF16) for _ in range(2)]
        yos = [P.sb([128, 4, 512], BF16) for _ in range(2)]
        sg = P.sb([128, 512])
        si = 0
        for ti, (tok0, TW, v, s0, s1, rl) in enumerate(token_tiles(with_ctx=not last)):
            gt = gts[ti % 2]
            yo_ = yos[ti % 2]
            for s in range(TW // 128):
                yp = yps[si % 2]
                ut = uts[si % 2]
                si += 1
                r0 = tok0 + s * 128
                P.dma("sp", yp[:], g.y5pre[r0:r0 + 128, :], writes=[yp])
                P.dma("sp", ut[:], g.uTok[r0:r0 + 128, :], writes=[ut])
                P.op("dve", lambda e, ut=ut: e.tensor_tensor(ut[:], ut[:], sm["d5full"][:, l, :], ALU.mult), reads=[ut, sm["d5full"]], writes=[ut])
                P.op("dve", lambda e, ut=ut, yp=yp: e.tensor_tensor(yp[:], yp[:], ut[:], ALU.add), reads=[ut, yp], writes=[yp])
                P.op("dve", lambda e, yp=yp: e.tensor_tensor(tg[:], yp[:], yp[:], ALU.mult), reads=[yp], writes=[tg])
                P.op("dve", lambda e: e.tensor_scalar(tg[:], tg[:], 0.044715, 1.0, ALU.mult, ALU.add), reads=[tg], writes=[tg])
                P.op("dve", lambda e, yp=yp: e.tensor_tensor(tg[:], tg[:], yp[:], ALU.mult), reads=[tg, yp], writes=[tg])
                P.op("act", lambda e: e.activation(tg[:], tg[:], AF.Sigmoid, scale=1.5957691216057308), reads=[tg], writes=[tg])
                P.op("dve", lambda e, yp=yp: e.tensor_tensor(gq[:], tg[:], yp[:], ALU.mult), reads=[tg, yp], writes=[gq])
                psG = P.nextps()
                for j in range(4):
                    P.op("pe", lambda e, j=j, psG=psG: e.transpose(psG[:, j * 128:(j + 1) * 128], gq[:, j * 128:(j + 1) * 128], g.ident()), reads=[gq, g.cst], writes=[psG])
                P.op("act", lambda e, gt=gt, psG=psG, s=s: e.copy(gt[:, :, s * 128:(s + 1) * 128], psG[:, :].rearrange("p (j t) -> p j t", j=4)), reads=[psG], writes=[gt])
            for fo in range(4):
                ps = P.nextps()
                for k in range(4):
                    P.op("pe", lambda e, ps=ps, k=k, fo=fo, gt=gt, TW=TW: e.matmul(ps[:, 0:TW], glw[:, k, fo * 128:(fo + 1) * 128], gt[:, k, 0:TW], start=(k == 0), stop=(k == 3)),
                         reads=[glw, gt], writes=[ps])
                P.op("act", lambda e, ps=ps, fo=fo, TW=TW: e.activation(sg[:, 0:TW], ps[:, 0:TW], AF.Sigmoid, bias=sm["glub"][:, l, fo:fo + 1]), reads=[ps, sm["glub"]], writes=[sg])
                P.op("dve", lambda e, fo=fo, gt=gt, yo_=yo_, TW=TW: e.tensor_tensor(yo_[:, fo, 0:TW], sg[:, 0:TW], gt[:, fo, 0:TW], ALU.mult), reads=[sg, gt], writes=[yo_])
            P.dma("pool", g.y5T[:, tok0:tok0 + TW].rearrange("(k p) t -> p k t", p=128), yo_[:, :, 0:TW], reads=[yo_])
        P.barrier()
        P.stack = old


def post_norm_res(P, g, xm, xt, xn, TW, sq, tmp, rstd, gate_ap):
    P.op("act", lambda e: e.activation(sq[:, :, 0:TW], xm[:, :, 0:TW], AF.Square), reads=[xm], writes=[sq])
    ps = P.nextps()
    for k in range(8):
        P.op("pe", lambda e, k=k: e.matmul(ps[:, 0:TW], g.onesb[:], sq[:, k, 0:TW], start=(k == 0), stop=(k == 7)), reads=[g.onesb, sq], writes=[ps])
    P.op("act", lambda e: e.activation(rstd[:, 0:TW], ps[:, 0:TW], AF.Sqrt, bias=g.eps[:], scale=1.0 / D), reads=[ps, g.eps], writes=[rstd])
    P.op("dve", lambda e: e.reciprocal(rstd[:, 0:TW], rstd[:, 0:TW]), reads=[rstd], writes=[rstd])
    for k in range(8):
        ga = gate_ap(k)
        P.op("dve", lambda e, k=k: e.tensor_tensor(tmp[:, k, 0:TW], xm[:, k, 0:TW], rstd[:, 0:TW], ALU.mult), reads=[xm, rstd], writes=[tmp])
        P.op("dve", lambda e, k=k, ga=ga: e.scalar_tensor_tensor(xn[:, k, 0:TW], tmp[:, k, 0:TW], ga, xt[:, k, 0:TW], ALU.mult, ALU.add), reads=[tmp, xt, g.mod], writes=[xn])


def stage_outproj(nc, g, l, xsrc, xdst, last):
    P = g.P
    with ExitStack() as st:
        old = P.stack
        P.stack = st
        wo = P.sb([128, 8, 1024], BF16)
        stg = [P.sb([128, 8, 512]) for _ in range(2)]
        load_weight_bf16(P, wo, 0, lambda c0, c1: g.w_out[l, :, c0:c1].rearrange("(k p) f -> p k f", p=128), 1024, stg)
        xts = [P.sb([128, 8, 512]) for _ in range(2)]
        yin = [P.sb([128, 8, 512], BF16) for _ in range(2)]
        xm = P.sb([128, 8, 512])
        xn = [P.sb([128, 8, 512]) for _ in range(2)]
        sq = P.sb([128, 8, 512], BF16)
        tmp = P.sb([128, 8, 512])
        rstd = P.sb([128, 512])
        ev = 0
        for ti, (tok0, TW, v, s0, s1, rl) in enumerate(token_tiles(with_ctx=not last)):
            xt = xts[ti % 2]
            yi = yin[ti % 2]
            xo = xn[ti % 2]
            P.dma("sp", xt[:, :, 0:TW], xsrc[:, tok0:tok0 + TW].rearrange("(k p) t -> p k t", p=128), writes=[xt])
            P.dma("sp", yi[:, 0:4, 0:TW], g.ysT[:, tok0:tok0 + TW].rearrange("(k p) t -> p k t", p=128), writes=[yi])
            P.dma("sp", yi[:, 4:8, 0:TW], g.y5T[:, tok0:tok0 + TW].rearrange("(k p) t -> p k t", p=128), writes=[yi])
            for j in range(8):
                ps = P.nextps()
                for k in range(8):
                    P.op("pe", lambda e, ps=ps, k=k, j=j, yi=yi, TW=TW: e.matmul(ps[:, 0:TW], wo[:, k, j * 128:(j + 1) * 128], yi[:, k, 0:TW], start=(k == 0), stop=(k == 7)),
                         reads=[wo, yi], writes=[ps])
                copy_op(P, "act" if ev % 2 == 0 else "dve", xm[:, j, 0:TW], ps[:, 0:TW], [ps], [xm])
                ev += 1
            post_norm_res(P, g, xm, xt, xo, TW, sq, tmp, rstd, lambda k, v=v: g.mod[:, l, 2, k, v:v + 1])
            P.dma("pool", xdst[:, tok0:tok0 + TW].rearrange("(k p) t -> p k t", p=128), xo[:, :, 0:TW], reads=[xo])
        P.barrier()
        P.stack = old


def stage_ffn(nc, g, l, xsrc, xdst, last):
    P = g.P
    cw = g.sm["cwf"]
    tl = token_tiles(with_ctx=not last)
    supers = []
    if not last:
        supers.append([tl[0]])
        tl = tl[1:]
    for i in range(0, len(tl), 2):
        supers.append(tl[i:i + 2])
    for sup in supers:
        with ExitStack() as st:
            old = P.stack
            P.stack = st
            act = P.sb([128, NJ, 1024], BF16)
            with ExitStack() as st2:
                P.stack = st2
                hT = P.sb([128, 8, 1024], BF16)
                xts = [P.sb([128, 8, 512]) for _ in range(2)]
                sq = P.sb([128, 8, 512], BF16)
                tmp = P.sb([128, 8, 512])
                rstd = P.sb([128, 512])
                for si, (tok0, TW, v, s0, s1, rl) in enumerate(sup):
                    xt = xts[si % 2]
                    P.dma("sp", xt[:, :, 0:TW], xsrc[:, tok0:tok0 + TW].rearrange("(k p) t -> p k t", p=128), writes=[xt])
                    norm_mod(P, g, xt, hT, TW, sq, tmp, rstd,
                             lambda k, v=v: g.mod[:, l, 3, k, v:v + 1], lambda k, v=v: g.mod[:, l, 4, k, v:v + 1], hoff=si * 512)
                wst = [P.sb([128, 8, 256]) for _ in range(2)]
                wjb = [P.sb([128, 8, 256], BF16) for _ in range(2)]
                accg = P.sb([128, 512])
                accv = P.sb([128, 512])
                for j in range(NJ):
                    ws = wst[j % 2]
                    wj = wjb[j % 2]
                    P.dma("sp", ws[:, :, 0:128], g.w_up[l, :, j * 128:(j + 1) * 128].rearrange("(k p) f -> p k f", p=128), writes=[ws])
                    P.dma("sp", ws[:, :, 128:256], g.w_up[l, :, DFF + j * 128:DFF + (j + 1) * 128].rearrange("(k p) f -> p k f", p=128), writes=[ws])
                    P.op("pool", lambda e, ws=ws, wj=wj: e.tensor_copy(wj[:], ws[:]), reads=[ws], writes=[wj])
                    for si, (tok0, TW, v, s0, s1, rl) in enumerate(sup):
                        nr_ = TW // rl
                        pss = []
                        for half in range(2):
                            ps = P.nextps()
                            pss.append(ps)
                            for k in range(8):
                                P.op("pe", lambda e, ps=ps, k=k, wj=wj, half=half, si=si, TW=TW: e.matmul(ps[:, 0:TW], wj[:, k, half * 128:(half + 1) * 128], hT[:, k, si * 512:si * 512 + TW], start=(k == 0), stop=(k == 7)),
                                     reads=[wj, hT], writes=[ps])
                        for half, (ps, ac) in enumerate(zip(pss, [accg, accv])):
                            ch = j + half * NJ
                            v3 = lambda ap, a, b, TW=TW, rl=rl: ap[:, 0:TW].rearrange("p (r w) -> p r w", w=rl)[:, :, a:b]
                            P.op("dve", lambda e, ps=ps, ac=ac, ch=ch, TW=TW: e.tensor_scalar(ac[:, 0:TW], ps[:, 0:TW], cw[:, l, ch, 1:2], cw[:, l, ch, 3:4], ALU.mult, ALU.add), reads=[ps, cw], writes=[ac])
                            P.op("dve", lambda e, ps=ps, ac=ac, ch=ch, v3=v3, rl=rl: e.scalar_tensor_tensor(v3(ac, 1, rl), v3(ps, 0, rl - 1), cw[:, l, ch, 0:1], v3(ac, 1, rl), ALU.mult, ALU.add), reads=[ps, cw, ac], writes=[ac])
                            P.op("dve", lambda e, ps=ps, ac=ac, ch=ch, v3=v3, rl=rl: e.scalar_tensor_tensor(v3(ac, 0, rl - 1), v3(ps, 1, rl), cw[:, l, ch, 2:3], v3(ac, 0, rl - 1), ALU.mult, ALU.add), reads=[ps, cw, ac], writes=[ac])
                        P.op("act", lambda e, TW=TW: e.activation(accg[:, 0:TW], accg[:, 0:TW], AF.Silu), reads=[accg], writes=[accg])
                        P.op("dve", lambda e, j=j, si=si, TW=TW: e.tensor_tensor(act[:, j, si * 512:si * 512 + TW], accg[:, 0:TW], accv[:, 0:TW], ALU.mult), reads=[accg, accv], writes=[act])
                P.barrier()
                P.stack = st
            wds = [P.sb([128, NJ, 128]) for _ in range(2)]
            wdb = [P.sb([128, NJ, 128], BF16) for _ in range(2)]
            xm = P.sb([128, 8, 1024])
            xts = [P.sb([128, 8, 512]) for _ in range(1)]
            xn = [P.sb([128, 8, 512]) for _ in range(1)]
            sq = P.sb([128, 8, 512], BF16)
            tmp = P.sb([128, 8, 512])
            rstd = P.sb([128, 512])
            ev = 0
            for jo in range(8):
                ws = wds[jo % 2]
                wd = wdb[jo % 2]
                P.dma("sp", ws[:], g.w_down[l, :, jo * 128:(jo + 1) * 128].rearrange("(c p) f -> p c f", p=128), writes=[ws])
                P.op("pool", lambda e, ws=ws, wd=wd: e.tensor_copy(wd[:], ws[:]), reads=[ws], writes=[wd])
                for si, (tok0, TW, v, s0, s1, rl) in enumerate(sup):
                    ps = P.nextps()
                    for c in range(NJ):
                        P.op("pe", lambda e, ps=ps, c=c, wd=wd, si=si, TW=TW: e.matmul(ps[:, 0:TW], wd[:, c, :], act[:, c, si * 512:si * 512 + TW], start=(c == 0), stop=(c == NJ - 1)),
                             reads=[wd, act], writes=[ps])
                    copy_op(P, "act" if ev % 2 == 0 else "dve", xm[:, jo, si * 512:si * 512 + TW], ps[:, 0:TW], [ps], [xm])
                    ev += 1
            for si, (tok0, TW, v, s0, s1, rl) in enumerate(sup):
                xt = xts[0]
                xo = xn[0]
                P.dma("sp", xt[:, :, 0:TW], xsrc[:, tok0:tok0 + TW].rearrange("(k p) t -> p k t", p=128), writes=[xt])
                xmv = T.__new__(T)
                xmv.t = xm.t[:, :, si * 512:si * 512 + 512]
                xmv.d = xm.d
                post_norm_res(P, g, xmv, xt, xo, TW, sq, tmp, rstd, lambda k, v=v: g.mod[:, l, 5, k, v:v + 1])
                if last:
                    P.dma("pool", xdst[:, tok0 - LC:tok0 - LC + TW].rearrange("(k p) t -> p k t", p=128), xo[:, :, 0:TW], reads=[xo])
                else:
                    P.dma("pool", xdst[:, tok0:tok0 + TW].rearrange("(k p) t -> p k t", p=128), xo[:, :, 0:TW], reads=[xo])
            P.barrier()
            P.stack = old


def _colvec(v):
    return np.ascontiguousarray(np.asarray(v, np.float32).reshape(-1, 128).T)


def shared_layout(inp):
    f = lambda a: np.ascontiguousarray(np.asarray(a, np.float32))
    m = {}
    consts = np.zeros((128, 6, 128), np.float32)
    consts[:, 0] = np.eye(128)
    consts[:, 1] = np.triu(np.ones((128, 128)))
    consts[:, 2] = np.tril(np.ones((128, 128)))
    consts[:, 3] = 1.0
    blk = np.arange(128) // 16
    consts[:, 4] = (blk[:, None] <= blk[None, :])
    consts[:, 5] = (blk[:, None] >= blk[None, :])
    m["consts"] = consts
    m["w_ada"] = f(inp["w_ada"])
    m["bada"] = np.ascontiguousarray(np.stack([_colvec(inp["b_ada"][l]) for l in range(DEPTH)], 1))
    gv = np.zeros((128, DEPTH, 4, 8), np.float32)
    for l in range(DEPTH):
        for i, nm in enumerate(["g_pre_mix", "g_post_mix", "g_pre_ffn", "g_post_ffn"]):
            gv[:, l, i, :] = _colvec(inp[nm][l])
    m["gvec"] = gv
    m["w_in"] = f(inp["w_in"])
    cp = np.zeros((128, DEPTH, 8, 4), np.float32)
    for l in range(DEPTH):
        for k in range(3):
            cp[:, l, :, k] = _colvec(inp["ssd_conv_w"][l, k])
        cp[:, l, :, 3] = _colvec(inp["ssd_conv_b"][l])
    m["convp"] = cp
    m["dtb"] = np.ascontiguousarray(np.broadcast_to(f(inp["ssd_dt_bias"]).reshape(1, DEPTH, 16), (128, DEPTH, 16)))
    m["alog"] = np.ascontiguousarray(np.broadcast_to(f(inp["ssd_a_log"]).reshape(1, DEPTH, 16), (128, DEPTH, 16)))
    m["dfull"] = np.ascontiguousarray(np.broadcast_to(np.repeat(f(inp["ssd_d"]), 64, axis=1).reshape(1, DEPTH, 512), (128, DEPTH, 512)))
    m["gssd"] = np.ascontiguousarray(np.broadcast_to(f(inp["ssd_norm_g"]).reshape(1, DEPTH, 512), (128, DEPTH, 512)))
    s5a = np.zeros((128, DEPTH, 2, 32), np.float32)
    s5dt = np.zeros((128, DEPTH, 32), np.float32)
    for l in range(DEPTH):
        for c, nm in enumerate(["s5_a_re", "s5_a_im"]):
            a = f(inp[nm][l]).reshape(2, 16, 2, 64)
            s5a[:, l, c, :] = a.transpose(2, 3, 0, 1).reshape(128, 32)
        ld = f(inp["s5_log_dt"][l]).reshape(2, 16, 2)
        s5dt[:, l, :] = np.repeat(ld.transpose(2, 0, 1).reshape(2, 1, 32), 64, axis=1).reshape(128, 32)
    m["s5a"] = s5a
    m["s5dt"] = s5dt
    btab = np.zeros((128, DEPTH, 2, 16, 16), np.float32)
    ctab = np.zeros((128, DEPTH, 2, 16, 16), np.float32)
    for l in range(DEPTH):
        for pl, (bn, cn) in enumerate([("s5_b_re", "s5_c_re"), ("s5_b_im", "s5_c_im")]):
            bb = f(inp[bn][l]).reshape(16, 2, 64, 16)
            cc = f(inp[cn][l]).reshape(16, 2, 16, 64)
            btab[:, l, pl] = bb.transpose(1, 2, 0, 3).reshape(128, 16, 16)
            ctab[:, l, pl] = cc.transpose(1, 3, 0, 2).reshape(128, 16, 16)
    m["btab"] = btab
    m["ctab"] = ctab
    m["d5full"] = np.ascontiguousarray(np.broadcast_to(f(inp["s5_d"]).reshape(1, DEPTH, 512), (128, DEPTH, 512)))
    m["glu_w"] = f(inp["s5_glu_w"])
    m["glub"] = np.ascontiguousarray(np.stack([_colvec(inp["s5_glu_b"][l]) for l in range(DEPTH)], 1))
    m["w_out"] = f(inp["w_out"])
    m["w_up"] = f(inp["ffn_w_up"])
    cwf = np.zeros((128, DEPTH, 2 * NJ, 4), np.float32)
    for l in range(DEPTH):
        for k in range(3):
            cwf[:, l, :, k] = _colvec(inp["ffn_conv_w"][l, k])
        cwf[:, l, :, 3] = _colvec(inp["ffn_conv_b"][l])
    m["cwf"] = cwf
    m["w_down"] = f(inp["ffn_w_down"])
    return m


def core_layout(inp, b):
    x = np.asarray(inp["x"][b], np.float32)
    ctx = np.asarray(inp["ctx"][b], np.float32)
    xc0 = np.ascontiguousarray(np.concatenate([ctx.T, x.T], axis=1))
    cv = np.zeros((128, 8, 2), np.float32)
    cv[:, :, 0] = _colvec(inp["c"][b])
    cv[:, :, 1] = _colvec(inp["c_ctx"])
    return {"xc0": xc0, "cvec": cv}


_CACHE = {}


def kernel(**inputs):
    if "nc" not in _CACHE:
        _CACHE["nc"] = build_program(None)[0]
    nc = _CACHE["nc"]
    sh = shared_layout(inputs)
    in_maps = []
    for core in range(8):
        m = dict(sh)
        m.update(core_layout(inputs, core % 4))
        in_maps.append(m)
    res = run_bass_kernel_spmd(nc, in_maps, core_ids=list(range(8)))
    out = np.stack([np.ascontiguousarray(res.results[b]["out"].T) for b in range(4)], 0)
    return out.astype(np.float32)
```

```python
import math
import numpy as np
from contextlib import ExitStack
import concourse.bass as bass
import concourse.mybir as mybir
from concourse.bass_utils import run_bass_kernel_spmd

F32 = mybir.dt.float32
BF16 = mybir.dt.bfloat16
AF = mybir.ActivationFunctionType
ALU = mybir.AluOpType

NSLOT = 24
EPOCH = 20000

D = 1024
L = 4096
LC = 256
LT = L + LC
DEPTH = 2
DFF = 2816
NJ = DFF // 128
EPS = 1e-6
NSTEP = 13


class Dep:
    __slots__ = ("w", "r")

    def __init__(self):
        self.w = None
        self.r = {}


class T:
    def __init__(self, t):
        self.t = t
        self.d = Dep()

    def __getitem__(self, k):
        return self.t[k]


class Prog:
    ENGS = ("pe", "act", "dve", "pool", "sp")

    def __init__(self, nc, stack):
        self.nc = nc
        self.stack = stack
        self.eng = {"pe": nc.tensor, "act": nc.scalar, "dve": nc.vector, "pool": nc.gpsimd, "sp": nc.sync}
        self.count = {e: 0 for e in self.ENGS}
        self.glob = []
        self.seen = {e: {} for e in self.ENGS}
        self.slot_cnt = [0] * NSLOT
        self.next_slot = 0
        self.uid = 0
        self.psl = []
        self.psi = 0

    def sb(self, shape, dtype=F32):
        self.uid += 1
        t = self.stack.enter_context(self.nc.sbuf_tensor(f"sb{self.uid}", list(shape), dtype))
        return T(t)

    def nextps(self):
        p = self.psl[self.psi % len(self.psl)]
        self.psi += 1
        return p

    def _collect(self, eng, reads, writes):
        waits = {}

        def add(k):
            if k is None:
                return
            key, val = k
            if eng == "pe" and key == "pe":
                return
            if waits.get(key, 0) < val:
                waits[key] = val

        for d in reads:
            add(d.w)
        for d in writes:
            add(d.w)
            for key, val in d.r.items():
                add((key, val))
        out = {}
        seen = self.seen[eng]
        for key, val in waits.items():
            if seen.get(key, 0) < val:
                seen[key] = val
                out[key] = val
        return out

    def _mark(self, reads, writes, mykey):
        key, val = mykey
        for d in reads:
            if d.r.get(key, 0) < val:
                d.r[key] = val
        for d in writes:
            d.w = mykey
            d.r = {}

    def op(self, eng, fn, reads=(), writes=()):
        reads = [x.d for x in reads]
        writes = [x.d for x in writes]
        waits = self._collect(eng, reads, writes)
        self.count[eng] += 1
        mykey = (eng, self.count[eng])
        self.glob.append((eng, fn, waits, "c", mykey))
        self._mark(reads, writes, mykey)

    def dma(self, q, out_ap, in_ap, reads=(), writes=()):
        reads = [x.d for x in reads]
        writes = [x.d for x in writes]
        waits = self._collect(q, reads, writes)
        s = self.next_slot
        self.next_slot = (s + 1) % NSLOT
        key = ("dma", s)
        prev = 16 * self.slot_cnt[s]
        if prev > 0 and self.seen[q].get(key, 0) < prev:
            self.seen[q][key] = prev
            waits[key] = prev
        self.slot_cnt[s] += 1
        mykey = (key, 16 * self.slot_cnt[s])

        def fn(e, out_ap=out_ap, in_ap=in_ap):
            return e.dma_start(out=out_ap, in_=in_ap)

        self.glob.append((q, fn, waits, "d", mykey))
        self._mark(reads, writes, mykey)

    def barrier(self):
        for e in self.ENGS:
            waits = {}
            for e2 in self.ENGS:
                if e2 != e and self.count[e2] > 0 and self.seen[e].get(e2, 0) < self.count[e2]:
                    self.seen[e][e2] = self.count[e2]
                    waits[e2] = self.count[e2]
            for s in range(NSLOT):
                v = 16 * self.slot_cnt[s]
                key = ("dma", s)
                if v > 0 and self.seen[e].get(key, 0) < v:
                    self.seen[e][key] = v
                    waits[key] = v
            if waits:
                self.glob.append((e, None, waits, "w", None))

    def emit(self):
        nc = self.nc
        waited = {e: set() for e in self.ENGS}
        for eng, fn, waits, kind, mykey in self.glob:
            for key, val in waits.items():
                if key in waited:
                    waited[key].add(val)
        rank = {}
        sems = {}
        for e in self.ENGS:
            vals = sorted(waited[e])
            rank[e] = {v: i for i, v in enumerate(vals)}
            nep = (len(vals) + EPOCH - 1) // EPOCH
            sems[e] = [self.stack.enter_context(nc.semaphore(f"sem_{e}_{k}")) for k in range(max(nep, 1))]
        dsem = [self.stack.enter_context(nc.semaphore(f"sem_dma_{s}")) for s in range(NSLOT)]

        def semval(key, val):
            if isinstance(key, tuple):
                return dsem[key[1]], val
            r = rank[key][val]
            return sems[key][r // EPOCH], (r % EPOCH) + 1

        n = 0
        for eng, fn, waits, kind, mykey in self.glob:
            e = self.eng[eng]
            for key, val in waits.items():
                s, v = semval(key, val)
                e.wait_ge(s, v)
            if fn is None:
                continue
            inst = fn(e)
            n += 1
            if kind == "d":
                inst.then_inc(dsem[mykey[0][1]], 16)
            elif mykey[1] in rank[eng]:
                s, v = semval(eng, mykey[1])
                inst.then_inc(s, 1)
        e = self.eng["sp"]
        for s in range(NSLOT):
            if self.slot_cnt[s] > 0:
                e.wait_ge(dsem[s], 16 * self.slot_cnt[s])
        return n


def token_tiles(with_ctx=True):
    tl = []
    if with_ctx:
        tl.append((0, LC, 1, 0, LC, LC))
    for k in range(L // 512):
        tl.append((LC + 512 * k, 512, 0, LC, LT, 64))
    return tl


class K:
    pass


def build_program(debug_stop=None):
    nc = bass.Bass("TRN2", target_bir_lowering=False)
    g = K()

    def din(name, shape, dt=F32):
        return nc.dram_tensor(name, list(shape), dt, kind="ExternalInput").ap()

    def dscr(name, shape, dt=F32):
        kind = "ExternalOutput" if debug_stop is not None else "Internal"
        return nc.dram_tensor(name, list(shape), dt, kind=kind).ap()

    g.xc0 = din("xc0", [D, LT])
    g.cvec = din("cvec", [128, 8, 2])
    g.consts = din("consts", [128, 6, 128])
    g.w_ada = din("w_ada", [DEPTH, D, 6 * D])
    g.bada = din("bada", [128, DEPTH, 48])
    g.gvec = din("gvec", [128, DEPTH, 4, 8])
    g.w_in = din("w_in", [DEPTH, D, 2064])
    g.convp = din("convp", [128, DEPTH, 8, 4])
    g.dtb = din("dtb", [128, DEPTH, 16])
    g.alog = din("alog", [128, DEPTH, 16])
    g.dfull = din("dfull", [128, DEPTH, 512])
    g.gssd = din("gssd", [128, DEPTH, 512])
    g.s5a = din("s5a", [128, DEPTH, 2, 32])
    g.s5dt = din("s5dt", [128, DEPTH, 32])
    g.btab = din("btab", [128, DEPTH, 2, 16, 16])
    g.ctab = din("ctab", [128, DEPTH, 2, 16, 16])
    g.d5full = din("d5full", [128, DEPTH, 512])
    g.glu_w = din("glu_w", [DEPTH, 512, 512])
    g.glub = din("glub", [128, DEPTH, 4])
    g.w_out = din("w_out", [DEPTH, D, D])
    g.w_up = din("w_up", [DEPTH, D, 2 * DFF])
    g.cwf = din("cwf", [128, DEPTH, 2 * NJ, 4])
    g.w_down = din("w_down", [DEPTH, DFF, D])
    g.out = nc.dram_tensor("out", [D, L], F32, kind="ExternalOutput").ap()

    g.xa = dscr("xa", [D, LT])
    g.xb = dscr("xb", [D, LT])
    g.zs = dscr("zs", [LT, 512])
    g.dtk = dscr("dtk", [LT, 32])
    g.xbc = dscr("xbc", [1024, LT])
    g.xbcA = dscr("xbcA", [1024, LT])
    g.yf = dscr("yf", [LT, 512])
    g.ysT = dscr("ysT", [512, LT], BF16)
    g.y5T = dscr("y5T", [512, LT], BF16)
    g.uTok = dscr("uTok", [LT, 512])
    g.y5pre = dscr("y5pre", [LT, 512])

    with ExitStack() as st:
        P = Prog(nc, st)
        g.P = P
        for i in range(8):
            P.psl.append(T(st.enter_context(nc.psum_tensor(f"psb{i}", [128, 512], F32))))
        emit_all(nc, g, debug_stop)
        n = P.emit()
    return nc, n


def copy_op(P, eng, out_ap, in_ap, reads, writes):
    if eng == "act":
        P.op("act", lambda e: e.copy(out_ap, in_ap), reads=reads, writes=writes)
    else:
        P.op(eng, lambda e: e.tensor_copy(out_ap, in_ap), reads=reads, writes=writes)


def emit_all(nc, g, debug_stop):
    P = g.P
    g.cst = P.sb([128, 6, 128])
    P.dma("sp", g.cst[:], g.consts[:, :, :], writes=[g.cst])
    g.ident = lambda: g.cst[:, 0, :]
    g.onesf = lambda: g.cst[:, 3, :]
    g.onesb = P.sb([128, 128], BF16)
    P.op("dve", lambda e: e.tensor_copy(g.onesb[:], g.cst[:, 3, :]), reads=[g.cst], writes=[g.onesb])
    g.eps = P.sb([128, 1])
    P.op("dve", lambda e: e.memset(g.eps[:], EPS), writes=[g.eps])
    small = {}
    for name, shape in [("bada", [128, DEPTH, 48]), ("gvec", [128, DEPTH, 4, 8]), ("convp", [128, DEPTH, 8, 4]),
                        ("dtb", [128, DEPTH, 16]), ("alog", [128, DEPTH, 16]), ("dfull", [128, DEPTH, 512]),
                        ("gssd", [128, DEPTH, 512]), ("s5a", [128, DEPTH, 2, 32]), ("s5dt", [128, DEPTH, 32]),
                        ("btab", [128, DEPTH, 2, 16, 16]), ("ctab", [128, DEPTH, 2, 16, 16]), ("d5full", [128, DEPTH, 512]), ("glub", [128, DEPTH, 4]), ("cwf", [128, DEPTH, 2 * NJ, 4]),
                        ("cvec", [128, 8, 2])]:
        t = P.sb(shape)
        src = getattr(g, name)
        P.dma("sp", t[:], src, writes=[t])
        small[name] = t
    g.sm = small
    g.abc = P.sb([128, DEPTH, 16])
    P.op("act", lambda e: e.activation(g.abc[:], small["alog"][:], AF.Exp), reads=[small["alog"]], writes=[g.abc])
    P.op("dve", lambda e: e.tensor_scalar(g.abc[:], g.abc[:], -1.0, None, ALU.mult), reads=[g.abc], writes=[g.abc])
    g.sc = P.sb([128, 8, 2])
    P.op("act", lambda e: e.activation(g.sc[:], small["cvec"][:], AF.Silu), reads=[small["cvec"]], writes=[g.sc])
    g.mod = P.sb([128, DEPTH, 6, 8, 2])
    g.HT = [P.sb([128, 512]) for _ in range(2)]
    g.HTb = [P.sb([128, 512], BF16) for _ in range(2)]

    stage_ada(nc, g)
    if debug_stop == "ada":
        return
    xsrc = g.xc0
    for l in range(DEPTH):
        last = l == DEPTH - 1
        xdst = g.out if last else g.xb
        stage_inproj(nc, g, l, xsrc)
        if debug_stop == f"inproj{l}":
            return
        stage_conv(nc, g, l)
        if debug_stop == f"conv{l}":
            return
        for d in range(2):
            stage_ssd(nc, g, l, d)
        if debug_stop == f"ssd{l}":
            return
        stage_s5(nc, g, l, last)
        if S5STOP:
            return
        if debug_stop == f"s5{l}":
            return
        stage_outproj(nc, g, l, xsrc, g.xa, last)
        if debug_stop == f"outproj{l}":
            return
        stage_ffn(nc, g, l, g.xa, xdst, last)
        if debug_stop == f"ffn{l}":
            return
        xsrc = g.xb


def stage_ada(nc, g):
    P = g.P
    sm = g.sm
    with ExitStack() as st:
        old = P.stack
        P.stack = st
        wb = [P.sb([128, 8, 512]) for _ in range(2)]
        ada = P.sb([128, 48, 2])
        tmp = P.sb([128, 8, 2])
        for l in range(DEPTH):
            for pc in range(12):
                w = wb[pc % 2]
                P.dma("sp", w[:], g.w_ada[l, :, pc * 512:(pc + 1) * 512].rearrange("(k p) f -> p k f", p=128), writes=[w])
                for jj in range(4):
                    j = pc * 4 + jj
                    ps = P.nextps()
                    for k in range(8):
                        P.op("pe", lambda e, ps=ps, w=w, k=k, jj=jj: e.matmul(ps[:, 0:2], w[:, k, jj * 128:(jj + 1) * 128], g.sc[:, k, :], start=(k == 0), stop=(k == 7)),
                             reads=[w, g.sc], writes=[ps])
                    P.op("dve", lambda e, ps=ps, j=j, l=l: e.tensor_tensor(ada[:, j, :], ps[:, 0:2], sm["bada"][:, l, j:j + 1].to_broadcast([128, 2]), ALU.add),
                         reads=[ps, sm["bada"]], writes=[ada])
            gv = sm["gvec"]
            md = g.mod
            for (dst, scl, gi) in [(0, 8, 0), (3, 32, 2)]:
                P.op("dve", lambda e, scl=scl: e.tensor_scalar(tmp[:], ada[:, scl:scl + 8, :], 1.0, None, ALU.add), reads=[ada], writes=[tmp])
                P.op("dve", lambda e, dst=dst, gi=gi, l=l: e.tensor_tensor(md[:, l, dst, :, :], tmp[:], gv[:, l, gi, :].unsqueeze(2).to_broadcast([128, 8, 2]), ALU.mult),
                     reads=[tmp, gv], writes=[md])
            for (dst, src) in [(1, 0), (4, 24)]:
                P.op("dve", lambda e, dst=dst, src=src, l=l: e.tensor_copy(md[:, l, dst, :, :], ada[:, src:src + 8, :]), reads=[ada], writes=[md])
            for (dst, src, gi) in [(2, 16, 1), (5, 40, 3)]:
                P.op("dve", lambda e, dst=dst, src=src, gi=gi, l=l: e.tensor_tensor(md[:, l, dst, :, :], ada[:, src:src + 8, :], gv[:, l, gi, :].unsqueeze(2).to_broadcast([128, 8, 2]), ALU.mult),
                     reads=[ada, gv], writes=[md])
        P.barrier()
        P.stack = old


def load_weight_bf16(P, dst, dst_cols, src_ap_fn, ncols, stg, piece=512):
    i = 0
    c0 = 0
    while c0 < ncols:
        c1 = min(ncols, c0 + piece)
        s = stg[i % len(stg)]
        i += 1
        w = c1 - c0
        P.dma("sp", s[:, :, 0:w], src_ap_fn(c0, c1), writes=[s])
        P.op("pool", lambda e, s=s, c0=c0, c1=c1, w=w: e.tensor_copy(dst[:, :, dst_cols + c0:dst_cols + c1], s[:, :, 0:w]), reads=[s], writes=[dst])
        c0 = c1


def norm_mod(P, g, xt, hT, TW, sq, tmp, rstd, s_ap, sh_ap, hoff=0):
    P.op("act", lambda e: e.activation(sq[:, :, 0:TW], xt[:, :, 0:TW], AF.Square), reads=[xt], writes=[sq])
    ps = P.nextps()
    for k in range(8):
        P.op("pe", lambda e, k=k: e.matmul(ps[:, 0:TW], g.onesb[:], sq[:, k, 0:TW], start=(k == 0), stop=(k == 7)), reads=[g.onesb, sq], writes=[ps])
    P.op("act", lambda e: e.activation(rstd[:, 0:TW], ps[:, 0:TW], AF.Sqrt, bias=g.eps[:], scale=1.0 / D), reads=[ps, g.eps], writes=[rstd])
    P.op("dve", lambda e: e.reciprocal(rstd[:, 0:TW], rstd[:, 0:TW]), reads=[rstd], writes=[rstd])
    for k in range(8):
        sa = s_ap(k)
        sha = sh_ap(k)
        P.op("dve", lambda e, k=k: e.tensor_tensor(tmp[:, k, 0:TW], xt[:, k, 0:TW], rstd[:, 0:TW], ALU.mult), reads=[xt, rstd], writes=[tmp])
        P.op("act", lambda e, k=k, sa=sa, sha=sha: e.activation(hT[:, k, hoff:hoff + TW], tmp[:, k, 0:TW], AF.Identity, bias=sha, scale=sa),
             reads=[tmp, g.mod], writes=[hT])


def stage_inproj(nc, g, l, xsrc):
    P = g.P
    sm = g.sm
    with ExitStack() as st:
        old = P.stack
        P.stack = st
        win = P.sb([128, 8, 2064], BF16)
        stg = [P.sb([128, 8, 512]) for _ in range(2)]
        load_weight_bf16(P, win, 0, lambda c0, c1: g.w_in[l, :, c0:c1].rearrange("(k p) f -> p k f", p=128), 2064, stg)
        xts = [P.sb([128, 8, 512]) for _ in range(2)]
        sq = P.sb([128, 8, 512], BF16)
        tmp = P.sb([128, 8, 512])
        rstd = P.sb([128, 512])
        hT = P.sb([128, 8, 512], BF16)
        zsb = [P.sb([128, 512]) for _ in range(2)]
        dts = [P.sb([128, 32]) for _ in range(2)]
        xo = [P.sb([128, 8, 512]) for _ in range(2)]
        uo = [P.sb([128, 512]) for _ in range(2)]
        ev = 0
        for ti, (tok0, TW, v, s0, s1, rl) in enumerate(token_tiles()):
            xt = xts[ti % 2]
            P.dma("sp", xt[:, :, 0:TW], xsrc[:, tok0:tok0 + TW].rearrange("(k p) t -> p k t", p=128), writes=[xt])
            norm_mod(P, g, xt, hT, TW, sq, tmp, rstd,
                     lambda k: g.mod[:, l, 0, k, v:v + 1], lambda k: g.mod[:, l, 1, k, v:v + 1])
            for s in range(TW // 128):
                ps = P.nextps()
                for k in range(8):
                    P.op("pe", lambda e, ps=ps, k=k, s=s: e.matmul(ps[:, :], hT[:, k, s * 128:(s + 1) * 128], win[:, k, 0:512], start=(k == 0), stop=(k == 7)),
                         reads=[hT, win], writes=[ps])
                zb = zsb[s % 2]
                P.op("act", lambda e, ps=ps, zb=zb: e.activation(zb[:], ps[:, :], AF.Silu), reads=[ps], writes=[zb])
                P.dma("pool", g.zs[tok0 + s * 128:tok0 + (s + 1) * 128, :], zb[:], reads=[zb])
                ps2 = P.nextps()
                for k in range(8):
                    P.op("pe", lambda e, ps2=ps2, k=k, s=s: e.matmul(ps2[:, 0:16], hT[:, k, s * 128:(s + 1) * 128], win[:, k, 1536:1552], start=(k == 0), stop=(k == 7)),
                         reads=[hT, win], writes=[ps2])
                db = dts[s % 2]
                P.op("dve", lambda e, ps2=ps2, db=db: e.tensor_tensor(db[:, 0:16], ps2[:, 0:16], sm["dtb"][:, l, :], ALU.add), reads=[ps2, sm["dtb"]], writes=[db])
                P.op("act", lambda e, db=db: e.activation(db[:, 0:16], db[:, 0:16], AF.Exp), reads=[db], writes=[db])
                P.op("act", lambda e, db=db: e.activation(db[:, 0:16], db[:, 0:16], AF.Ln, bias=1.0), reads=[db], writes=[db])
                P.op("dve", lambda e, db=db: e.tensor_tensor(db[:, 16:32], db[:, 0:16], g.abc[:, l, :], ALU.mult), reads=[db, g.abc], writes=[db])
                P.dma("pool", g.dtk[tok0 + s * 128:tok0 + (s + 1) * 128, :], db[:], reads=[db])
                ps3 = P.nextps()
                for k in range(8):
                    P.op("pe", lambda e, ps3=ps3, k=k, s=s: e.matmul(ps3[:, :], hT[:, k, s * 128:(s + 1) * 128], win[:, k, 1552:2064], start=(k == 0), stop=(k == 7)),
                         reads=[hT, win], writes=[ps3])
                ub = uo[s % 2]
                P.op("dve", lambda e, ps3=ps3, ub=ub: e.tensor_copy(ub[:], ps3[:, :]), reads=[ps3], writes=[ub])
                P.dma("pool", g.uTok[tok0 + s * 128:tok0 + (s + 1) * 128, :], ub[:], reads=[ub])
            xob = xo[ti % 2]
            for j in range(8):
                c0 = 512 + j * 128
                ps = P.nextps()
                for k in range(8):
                    P.op("pe", lambda e, ps=ps, k=k, c0=c0, TW=TW: e.matmul(ps[:, 0:TW], win[:, k, c0:c0 + 128], hT[:, k, 0:TW], start=(k == 0), stop=(k == 7)),
                         reads=[hT, win], writes=[ps])
                copy_op(P, "act" if ev % 2 == 0 else "dve", xob[:, j, 0:TW], ps[:, 0:TW], [ps], [xob])
                ev += 1
            P.dma("pool", g.xbc[:, tok0:tok0 + TW].rearrange("(k p) t -> p k t", p=128), xob[:, :, 0:TW], reads=[xob])
        P.barrier()
        P.stack = old


def stage_conv(nc, g, l):
    P = g.P
    cp = g.sm["convp"]
    with ExitStack() as st:
        old = P.stack
        P.stack = st
        xin = [P.sb([128, 8, 514]) for _ in range(2)]
        acc = [P.sb([128, 8, 512]) for _ in range(2)]
        for ti, (tok0, TW, v, s0, s1, rl) in enumerate(token_tiles()):
            xi = xin[ti % 2]
            ac = acc[ti % 2]
            P.op("pool", lambda e, xi=xi: e.memset(xi[:, :, 0:1], 0.0), writes=[xi])
            P.op("pool", lambda e, xi=xi, TW=TW: e.memset(xi[:, :, TW + 1:TW + 2], 0.0), writes=[xi])
            a = max(s0, tok0 - 1)
            b = min(s1, tok0 + TW + 1)
            P.dma("sp", xi[:, :, a - (tok0 - 1):b - (tok0 - 1)], g.xbc[:, a:b].rearrange("(k p) t -> p k t", p=128), writes=[xi])
            for j in range(8):
                P.op("act", lambda e, xi=xi, ac=ac, j=j, TW=TW: e.activation(ac[:, j, 0:TW], xi[:, j, 0:TW], AF.Identity, bias=cp[:, l, j, 3:4], scale=cp[:, l, j, 0:1]),
                     reads=[xi, cp], writes=[ac])
                P.op("dve", lambda e, xi=xi, ac=ac, j=j, TW=TW: e.scalar_tensor_tensor(ac[:, j, 0:TW], xi[:, j, 1:TW + 1], cp[:, l, j, 1:2], ac[:, j, 0:TW], ALU.mult, ALU.add),
                     reads=[xi, cp, ac], writes=[ac])
                P.op("dve", lambda e, xi=xi, ac=ac, j=j, TW=TW: e.scalar_tensor_tensor(ac[:, j, 0:TW], xi[:, j, 2:TW + 2], cp[:, l, j, 2:3], ac[:, j, 0:TW], ALU.mult, ALU.add),
                     reads=[xi, cp, ac], writes=[ac])
            P.op("act", lambda e, ac=ac, TW=TW: e.activation(ac[:, :, 0:TW], ac[:, :, 0:TW], AF.Silu), reads=[ac], writes=[ac])
            P.dma("pool", g.xbcA[:, tok0:tok0 + TW].rearrange("(k p) t -> p k t", p=128), ac[:, :, 0:TW], reads=[ac])
        P.barrier()
        P.stack = old


STAGES = {}


def chunk_list(d):
    ctx = [c * 128 for c in range(LC // 128)]
    lat = [LC + c * 128 for c in range(L // 128)]
    if d == 0:
        return ctx + lat
    return ctx[::-1] + lat[::-1]


def stage_ssd(nc, g, l, d):
    P = g.P
    sm = g.sm
    msk = (lambda: g.cst[:, 1, :]) if d == 0 else (lambda: g.cst[:, 2, :])
    idx = 127 if d == 0 else 0
    HT = g.HT[d]
    HTb = g.HTb[d]
    with ExitStack() as st:
        old = P.stack
        P.stack = st
        P.op("dve", lambda e: e.memset(HT[:], 0.0), writes=[HT])
        P.op("dve", lambda e: e.memset(HTb[:], 0.0), writes=[HTb])
        xin = [P.sb([128, 8, 128]) for _ in range(2)]
        dtk = [P.sb([128, 32]) for _ in range(2)]
        yfl = [P.sb([128, 512]) for _ in range(2)]
        zsl = [P.sb([128, 512]) for _ in range(2)]
        def alloc_set():
            return (P.sb([128, 512]), P.sb([128, 256], BF16), P.sb([128, 1024]), P.sb([128, 1024]), P.sb([128, 256]), P.sb([128, 1024], BF16),
                    P.sb([128, 1024]), P.sb([128, 1024], BF16), P.sb([128, 512], BF16), P.sb([128, 512], BF16), P.sb([128, 8]), P.sb([128, 8]),
                    P.sb([128, 512]), P.sb([128, 512]), P.sb([128, 1]), P.sb([128, 512]))

        sets = [alloc_set() for _ in range(2)]
        yo = [P.sb([128, 512]) for _ in range(2)]
        yst = [P.sb([128, 4, 128], BF16) for _ in range(2)]
        def body(ci, tok, S_):
            xtok, btok, R, seg, cbt, MT, E, CTs, xdt, xdtw, te, cumtok, dx, htmp, ss, junk = S_
            xi = xin[ci % 2]
            dk = dtk[ci % 2]
            P.dma("sp", xi[:], g.xbcA[:, tok:tok + 128].rearrange("(k p) t -> p k t", p=128), writes=[xi])
            P.dma("sp", dk[:], g.dtk[tok:tok + 128, :], writes=[dk])
            if d == 1:
                yl = yfl[ci % 2]
                zl = zsl[ci % 2]
                P.dma("sp", yl[:], g.yf[tok:tok + 128, :], writes=[yl])
                P.dma("sp", zl[:], g.zs[tok:tok + 128, :], writes=[zl])
            dtA = lambda dk=dk: dk[:, 16 + d * 8:24 + d * 8]
            dtv = lambda dk=dk: dk[:, d * 8:d * 8 + 8]
            psX = P.nextps()
            for j in range(4):
                P.op("pe", lambda e, j=j, xi=xi, psX=psX: e.transpose(psX[:, j * 128:(j + 1) * 128], xi[:, j, :], g.ident()), reads=[xi, g.cst], writes=[psX])
            P.op("act", lambda e, psX=psX: e.copy(xtok[:], psX[:, :]), reads=[psX], writes=[xtok])
            psB = P.nextps()
            for gg in range(2):
                P.op("pe", lambda e, gg=gg, xi=xi, psB=psB: e.transpose(psB[:, gg * 128:(gg + 1) * 128], xi[:, 4 + gg, :], g.ident()), reads=[xi, g.cst], writes=[psB])
            P.op("dve", lambda e, psB=psB: e.tensor_copy(btok[:], psB[:, 0:256]), reads=[psB], writes=[btok])
            P.op("dve", lambda e, dtA=dtA: e.tensor_tensor(R[:].rearrange("p (h i) -> p h i", h=8), msk().unsqueeze(1).to_broadcast([128, 8, 128]),
                                                          dtA().unsqueeze(2).to_broadcast([128, 8, 128]), ALU.mult), reads=[g.cst, dk], writes=[R])
            cum = [P.nextps(), P.nextps()]
            for hh in range(2):
                P.op("pe", lambda e, hh=hh, cum=cum: e.matmul(cum[hh][:, :], g.onesf(), R[:, hh * 512:(hh + 1) * 512], start=True, stop=True), reads=[g.cst, R], writes=[cum[hh]])
            psS = P.nextps()
            P.op("pe", lambda e, psS=psS, dtA=dtA: e.matmul(psS[:, 256:264], msk(), dtA(), start=True, stop=True), reads=[g.cst, dk], writes=[psS])
            for gg in range(2):
                P.op("pe", lambda e, psS=psS, gg=gg, xi=xi: e.matmul(psS[:, gg * 128:(gg + 1) * 128], xi[:, 4 + gg, :], xi[:, 6 + gg, :], start=True, stop=True), reads=[xi], writes=[psS])
            P.op("act", lambda e, psS=psS: e.copy(cumtok[:], psS[:, 256:264]), reads=[psS], writes=[cumtok])
            for h in range(8):
                P.op("dve", lambda e, h=h, cum=cum: e.tensor_scalar(seg[:, h * 128:(h + 1) * 128], cum[h // 4][:, (h % 4) * 128:(h % 4 + 1) * 128], cumtok[:, h:h + 1], 0.0, ALU.subtract, ALU.min),
                     reads=[cum[h // 4], cumtok], writes=[seg])
            P.op("act", lambda e: e.activation(seg[:], seg[:], AF.Exp), reads=[seg], writes=[seg])
            P.op("dve", lambda e, psS=psS: e.tensor_tensor(cbt[:].rearrange("p (g i) -> p g i", g=2), psS[:, 0:256].rearrange("p (g i) -> p g i", g=2),
                                                          msk().unsqueeze(1).to_broadcast([128, 2, 128]), ALU.mult), reads=[psS, g.cst], writes=[cbt])
            P.op("dve", lambda e: e.tensor_tensor(MT[:].rearrange("p (g e i) -> p g e i", g=2, e=4), seg[:].rearrange("p (g e i) -> p g e i", g=2, e=4),
                                                 cbt[:].rearrange("p (g i) -> p g i", g=2).unsqueeze(2).to_broadcast([128, 2, 4, 128]), ALU.mult), reads=[seg, cbt], writes=[MT])
            for hh in range(2):
                P.op("act", lambda e, hh=hh, cum=cum: e.activation(E[:, hh * 512:(hh + 1) * 512], cum[hh][:, :], AF.Exp), reads=[cum[hh]], writes=[E])
            P.op("dve", lambda e, xi=xi: e.tensor_tensor(CTs[:].rearrange("p (g e i) -> p g e i", g=2, e=4), E[:].rearrange("p (g e i) -> p g e i", g=2, e=4),
                                                        xi[:, 6:8, :].unsqueeze(2).to_broadcast([128, 2, 4, 128]), ALU.mult), reads=[E, xi], writes=[CTs])
            P.op("dve", lambda e, dtv=dtv: e.tensor_tensor(xdt[:].rearrange("p (h q) -> p h q", h=8), xtok[:].rearrange("p (h q) -> p h q", h=8),
                                                          dtv().unsqueeze(2).to_broadcast([128, 8, 64]), ALU.mult), reads=[xtok, dk], writes=[xdt])
            for hh in range(2):
                P.op("dve", lambda e, hh=hh, cum=cum: e.tensor_tensor(te[:, hh * 4:(hh + 1) * 4], cum[hh][:, :].rearrange("p (h i) -> p h i", h=4)[:, :, idx], cumtok[:, hh * 4:(hh + 1) * 4], ALU.subtract),
                     reads=[cum[hh], cumtok], writes=[te])
            P.op("act", lambda e: e.activation(te[:], te[:], AF.Exp), reads=[te], writes=[te])
            P.op("dve", lambda e: e.tensor_tensor(xdtw[:].rearrange("p (h q) -> p h q", h=8), xdt[:].rearrange("p (h q) -> p h q", h=8),
                                                 te[:].unsqueeze(2).to_broadcast([128, 8, 64]), ALU.mult), reads=[xdt, te], writes=[xdtw])
            psY = P.nextps()
            for h in range(8):
                P.op("pe", lambda e, h=h, psY=psY: e.matmul(psY[:, h * 64:(h + 1) * 64], MT[:, h * 128:(h + 1) * 128], xdt[:, h * 64:(h + 1) * 64], start=True, stop=False),
                     reads=[MT, xdt], writes=[psY])
                P.op("pe", lambda e, h=h, psY=psY: e.matmul(psY[:, h * 64:(h + 1) * 64], CTs[:, h * 128:(h + 1) * 128], HTb[:, h * 64:(h + 1) * 64], start=False, stop=True),
                     reads=[CTs, HTb], writes=[psY])
            psH = P.nextps()
            for gg in range(2):
                P.op("pe", lambda e, gg=gg, psH=psH: e.matmul(psH[:, gg * 256:(gg + 1) * 256], btok[:, gg * 128:(gg + 1) * 128], xdtw[:, gg * 256:(gg + 1) * 256], start=True, stop=True),
                     reads=[btok, xdtw], writes=[psH])
            P.op("dve", lambda e: e.tensor_tensor(htmp[:].rearrange("p (h q) -> p h q", h=8), HT[:].rearrange("p (h q) -> p h q", h=8),
                                                 E[:].rearrange("p (h i) -> p h i", h=8)[:, :, idx:idx + 1].to_broadcast([128, 8, 64]), ALU.mult), reads=[HT, E], writes=[htmp])
            P.op("dve", lambda e, psH=psH: e.tensor_tensor(HT[:], htmp[:], psH[:, :], ALU.add), reads=[htmp, psH], writes=[HT])
            P.op("act", lambda e: e.copy(HTb[:], HT[:]), reads=[HT], writes=[HTb])
            y = yo[ci % 2]
            if d == 0:
                P.op("dve", lambda e: e.tensor_tensor(dx[:], xtok[:], sm["dfull"][:, l, :], ALU.mult), reads=[xtok, sm["dfull"]], writes=[dx])
                P.op("dve", lambda e, y=y, psY=psY: e.tensor_tensor(y[:], psY[:, :], dx[:], ALU.add), reads=[psY, dx], writes=[y])
                P.dma("pool", g.yf[tok:tok + 128, :], y[:], reads=[y])
            else:
                P.op("dve", lambda e, y=y, psY=psY, yl=yl: e.tensor_tensor(y[:], psY[:, :], yl[:], ALU.add), reads=[psY, yl], writes=[y])
                P.op("dve", lambda e, y=y, zl=zl: e.tensor_tensor(y[:], y[:], zl[:], ALU.mult), reads=[y, zl], writes=[y])
                P.op("act", lambda e, y=y: e.activation(junk[:], y[:], AF.Square, accum_out=ss[:]), reads=[y], writes=[junk, ss])
                P.op("act", lambda e: e.activation(ss[:], ss[:], AF.Sqrt, bias=g.eps[:], scale=1.0 / 512), reads=[ss, g.eps], writes=[ss])
                P.op("dve", lambda e: e.reciprocal(ss[:], ss[:]), reads=[ss], writes=[ss])
                P.op("dve", lambda e, y=y: e.scalar_tensor_tensor(y[:], y[:], ss[:, 0:1], sm["gssd"][:, l, :], ALU.mult, ALU.mult), reads=[y, ss, sm["gssd"]], writes=[y])
                psT = P.nextps()
                for j in range(4):
                    P.op("pe", lambda e, j=j, y=y, psT=psT: e.transpose(psT[:, j * 128:(j + 1) * 128], y[:, j * 128:(j + 1) * 128], g.ident()), reads=[y, g.cst], writes=[psT])
                ys = yst[ci % 2]
                P.op("act", lambda e, ys=ys, psT=psT: e.copy(ys[:].rearrange("p j t -> p (j t)"), psT[:, :]), reads=[psT], writes=[ys])
                P.dma("pool", g.ysT[:, tok:tok + 128].rearrange("(k p) t -> p k t", p=128), ys[:], reads=[ys])
        for ci, tok in enumerate(chunk_list(d)):
            body(ci, tok, sets[ci % 2])
        P.barrier()
        P.stack = old


NB = LT // 8
NBC = LC // 8
NBL = L // 8
NS5 = 10
BT = [(0, 128), (128, 128), (256, 128), (384, 128), (512, 32)]
CH = [(0, 272), (272, 544)]


import os
S5STOP = int(os.environ.get('S5STOP', '0'))


def stage_s5(nc, g, l, last=False):
    P = g.P
    sm = g.sm
    with ExitStack() as st:
        old = P.stack
        P.stack = st
        Pasc = [P.sb([128, 32, 9]) for _ in range(2)]
        Pneg = [P.sb([128, 32, 8]) for _ in range(2)]
        Pdsc = [P.sb([128, 32, 9]) for _ in range(2)]
        Qasc = [P.sb([128, 32, 8]) for _ in range(2)]
        Qneg = [P.sb([128, 32, 8]) for _ in range(2)]
        Qdsc = [P.sb([128, 32, 8]) for _ in range(2)]
        HS = P.sb([128, NS5, 3, 32])
        with ExitStack() as st2:
            P.stack = st2
            tt = [P.sb([128, 32]) for _ in range(16)]
            step, xr, xi_, mag, cs, sn, t1, t2, t3, nr, den, cr, ci, ir, ii, t4 = tt
            a_re = lambda: sm["s5a"][:, l, 0, :]
            a_im = lambda: sm["s5a"][:, l, 1, :]

            def TTm(o, a, b, opx, rd, wr):
                P.op("dve", lambda e: e.tensor_tensor(o(), a(), b(), opx), reads=rd, writes=wr)

            P.op("act", lambda e: e.activation(step[:], sm["s5dt"][:, l, :], AF.Exp), reads=[sm["s5dt"]], writes=[step])
            TTm(lambda: xr[:], a_re, lambda: step[:], ALU.mult, [sm["s5a"], step], [xr])
            TTm(lambda: xi_[:], a_im, lambda: step[:], ALU.mult, [sm["s5a"], step], [xi_])
            P.op("act", lambda e: e.activation(mag[:], xr[:], AF.Exp), reads=[xr], writes=[mag])
            P.op("act", lambda e: e.activation(sn[:], xi_[:], AF.Sin, scale=1.0 / 16), reads=[xi_], writes=[sn])
            hp = P.sb([128, 1])
            P.op("dve", lambda e: e.memset(hp[:], math.pi / 2), writes=[hp])
            P.op("act", lambda e: e.activation(cs[:], xi_[:], AF.Sin, bias=hp[:], scale=1.0 / 16), reads=[xi_, hp], writes=[cs])
            for _ in range(4):
                TTm(lambda: t1[:], lambda: cs[:], lambda: cs[:], ALU.mult, [cs], [t1])
                TTm(lambda: t2[:], lambda: sn[:], lambda: sn[:], ALU.mult, [sn], [t2])
                P.op("dve", lambda e: e.scalar_tensor_tensor(t3[:], sn[:], 2.0, cs[:], ALU.mult, ALU.mult), reads=[sn, cs], writes=[t3])
                TTm(lambda: cs[:], lambda: t1[:], lambda: t2[:], ALU.subtract, [t1, t2], [cs])
                P.op("dve", lambda e: e.tensor_copy(sn[:], t3[:]), reads=[t3], writes=[sn])
            P.op("dve", lambda e: e.memset(Pasc[0][:, :, 0:1], 1.0), writes=[Pasc[0]])
            P.op("dve", lambda e: e.memset(Pasc[1][:, :, 0:1], 0.0), writes=[Pasc[1]])
            P.op("dve", lambda e: e.memset(Pneg[0][:, :, 0:1], 1.0), writes=[Pneg[0]])
            P.op("dve", lambda e: e.memset(Pneg[1][:, :, 0:1], 0.0), writes=[Pneg[1]])
            TTm(lambda: Pasc[0][:, :, 1], lambda: mag[:], lambda: cs[:], ALU.mult, [mag, cs], [Pasc[0]])
            TTm(lambda: Pasc[1][:, :, 1], lambda: mag[:], lambda: sn[:], ALU.mult, [mag, sn], [Pasc[1]])
            lbr = lambda: Pasc[0][:, :, 1]
            lbi = lambda: Pasc[1][:, :, 1]

            def cmul(o_r, o_i, ar_, ai_, br_, bi_, rd, wr):
                TTm(lambda: t1[:], ar_, br_, ALU.mult, rd, [t1])
                TTm(lambda: t2[:], ai_, bi_, ALU.mult, rd, [t2])
                TTm(lambda: t3[:], ar_, bi_, ALU.mult, rd, [t3])
                TTm(lambda: t4[:], ai_, br_, ALU.mult, rd, [t4])
                TTm(o_r, lambda: t1[:], lambda: t2[:], ALU.subtract, [t1, t2], wr)
                TTm(o_i, lambda: t3[:], lambda: t4[:], ALU.add, [t3, t4], wr)

            for k in range(1, 8):
                cmul(lambda k=k: Pasc[0][:, :, k + 1], lambda k=k: Pasc[1][:, :, k + 1], lambda k=k: Pasc[0][:, :, k], lambda k=k: Pasc[1][:, :, k], lbr, lbi, Pasc, Pasc)
            TTm(lambda: t1[:], lbr, lbr, ALU.mult, Pasc, [t1])
            TTm(lambda: t2[:], lbi, lbi, ALU.mult, Pasc, [t2])
            TTm(lambda: den[:], lambda: t1[:], lambda: t2[:], ALU.add, [t1, t2], [den])
            P.op("dve", lambda e: e.reciprocal(den[:], den[:]), reads=[den], writes=[den])
            TTm(lambda: ir[:], lbr, lambda: den[:], ALU.mult, Pasc + [den], [ir])
            P.op("dve", lambda e: e.scalar_tensor_tensor(ii[:], lbi(), -1.0, den[:], ALU.mult, ALU.mult), reads=Pasc + [den], writes=[ii])
            P.op("dve", lambda e: e.tensor_copy(Pneg[0][:, :, 1], ir[:]), reads=[ir], writes=[Pneg[0]])
            P.op("dve", lambda e: e.tensor_copy(Pneg[1][:, :, 1], ii[:]), reads=[ii], writes=[Pneg[1]])
            for k in range(1, 7):
                cmul(lambda k=k: Pneg[0][:, :, k + 1], lambda k=k: Pneg[1][:, :, k + 1], lambda k=k: Pneg[0][:, :, k], lambda k=k: Pneg[1][:, :, k], lambda: ir[:], lambda: ii[:], Pneg + [ir, ii], Pneg)
            for c in range(2):
                for i in range(9):
                    P.op("dve", lambda e, c=c, i=i: e.tensor_copy(Pdsc[c][:, :, i], Pasc[c][:, :, 8 - i]), reads=[Pasc[c]], writes=[Pdsc[c]])
            P.op("dve", lambda e: e.tensor_scalar(nr[:], lbr(), -1.0, None, ALU.add), reads=Pasc, writes=[nr])
            TTm(lambda: t1[:], a_re, a_re, ALU.mult, [sm["s5a"]], [t1])
            TTm(lambda: t2[:], a_im, a_im, ALU.mult, [sm["s5a"]], [t2])
            TTm(lambda: den[:], lambda: t1[:], lambda: t2[:], ALU.add, [t1, t2], [den])
            P.op("dve", lambda e: e.reciprocal(den[:], den[:]), reads=[den], writes=[den])
            TTm(lambda: t1[:], lambda: nr[:], a_re, ALU.mult, [nr, sm["s5a"]], [t1])
            TTm(lambda: t2[:], lbi, a_im, ALU.mult, Pasc + [sm["s5a"]], [t2])
            TTm(lambda: t1[:], lambda: t1[:], lambda: t2[:], ALU.add, [t1, t2], [t1])
            TTm(lambda: cr[:], lambda: t1[:], lambda: den[:], ALU.mult, [t1, den], [cr])
            TTm(lambda: t1[:], lbi, a_re, ALU.mult, Pasc + [sm["s5a"]], [t1])
            TTm(lambda: t2[:], lambda: nr[:], a_im, ALU.mult, [nr, sm["s5a"]], [t2])
            TTm(lambda: t1[:], lambda: t1[:], lambda: t2[:], ALU.subtract, [t1, t2], [t1])
            TTm(lambda: ci[:], lambda: t1[:], lambda: den[:], ALU.mult, [t1, den], [ci])
            big = [P.sb([128, 32, 8]) for _ in range(4)]
            crb = lambda: cr[:].unsqueeze(2).to_broadcast([128, 32, 8])
            cib = lambda: ci[:].unsqueeze(2).to_broadcast([128, 32, 8])
            for (Qt, Pt) in [(Qasc, Pasc), (Qneg, Pneg)]:
                pr_ = lambda Pt=Pt: Pt[0][:, :, 0:8]
                pi_ = lambda Pt=Pt: Pt[1][:, :, 0:8]
                TTm(lambda: big[0][:], crb, pr_, ALU.mult, [cr] + Pt, [big[0]])
                TTm(lambda: big[1][:], cib, pi_, ALU.mult, [ci] + Pt, [big[1]])
                TTm(lambda: big[2][:], crb, pi_, ALU.mult, [cr] + Pt, [big[2]])
                TTm(lambda: big[3][:], cib, pr_, ALU.mult, [ci] + Pt, [big[3]])
                TTm(lambda Qt=Qt: Qt[0][:], lambda: big[0][:], lambda: big[1][:], ALU.subtract, [big[0], big[1]], [Qt[0]])
                TTm(lambda Qt=Qt: Qt[1][:], lambda: big[2][:], lambda: big[3][:], ALU.add, [big[2], big[3]], [Qt[1]])
            for c in range(2):
                for i in range(8):
                    P.op("dve", lambda e, c=c, i=i: e.tensor_copy(Qdsc[c][:, :, i], Qasc[c][:, :, 7 - i]), reads=[Qasc[c]], writes=[Qdsc[c]])
            P.op("dve", lambda e: e.tensor_copy(HS[:, 0, 0, :], Pasc[0][:, :, 8]), reads=[Pasc[0]], writes=[HS])
            P.op("dve", lambda e: e.tensor_copy(HS[:, 0, 1, :], Pasc[1][:, :, 8]), reads=[Pasc[1]], writes=[HS])
            for s_ in range(1, NS5):
                TTm(lambda s_=s_: t1[:], lambda s_=s_: HS[:, s_ - 1, 0, :], lambda s_=s_: HS[:, s_ - 1, 0, :], ALU.mult, [HS], [t1])
                TTm(lambda s_=s_: t2[:], lambda s_=s_: HS[:, s_ - 1, 1, :], lambda s_=s_: HS[:, s_ - 1, 1, :], ALU.mult, [HS], [t2])
                P.op("dve", lambda e, s_=s_: e.scalar_tensor_tensor(HS[:, s_, 1, :], HS[:, s_ - 1, 0, :], 2.0, HS[:, s_ - 1, 1, :], ALU.mult, ALU.mult), reads=[HS], writes=[HS])
                TTm(lambda s_=s_: HS[:, s_, 0, :], lambda: t1[:], lambda: t2[:], ALU.subtract, [t1, t2, HS], [HS])
            P.op("dve", lambda e: e.tensor_scalar(HS[:, :, 2, :], HS[:, :, 1, :], -1.0, None, ALU.mult), reads=[HS], writes=[HS])
            P.barrier()
            P.stack = st
        if S5STOP == 1:
            P.barrier()
            P.stack = old
            return
        T8 = [[P.sb([128, 2, 128], BF16) for _ in range(2)] for _ in range(2)]
        Bz = [[P.sb([128, 2, 2, 128], BF16) for _ in range(2)] for _ in range(2)]
        OC = [[[P.sb([128, 2, 128], BF16) for _ in range(2)] for _ in range(2)] for _ in range(2)]
        for par in range(2):
            for d in range(2):
                for g2 in range(2):
                    P.op("pool", lambda e, par=par, d=d, g2=g2: e.memset(OC[par][d][g2][:], 0.0), writes=[OC[par][d][g2]])
        for par in range(2):
            for d in range(2):
                P.op("pool", lambda e, par=par, d=d: e.memset(Bz[par][d][:], 0.0), writes=[Bz[par][d]])
        tb = [P.sb([128, 128]) for _ in range(4)]
        LW = [[P.sb([128, 128], BF16) for _ in range(2)] for _ in range(2)]
        for g2 in range(2):
            for c_ in range(2):
                P.op("pool", lambda e, g2=g2, c_=c_: e.memset(LW[g2][c_][:], 0.0), writes=[LW[g2][c_]])
        RC = [P.sb([128, 128], BF16) for _ in range(2)]
        PB = [P.sb([128, 128]) for _ in range(2)]
        u8 = [P.sb([128, 2, NB], BF16) for _ in range(2)]
        ul = [P.sb([128, 128]) for _ in range(3)]
        ulz = P.sb([128, 128])
        P.op("pool", lambda e: e.memset(ulz[:], 0.0), writes=[ulz])
        X = [P.sb([128, 2, NB]) for _ in range(2)]
        X2 = P.sb([128, 2, NB])
        Xb = [P.sb([128, 2, NB], BF16) for _ in range(2)]
        Y8 = [P.sb([128, 640]) for _ in range(2)]
        for i_ in range(2):
            P.op("pool", lambda e, i_=i_: e.memset(Y8[i_][:], 0.0), writes=[Y8[i_]])
        yb = [P.sb([128, 5, 128]) for _ in range(2)]
        uli = 0
        for gp in range(16):
            par = gp % 2
            for g2 in range(2):
                gi = 2 * gp + g2
                psU = [P.nextps(), P.nextps()]
                for bi, (b0, nb) in enumerate(BT):
                    if nb == 128:
                        u_ = ul[uli % 3]
                        uli += 1
                    else:
                        u_ = ulz
                    P.dma("sp", u_[0:nb, :].rearrange("b (t c) -> b t c", t=8), g.uTok[b0 * 8:(b0 + nb) * 8, gi * 16:(gi + 1) * 16].rearrange("(b t) c -> b t c", t=8), writes=[u_])
                    tgt = psU[0][:, b0:b0 + 128] if bi < 4 else psU[1][:, 0:128]
                    P.op("pe", lambda e, u_=u_, tgt=tgt: e.transpose(tgt, u_[:, :], g.ident()), reads=[u_, g.cst], writes=[psU[0] if bi < 4 else psU[1]])
                P.op("act", lambda e, par=par, g2=g2, psU=psU: e.copy(u8[par][:, g2, 0:512], psU[0][:, :]), reads=[psU[0]], writes=[u8[par]])
                P.op("act", lambda e, par=par, g2=g2, psU=psU: e.copy(u8[par][:, g2, 512:NB], psU[1][:, 0:32]), reads=[psU[1]], writes=[u8[par]])
            if S5STOP == 2:
                P.barrier()
                P.stack = old
                return
            for d in range(2):
                col = d * 16 + gp
                bre = lambda gp=gp: sm["btab"][:, l, 0, gp, :].unsqueeze(1).to_broadcast([128, 8, 16])
                bim = lambda gp=gp: sm["btab"][:, l, 1, gp, :].unsqueeze(1).to_broadcast([128, 8, 16])
                cre = lambda gp=gp: sm["ctab"][:, l, 0, gp, :].unsqueeze(1).to_broadcast([128, 8, 16])
                cim = lambda gp=gp: sm["ctab"][:, l, 1, gp, :].unsqueeze(1).to_broadcast([128, 8, 16])
                if d == 0:
                    specs = [(Qneg, 0, bre, bim, LW, True, False), (Pasc, 0, cre, cim, RC, False, False),
                             (Qdsc, 0, bre, bim, PB, False, False), (Pasc, 1, cre, cim, None, True, True)]
                else:
                    specs = [(Qasc, 0, bre, bim, LW, True, False), (Pneg, 0, cre, cim, RC, False, False),
                             (Qasc, 0, bre, bim, PB, False, False), (Pdsc, 0, cre, cim, None, True, True)]
                v3 = lambda t: t[:].rearrange("p (a b) -> p a b", a=8)
                for (Tb, k0, yr, yi, dst, negim, isoc) in specs:
                    Tr = lambda Tb=Tb, k0=k0, col=col: Tb[0][:, col, k0:k0 + 8].unsqueeze(2).to_broadcast([128, 8, 16])
                    Ti = lambda Tb=Tb, k0=k0, col=col: Tb[1][:, col, k0:k0 + 8].unsqueeze(2).to_broadcast([128, 8, 16])
                    rdT = [Tb[0], Tb[1], sm["btab"], sm["ctab"]]
                    P.op("pool", lambda e, Tr=Tr, yr=yr: e.tensor_tensor(v3(tb[0]), Tr(), yr(), ALU.mult), reads=rdT, writes=[tb[0]])
                    P.op("pool", lambda e, Ti=Ti, yi=yi: e.tensor_tensor(v3(tb[1]), Ti(), yi(), ALU.mult), reads=rdT, writes=[tb[1]])
                    P.op("pool", lambda e, Ti=Ti, yr=yr: e.tensor_tensor(v3(tb[2]), Ti(), yr(), ALU.mult), reads=rdT, writes=[tb[2]])
                    P.op("pool", lambda e, Tr=Tr, yi=yi: e.tensor_tensor(v3(tb[3]), Tr(), yi(), ALU.mult), reads=rdT, writes=[tb[3]])
                    if negim:
                        P.op("pool", lambda e: e.tensor_scalar(tb[2][:], tb[2][:], -1.0, None, ALU.mult), reads=[tb[2]], writes=[tb[2]])
                    imop = ALU.subtract if negim else ALU.add
                    if isoc or dst is LW:
                        for g2 in range(2):
                            r0, r1 = g2 * 64, (g2 + 1) * 64
                            if isoc:
                                tre, tim = OC[par][d][g2], OC[par][d][g2]
                                ore = lambda tre=tre, r0=r0, r1=r1: tre[r0:r1, 0, :]
                                oim = lambda tim=tim, r0=r0, r1=r1: tim[r0:r1, 1, :]
                            else:
                                tre, tim = LW[g2][0], LW[g2][1]
                                ore = lambda tre=tre, r0=r0, r1=r1: tre[r0:r1, :]
                                oim = lambda tim=tim, r0=r0, r1=r1: tim[r0:r1, :]
                            P.op("pool", lambda e, ore=ore, r0=r0, r1=r1: e.tensor_tensor(ore(), tb[0][r0:r1, :], tb[1][r0:r1, :], ALU.subtract), reads=[tb[0], tb[1]], writes=[tre])
                            P.op("pool", lambda e, oim=oim, r0=r0, r1=r1, imop=imop: e.tensor_tensor(oim(), tb[2][r0:r1, :], tb[3][r0:r1, :], imop), reads=[tb[2], tb[3]], writes=[tim])
                    else:
                        P.op("pool", lambda e, dst=dst: e.tensor_tensor(dst[0][:], tb[0][:], tb[1][:], ALU.subtract), reads=[tb[0], tb[1]], writes=[dst[0]])
                        P.op("pool", lambda e, dst=dst, imop=imop: e.tensor_tensor(dst[1][:], tb[2][:], tb[3][:], imop), reads=[tb[2], tb[3]], writes=[dst[1]])
                if S5STOP == 3:
                    P.barrier()
                    P.stack = old
                    return
                psT8 = P.nextps()
                for g2 in range(2):
                    P.op("pe", lambda e, psT8=psT8, g2=g2: e.matmul(psT8[:, g2 * 128:(g2 + 1) * 128], LW[g2][0][:], RC[0][:], start=True, stop=False), reads=[LW[g2][0], RC[0]], writes=[psT8])
                    P.op("pe", lambda e, psT8=psT8, g2=g2: e.matmul(psT8[:, g2 * 128:(g2 + 1) * 128], LW[g2][1][:], RC[1][:], start=False, stop=True), reads=[LW[g2][1], RC[1]], writes=[psT8])
                mk = 4 + d
                P.op("dve", lambda e, psT8=psT8, par=par, d=d, mk=mk: e.tensor_tensor(T8[par][d][:], psT8[:, 0:256].rearrange("p (a b) -> p a b", a=2),
                                                                              g.cst[:, mk, :].unsqueeze(1).to_broadcast([128, 2, 128]), ALU.mult), reads=[psT8, g.cst], writes=[T8[par][d]])
                psP = P.nextps()
                for pl in range(2):
                    P.op("pe", lambda e, psP=psP, pl=pl: e.transpose(psP[:, pl * 128:(pl + 1) * 128], PB[pl][:], g.ident()), reads=[PB[pl], g.cst], writes=[psP])
                pv = lambda psP=psP: psP[:, 0:256].rearrange("p (a b) -> p a b", a=2)
                P.op("act", lambda e, pv=pv, par=par, d=d: e.copy(Bz[par][d][:, 0, :, 0:64], pv()[:, :, 0:64]), reads=[psP], writes=[Bz[par][d]])
                P.op("act", lambda e, pv=pv, par=par, d=d: e.copy(Bz[par][d][:, 1, :, 64:128], pv()[:, :, 64:128]), reads=[psP], writes=[Bz[par][d]])
                if S5STOP == 4:
                    P.barrier()
                    P.stack = old
                    return
                for pl in range(2):
                    for (c0, c1) in CH:
                        ps = P.nextps()
                        for g2 in range(2):
                            P.op("pe", lambda e, ps=ps, par=par, d=d, g2=g2, pl=pl, c0=c0, c1=c1: e.matmul(ps[:, 0:c1 - c0], Bz[par][d][:, g2, pl, :], u8[par][:, g2, c0:c1], start=(g2 == 0), stop=(g2 == 1)),
                                 reads=[Bz[par][d], u8[par]], writes=[ps])
                        if d == 0:
                            P.op("act", lambda e, ps=ps, pl=pl, c0=c0, c1=c1: e.copy(X[0][:, pl, c0:c1], ps[:, 0:c1 - c0]), reads=[ps], writes=[X[0]])
                        elif c0 == 0:
                            P.op("act", lambda e, ps=ps, pl=pl: e.copy(X[1][:, pl, NBL:NB], ps[:, 0:NBC]), reads=[ps], writes=[X[1]])
                            P.op("act", lambda e, ps=ps, pl=pl, c1=c1: e.copy(X[1][:, pl, 0:c1 - NBC], ps[:, NBC:c1]), reads=[ps], writes=[X[1]])
                        else:
                            P.op("act", lambda e, ps=ps, pl=pl, c0=c0, c1=c1: e.copy(X[1][:, pl, c0 - NBC:c1 - NBC], ps[:, 0:c1 - c0]), reads=[ps], writes=[X[1]])
                if S5STOP == 5:
                    P.barrier()
                    P.stack = old
                    return
                src, dst = X[d], X2
                for s_ in range(NS5):
                    sh = 1 << s_
                    n = NB - sh
                    ar = HS[:, s_, 0, col:col + 1]
                    ai = HS[:, s_, 1, col:col + 1]
                    nai = HS[:, s_, 2, col:col + 1]
                    if d == 0:
                        keep = (0, sh)
                        o0, i0 = sh, 0
                    else:
                        keep = (n, NB)
                        o0, i0 = 0, sh
                    P.op("act", lambda e, src=src, dst=dst, keep=keep: e.copy(dst[:, :, keep[0]:keep[1]], src[:, :, keep[0]:keep[1]]), reads=[src], writes=[dst])
                    P.op("dve", lambda e, src=src, dst=dst, ar=ar, o0=o0, i0=i0, n=n: e.scalar_tensor_tensor(dst[:, 0, o0:o0 + n], src[:, 0, i0:i0 + n], ar, src[:, 0, o0:o0 + n], ALU.mult, ALU.add), reads=[src, HS], writes=[dst])
                    P.op("dve", lambda e, src=src, dst=dst, nai=nai, o0=o0, i0=i0, n=n: e.scalar_tensor_tensor(dst[:, 0, o0:o0 + n], src[:, 1, i0:i0 + n], nai, dst[:, 0, o0:o0 + n], ALU.mult, ALU.add), reads=[src, HS, dst], writes=[dst])
                    P.op("dve", lambda e, src=src, dst=dst, ai=ai, o0=o0, i0=i0, n=n: e.scalar_tensor_tensor(dst[:, 1, o0:o0 + n], src[:, 0, i0:i0 + n], ai, src[:, 1, o0:o0 + n], ALU.mult, ALU.add), reads=[src, HS, dst], writes=[dst])
                    P.op("dve", lambda e, src=src, dst=dst, ar=ar, o0=o0, i0=i0, n=n: e.scalar_tensor_tensor(dst[:, 1, o0:o0 + n], src[:, 1, i0:i0 + n], ar, dst[:, 1, o0:o0 + n], ALU.mult, ALU.add), reads=[src, HS, dst], writes=[dst])
                    src, dst = dst, src
                assert src is X[d]
                P.op("act", lambda e, d=d: e.copy(Xb[d][:], X[d][:]), reads=[X[d]], writes=[Xb[d]])
            if S5STOP == 6:
                P.barrier()
                P.stack = old
                return
            for g2 in range(2):
                gi = 2 * gp + g2
                r0, r1 = g2 * 64, (g2 + 1) * 64
                y8 = Y8[g2]
                for (c0, c1) in CH:
                    ps = P.nextps()
                    w = c1 - c0
                    mms = []
                    for d in range(2):
                        mms.append((T8[par][d], lambda d=d, par=par, g2=g2: T8[par][d][:, g2, :], u8[par], lambda c0=c0, c1=c1, par=par, g2=g2: u8[par][:, g2, c0:c1], 0, w))
                    for pl in range(2):
                        oc0 = lambda pl=pl, par=par, g2=g2: OC[par][0][g2][:, pl, :]
                        oc1 = lambda pl=pl, par=par, g2=g2: OC[par][1][g2][:, pl, :]
                        if c0 == 0:
                            mms.append((OC[par][0][g2], oc0, Xb[0], lambda pl=pl, c1=c1: Xb[0][:, pl, 0:c1 - 1], 1, c1))
                            mms.append((OC[par][1][g2], oc1, Xb[1], lambda pl=pl: Xb[1][:, pl, NBL + 1:NB], 0, NBC - 1))
                            mms.append((OC[par][1][g2], oc1, Xb[1], lambda pl=pl, c1=c1: Xb[1][:, pl, 1:c1 - NBC + 1], NBC, c1))
                        else:
                            mms.append((OC[par][0][g2], oc0, Xb[0], lambda pl=pl, c0=c0, c1=c1: Xb[0][:, pl, c0 - 1:c1 - 1], 0, w))
                            mms.append((OC[par][1][g2], oc1, Xb[1], lambda pl=pl, c0=c0, c1=c1: Xb[1][:, pl, c0 - NBC + 1:c1 - NBC + 1], 0, w))
                    for mi, (lt_, lf, rt_, rf, o0, o1) in enumerate(mms):
                        P.op("pe", lambda e, ps=ps, lf=lf, rf=rf, o0=o0, o1=o1, mi=mi, nm=len(mms): e.matmul(ps[:, o0:o1], lf(), rf(), start=(mi == 0), stop=(mi == nm - 1)),
                             reads=[lt_, rt_], writes=[ps])
                    P.op("act", lambda e, ps=ps, y8=y8, c0=c0, c1=c1, w=w: e.copy(y8[:, c0:c1], ps[:, 0:w]), reads=[ps], writes=[y8])
                if S5STOP == 7:
                    P.barrier()
                    P.stack = old
                    return
                psY = [P.nextps(), P.nextps()]
                ybb = yb[g2]
                for bi, (b0, nb) in enumerate(BT):
                    tgt = psY[0][:, bi * 128:(bi + 1) * 128] if bi < 4 else psY[1][:, 0:128]
                    P.op("pe", lambda e, y8=y8, b0=b0, tgt=tgt: e.transpose(tgt, y8[:, b0:b0 + 128], g.ident()), reads=[y8, g.cst], writes=[psY[0] if bi < 4 else psY[1]])
                P.op("dve", lambda e, ybb=ybb, psY=psY: e.tensor_copy(ybb[:, 0:4, :].rearrange("p a b -> p (a b)"), psY[0][:, :]), reads=[psY[0]], writes=[ybb])
                P.op("dve", lambda e, ybb=ybb, psY=psY: e.tensor_copy(ybb[0:32, 4, :], psY[1][0:32, 0:128]), reads=[psY[1]], writes=[ybb])
                if S5STOP == 8:
                    P.barrier()
                    P.stack = old
                    return
                for bi, (b0, nb) in enumerate(BT):
                    P.dma("sp", g.y5pre[b0 * 8:(b0 + nb) * 8, gi * 16:(gi + 1) * 16].rearrange("(b t) c -> b t c", t=8), ybb[0:nb, bi, :].rearrange("b (t c) -> b t c", t=8), reads=[ybb])
        P.barrier()
        P.stack = old
    with ExitStack() as st:
        old = P.stack
        P.stack = st
        glw = P.sb([128, 4, 512], BF16)
        gst = [P.sb([128, 4, 512])]
        load_weight_bf16(P, glw, 0, lambda c0, c1: g.glu_w[l, :, c0:c1].rearrange("(k p) f -> p k f", p=128), 512, gst)
        yps = [P.sb([128, 512]) for _ in range(2)]
        uts = [P.sb([128, 512]) for _ in range(2)]
        tg = P.sb([128, 512])
        gq = P.sb([128, 512])
        gts = [P.sb([128, 4, 512], BF16) for _ in range(2)]
        yos = [P.sb([128, 4, 512], BF16) for _ in range(2)]
        sg = P.sb([128, 512])
        si = 0
        for ti, (tok0, TW, v, s0, s1, rl) in enumerate(token_tiles(with_ctx=not last)):
            gt = gts[ti % 2]
            yo_ = yos[ti % 2]
            for s in range(TW // 128):
                yp = yps[si % 2]
                ut = uts[si % 2]
                si += 1
                r0 = tok0 + s * 128
                P.dma("sp", yp[:], g.y5pre[r0:r0 + 128, :], writes=[yp])
                P.dma("sp", ut[:], g.uTok[r0:r0 + 128, :], writes=[ut])
                P.op("dve", lambda e, ut=ut: e.tensor_tensor(ut[:], ut[:], sm["d5full"][:, l, :], ALU.mult), reads=[ut, sm["d5full"]], writes=[ut])
                P.op("dve", lambda e, ut=ut, yp=yp: e.tensor_tensor(yp[:], yp[:], ut[:], ALU.add), reads=[ut, yp], writes=[yp])
                P.op("dve", lambda e, yp=yp: e.tensor_tensor(tg[:], yp[:], yp[:], ALU.mult), reads=[yp], writes=[tg])
                P.op("dve", lambda e: e.tensor_scalar(tg[:], tg[:], 0.044715, 1.0, ALU.mult, ALU.add), reads=[tg], writes=[tg])
                P.op("dve", lambda e, yp=yp: e.tensor_tensor(tg[:], tg[:], yp[:], ALU.mult), reads=[tg, yp], writes=[tg])
                P.op("act", lambda e: e.activation(tg[:], tg[:], AF.Sigmoid, scale=1.5957691216057308), reads=[tg], writes=[tg])
                P.op("dve", lambda e, yp=yp: e.tensor_tensor(gq[:], tg[:], yp[:], ALU.mult), reads=[tg, yp], writes=[gq])
                psG = P.nextps()
                for j in range(4):
                    P.op("pe", lambda e, j=j, psG=psG: e.transpose(psG[:, j * 128:(j + 1) * 128], gq[:, j * 128:(j + 1) * 128], g.ident()), reads=[gq, g.cst], writes=[psG])
                P.op("act", lambda e, gt=gt, psG=psG, s=s: e.copy(gt[:, :, s * 128:(s + 1) * 128], psG[:, :].rearrange("p (j t) -> p j t", j=4)), reads=[psG], writes=[gt])
            for fo in range(4):
                ps = P.nextps()
                for k in range(4):
                    P.op("pe", lambda e, ps=ps, k=k, fo=fo, gt=gt, TW=TW: e.matmul(ps[:, 0:TW], glw[:, k, fo * 128:(fo + 1) * 128], gt[:, k, 0:TW], start=(k == 0), stop=(k == 3)),
                         reads=[glw, gt], writes=[ps])
                P.op("act", lambda e, ps=ps, fo=fo, TW=TW: e.activation(sg[:, 0:TW], ps[:, 0:TW], AF.Sigmoid, bias=sm["glub"][:, l, fo:fo + 1]), reads=[ps, sm["glub"]], writes=[sg])
                P.op("dve", lambda e, fo=fo, gt=gt, yo_=yo_, TW=TW: e.tensor_tensor(yo_[:, fo, 0:TW], sg[:, 0:TW], gt[:, fo, 0:TW], ALU.mult), reads=[sg, gt], writes=[yo_])
            P.dma("pool", g.y5T[:, tok0:tok0 + TW].rearrange("(k p) t -> p k t", p=128), yo_[:, :, 0:TW], reads=[yo_])
        P.barrier()
        P.stack = old


def post_norm_res(P, g, xm, xt, xn, TW, sq, tmp, rstd, gate_ap):
    P.op("act", lambda e: e.activation(sq[:, :, 0:TW], xm[:, :, 0:TW], AF.Square), reads=[xm], writes=[sq])
    ps = P.nextps()
    for k in range(8):
        P.op("pe", lambda e, k=k: e.matmul(ps[:, 0:TW], g.onesb[:], sq[:, k, 0:TW], start=(k == 0), stop=(k == 7)), reads=[g.onesb, sq], writes=[ps])
    P.op("act", lambda e: e.activation(rstd[:, 0:TW], ps[:, 0:TW], AF.Sqrt, bias=g.eps[:], scale=1.0 / D), reads=[ps, g.eps], writes=[rstd])
    P.op("dve", lambda e: e.reciprocal(rstd[:, 0:TW], rstd[:, 0:TW]), reads=[rstd], writes=[rstd])
    for k in range(8):
        ga = gate_ap(k)
        P.op("dve", lambda e, k=k: e.tensor_tensor(tmp[:, k, 0:TW], xm[:, k, 0:TW], rstd[:, 0:TW], ALU.mult), reads=[xm, rstd], writes=[tmp])
        P.op("dve", lambda e, k=k, ga=ga: e.scalar_tensor_tensor(xn[:, k, 0:TW], tmp[:, k, 0:TW], ga, xt[:, k, 0:TW], ALU.mult, ALU.add), reads=[tmp, xt, g.mod], writes=[xn])


def stage_outproj(nc, g, l, xsrc, xdst, last):
    P = g.P
    with ExitStack() as st:
        old = P.stack
        P.stack = st
        wo = P.sb([128, 8, 1024], BF16)
        stg = [P.sb([128, 8, 512]) for _ in range(2)]
        load_weight_bf16(P, wo, 0, lambda c0, c1: g.w_out[l, :, c0:c1].rearrange("(k p) f -> p k f", p=128), 1024, stg)
        xts = [P.sb([128, 8, 512]) for _ in range(2)]
        yin = [P.sb([128, 8, 512], BF16) for _ in range(2)]
        xm = P.sb([128, 8, 512])
        xn = [P.sb([128, 8, 512]) for _ in range(2)]
        sq = P.sb([128, 8, 512], BF16)
        tmp = P.sb([128, 8, 512])
        rstd = P.sb([128, 512])
        ev = 0
        for ti, (tok0, TW, v, s0, s1, rl) in enumerate(token_tiles(with_ctx=not last)):
            xt = xts[ti % 2]
            yi = yin[ti % 2]
            xo = xn[ti % 2]
            P.dma("sp", xt[:, :, 0:TW], xsrc[:, tok0:tok0 + TW].rearrange("(k p) t -> p k t", p=128), writes=[xt])
            P.dma("sp", yi[:, 0:4, 0:TW], g.ysT[:, tok0:tok0 + TW].rearrange("(k p) t -> p k t", p=128), writes=[yi])
            P.dma("sp", yi[:, 4:8, 0:TW], g.y5T[:, tok0:tok0 + TW].rearrange("(k p) t -> p k t", p=128), writes=[yi])
            for j in range(8):
                ps = P.nextps()
                for k in range(8):
                    P.op("pe", lambda e, ps=ps, k=k, j=j, yi=yi, TW=TW: e.matmul(ps[:, 0:TW], wo[:, k, j * 128:(j + 1) * 128], yi[:, k, 0:TW], start=(k == 0), stop=(k == 7)),
                         reads=[wo, yi], writes=[ps])
                copy_op(P, "act" if ev % 2 == 0 else "dve", xm[:, j, 0:TW], ps[:, 0:TW], [ps], [xm])
                ev += 1
            post_norm_res(P, g, xm, xt, xo, TW, sq, tmp, rstd, lambda k, v=v: g.mod[:, l, 2, k, v:v + 1])
            P.dma("pool", xdst[:, tok0:tok0 + TW].rearrange("(k p) t -> p k t", p=128), xo[:, :, 0:TW], reads=[xo])
        P.barrier()
        P.stack = old


def stage_ffn(nc, g, l, xsrc, xdst, last):
    P = g.P
    cw = g.sm["cwf"]
    tl = token_tiles(with_ctx=not last)
    supers = []
    if not last:
        supers.append([tl[0]])
        tl = tl[1:]
    for i in range(0, len(tl), 2):
        supers.append(tl[i:i + 2])
    for sup in supers:
        with ExitStack() as st:
            old = P.stack
            P.stack = st
            act = P.sb([128, NJ, 1024], BF16)
            with ExitStack() as st2:
                P.stack = st2
                hT = P.sb([128, 8, 1024], BF16)
                xts = [P.sb([128, 8, 512]) for _ in range(2)]
                sq = P.sb([128, 8, 512], BF16)
                tmp = P.sb([128, 8, 512])
                rstd = P.sb([128, 512])
                for si, (tok0, TW, v, s0, s1, rl) in enumerate(sup):
                    xt = xts[si % 2]
                    P.dma("sp", xt[:, :, 0:TW], xsrc[:, tok0:tok0 + TW].rearrange("(k p) t -> p k t", p=128), writes=[xt])
                    norm_mod(P, g, xt, hT, TW, sq, tmp, rstd,
                             lambda k, v=v: g.mod[:, l, 3, k, v:v + 1], lambda k, v=v: g.mod[:, l, 4, k, v:v + 1], hoff=si * 512)
                wst = [P.sb([128, 8, 256]) for _ in range(2)]
                wjb = [P.sb([128, 8, 256], BF16) for _ in range(2)]
                accgl = [P.sb([128, 512]) for _ in range(2)]
                accvl = [P.sb([128, 512]) for _ in range(2)]
                it_ = 0
                for j in range(NJ):
                    ws = wst[j % 2]
                    wj = wjb[j % 2]
                    P.dma("sp", ws[:, :, 0:128], g.w_up[l, :, j * 128:(j + 1) * 128].rearrange("(k p) f -> p k f", p=128), writes=[ws])
                    P.dma("sp", ws[:, :, 128:256], g.w_up[l, :, DFF + j * 128:DFF + (j + 1) * 128].rearrange("(k p) f -> p k f", p=128), writes=[ws])
                    P.op("pool", lambda e, ws=ws, wj=wj: e.tensor_copy(wj[:], ws[:]), reads=[ws], writes=[wj])
                    for si, (tok0, TW, v, s0, s1, rl) in enumerate(sup):
                        accg = accgl[it_ % 2]
                        accv = accvl[it_ % 2]
                        it_ += 1
                        pss = []
                        for half in range(2):
                            ps = P.nextps()
                            pss.append(ps)
                            for k in range(8):
                                P.op("pe", lambda e, ps=ps, k=k, wj=wj, half=half, si=si, TW=TW: e.matmul(ps[:, 0:TW], wj[:, k, half * 128:(half + 1) * 128], hT[:, k, si * 512:si * 512 + TW], start=(k == 0), stop=(k == 7)),
                                     reads=[wj, hT], writes=[ps])
                        for half, (ps, ac) in enumerate(zip(pss, [accg, accv])):
                            ch = j + half * NJ
                            v3 = lambda ap, a, b, TW=TW, rl=rl: ap[:, 0:TW].rearrange("p (r w) -> p r w", w=rl)[:, :, a:b]
                            P.op("act", lambda e, ps=ps, ac=ac, ch=ch, TW=TW: e.activation(ac[:, 0:TW], ps[:, 0:TW], AF.Identity, bias=cw[:, l, ch, 3:4], scale=cw[:, l, ch, 1:2]), reads=[ps, cw], writes=[ac])
                            P.op("dve", lambda e, ps=ps, ac=ac, ch=ch, v3=v3, rl=rl: e.scalar_tensor_tensor(v3(ac, 1, rl), v3(ps, 0, rl - 1), cw[:, l, ch, 0:1], v3(ac, 1, rl), ALU.mult, ALU.add), reads=[ps, cw, ac], writes=[ac])
                            P.op("dve", lambda e, ps=ps, ac=ac, ch=ch, v3=v3, rl=rl: e.scalar_tensor_tensor(v3(ac, 0, rl - 1), v3(ps, 1, rl), cw[:, l, ch, 2:3], v3(ac, 0, rl - 1), ALU.mult, ALU.add), reads=[ps, cw, ac], writes=[ac])
                        P.op("act", lambda e, TW=TW, accg=accg: e.activation(accg[:, 0:TW], accg[:, 0:TW], AF.Silu), reads=[accg], writes=[accg])
                        P.op("dve", lambda e, j=j, si=si, TW=TW, accg=accg, accv=accv: e.tensor_tensor(act[:, j, si * 512:si * 512 + TW], accg[:, 0:TW], accv[:, 0:TW], ALU.mult), reads=[accg, accv], writes=[act])
                P.barrier()
                P.stack = st
            wds = [P.sb([128, NJ, 128]) for _ in range(2)]
            wdb = [P.sb([128, NJ, 128], BF16) for _ in range(2)]
            xm = P.sb([128, 8, 1024])
            xts = [P.sb([128, 8, 512]) for _ in range(1)]
            xn = [P.sb([128, 8, 512]) for _ in range(1)]
            sq = P.sb([128, 8, 512], BF16)
            tmp = P.sb([128, 8, 512])
            rstd = P.sb([128, 512])
            ev = 0
            for jo in range(8):
                ws = wds[jo % 2]
                wd = wdb[jo % 2]
                P.dma("sp", ws[:], g.w_down[l, :, jo * 128:(jo + 1) * 128].rearrange("(c p) f -> p c f", p=128), writes=[ws])
                P.op("pool", lambda e, ws=ws, wd=wd: e.tensor_copy(wd[:], ws[:]), reads=[ws], writes=[wd])
                for si, (tok0, TW, v, s0, s1, rl) in enumerate(sup):
                    ps = P.nextps()
                    for c in range(NJ):
                        P.op("pe", lambda e, ps=ps, c=c, wd=wd, si=si, TW=TW: e.matmul(ps[:, 0:TW], wd[:, c, :], act[:, c, si * 512:si * 512 + TW], start=(c == 0), stop=(c == NJ - 1)),
                             reads=[wd, act], writes=[ps])
                    copy_op(P, "act" if ev % 2 == 0 else "dve", xm[:, jo, si * 512:si * 512 + TW], ps[:, 0:TW], [ps], [xm])
                    ev += 1
            for si, (tok0, TW, v, s0, s1, rl) in enumerate(sup):
                xt = xts[0]
                xo = xn[0]
                P.dma("sp", xt[:, :, 0:TW], xsrc[:, tok0:tok0 + TW].rearrange("(k p) t -> p k t", p=128), writes=[xt])
                xmv = T.__new__(T)
                xmv.t = xm.t[:, :, si * 512:si * 512 + 512]
                xmv.d = xm.d
                post_norm_res(P, g, xmv, xt, xo, TW, sq, tmp, rstd, lambda k, v=v: g.mod[:, l, 5, k, v:v + 1])
                if last:
                    P.dma("pool", xdst[:, tok0 - LC:tok0 - LC + TW].rearrange("(k p) t -> p k t", p=128), xo[:, :, 0:TW], reads=[xo])
                else:
                    P.dma("pool", xdst[:, tok0:tok0 + TW].rearrange("(k p) t -> p k t", p=128), xo[:, :, 0:TW], reads=[xo])
            P.barrier()
            P.stack = old


def _colvec(v):
    return np.ascontiguousarray(np.asarray(v, np.float32).reshape(-1, 128).T)


def shared_layout(inp):
    f = lambda a: np.ascontiguousarray(np.asarray(a, np.float32))
    m = {}
    consts = np.zeros((128, 6, 128), np.float32)
    consts[:, 0] = np.eye(128)
    consts[:, 1] = np.triu(np.ones((128, 128)))
    consts[:, 2] = np.tril(np.ones((128, 128)))
    consts[:, 3] = 1.0
    blk = np.arange(128) // 16
    consts[:, 4] = (blk[:, None] <= blk[None, :])
    consts[:, 5] = (blk[:, None] >= blk[None, :])
    m["consts"] = consts
    m["w_ada"] = f(inp["w_ada"])
    m["bada"] = np.ascontiguousarray(np.stack([_colvec(inp["b_ada"][l]) for l in range(DEPTH)], 1))
    gv = np.zeros((128, DEPTH, 4, 8), np.float32)
    for l in range(DEPTH):
        for i, nm in enumerate(["g_pre_mix", "g_post_mix", "g_pre_ffn", "g_post_ffn"]):
            gv[:, l, i, :] = _colvec(inp[nm][l])
    m["gvec"] = gv
    m["w_in"] = f(inp["w_in"])
    cp = np.zeros((128, DEPTH, 8, 4), np.float32)
    for l in range(DEPTH):
        for k in range(3):
            cp[:, l, :, k] = _colvec(inp["ssd_conv_w"][l, k])
        cp[:, l, :, 3] = _colvec(inp["ssd_conv_b"][l])
    m["convp"] = cp
    m["dtb"] = np.ascontiguousarray(np.broadcast_to(f(inp["ssd_dt_bias"]).reshape(1, DEPTH, 16), (128, DEPTH, 16)))
    m["alog"] = np.ascontiguousarray(np.broadcast_to(f(inp["ssd_a_log"]).reshape(1, DEPTH, 16), (128, DEPTH, 16)))
    m["dfull"] = np.ascontiguousarray(np.broadcast_to(np.repeat(f(inp["ssd_d"]), 64, axis=1).reshape(1, DEPTH, 512), (128, DEPTH, 512)))
    m["gssd"] = np.ascontiguousarray(np.broadcast_to(f(inp["ssd_norm_g"]).reshape(1, DEPTH, 512), (128, DEPTH, 512)))
    s5a = np.zeros((128, DEPTH, 2, 32), np.float32)
    s5dt = np.zeros((128, DEPTH, 32), np.float32)
    for l in range(DEPTH):
        for c, nm in enumerate(["s5_a_re", "s5_a_im"]):
            a = f(inp[nm][l]).reshape(2, 16, 2, 64)
            s5a[:, l, c, :] = a.transpose(2, 3, 0, 1).reshape(128, 32)
        ld = f(inp["s5_log_dt"][l]).reshape(2, 16, 2)
        s5dt[:, l, :] = np.repeat(ld.transpose(2, 0, 1).reshape(2, 1, 32), 64, axis=1).reshape(128, 32)
    m["s5a"] = s5a
    m["s5dt"] = s5dt
    btab = np.zeros((128, DEPTH, 2, 16, 16), np.float32)
    ctab = np.zeros((128, DEPTH, 2, 16, 16), np.float32)
    for l in range(DEPTH):
        for pl, (bn, cn) in enumerate([("s5_b_re", "s5_c_re"), ("s5_b_im", "s5_c_im")]):
            bb = f(inp[bn][l]).reshape(16, 2, 64, 16)
            cc = f(inp[cn][l]).reshape(16, 2, 16, 64)
            btab[:, l, pl] = bb.transpose(1, 2, 0, 3).reshape(128, 16, 16)
            ctab[:, l, pl] = cc.transpose(1, 3, 0, 2).reshape(128, 16, 16)
    m["btab"] = btab
    m["ctab"] = ctab
    m["d5full"] = np.ascontiguousarray(np.broadcast_to(f(inp["s5_d"]).reshape(1, DEPTH, 512), (128, DEPTH, 512)))
    m["glu_w"] = f(inp["s5_glu_w"])
    m["glub"] = np.ascontiguousarray(np.stack([_colvec(inp["s5_glu_b"][l]) for l in range(DEPTH)], 1))
    m["w_out"] = f(inp["w_out"])
    m["w_up"] = f(inp["ffn_w_up"])
    cwf = np.zeros((128, DEPTH, 2 * NJ, 4), np.float32)
    for l in range(DEPTH):
        for k in range(3):
            cwf[:, l, :, k] = _colvec(inp["ffn_conv_w"][l, k])
        cwf[:, l, :, 3] = _colvec(inp["ffn_conv_b"][l])
    m["cwf"] = cwf
    m["w_down"] = f(inp["ffn_w_down"])
    return m


def core_layout(inp, b):
    x = np.asarray(inp["x"][b], np.float32)
    ctx = np.asarray(inp["ctx"][b], np.float32)
    xc0 = np.ascontiguousarray(np.concatenate([ctx.T, x.T], axis=1))
    cv = np.zeros((128, 8, 2), np.float32)
    cv[:, :, 0] = _colvec(inp["c"][b])
    cv[:, :, 1] = _colvec(inp["c_ctx"])
    return {"xc0": xc0, "cvec": cv}


_CACHE = {}


def kernel(**inputs):
    if "nc" not in _CACHE:
        _CACHE["nc"] = build_program(None)[0]
    nc = _CACHE["nc"]
    sh = shared_layout(inputs)
    in_maps = []
    for core in range(8):
        m = dict(sh)
        m.update(core_layout(inputs, core % 4))
        in_maps.append(m)
    res = run_bass_kernel_spmd(nc, in_maps, core_ids=list(range(8)))
    out = np.stack([np.ascontiguousarray(res.results[b]["out"].T) for b in range(4)], 0)
    return out.astype(np.float32)
```
